# Optimizing a Trainium2 kernel written in Bass

```python
import math
import jax, jax.numpy as jnp
from jax import lax
import numpy as np

D_MODEL = 2048
BATCH = 1
SEQ = 16384
DEPTH = 1

GRID_W = 64
WIN_H = 8
WIN_W = 16
Q_BLOCK = 128
HEAD_DIM = 128
NA_WIDTH = D_MODEL // 2
NA_HEADS = NA_WIDTH // HEAD_DIM
SSM_WIDTH = D_MODEL // 4
SSM_CH = 16
SSM_GROUPS = SSM_WIDTH // SSM_CH
SSM_STATE = 64
MEM_WIDTH = D_MODEL // 4
MEM_HEADS = 4
MEM_HEAD_DIM = MEM_WIDTH // MEM_HEADS
N_MEM = 256
MIX_WIDTH = NA_WIDTH + SSM_WIDTH + MEM_WIDTH
IN_WIDTH = 3 * NA_WIDTH + SSM_WIDTH + MEM_WIDTH
D_FF = 4 * D_MODEL
EPS = 1e-6
DT_MIN = 1e-3
DT_MAX = 1e-1
LAM_RE_MAX = -1e-4

kernel_name = "hybrid_natten_s5_memory_block"


def rmsnorm(x, g):
    xf = x.astype(jnp.float32)
    y = xf * lax.rsqrt(jnp.mean(xf * xf, axis=-1, keepdims=True) + EPS)
    return (y * g.astype(jnp.float32)).astype(x.dtype)


def neighbourhood_tables(S):
    rows = S // GRID_W
    kh = min(WIN_H, rows)
    kw = min(WIN_W, GRID_W)
    t = jnp.arange(S, dtype=jnp.int32)
    r = t // GRID_W
    c = t % GRID_W
    rs = jnp.clip(r - kh // 2, 0, rows - kh)
    cs = jnp.clip(c - kw // 2, 0, GRID_W - kw)
    kr = rs[:, None, None] + jnp.arange(kh, dtype=jnp.int32)[None, :, None]
    kc = cs[:, None, None] + jnp.arange(kw, dtype=jnp.int32)[None, None, :]
    nbr = (kr * GRID_W + kc).reshape(S, kh * kw)
    off = ((kr - r[:, None, None] + WIN_H - 1) * (2 * WIN_W - 1)
           + (kc - c[:, None, None] + WIN_W - 1)).reshape(S, kh * kw)
    return nbr, off


def neighbourhood_attention(q, k, v, rpb):
    B, S, H, Dh = q.shape
    nbr, off = neighbourhood_tables(S)
    nk = nbr.shape[1]
    nb = S // Q_BLOCK
    rpb_flat = rpb.reshape(H, -1).astype(jnp.float32)
    scale = Dh ** -0.5
    qb = q.reshape(B, nb, Q_BLOCK, H, Dh).transpose(1, 0, 2, 3, 4)

    def block(args):
        q_blk, idx, bidx = args
        k_g = k[:, idx]
        v_g = v[:, idx]
        s = jnp.einsum('bqhd,bqkhd->bhqk', q_blk, k_g,
                       preferred_element_type=jnp.float32) * scale + rpb_flat[:, bidx][None]
        p = jax.nn.softmax(s, axis=-1).astype(v.dtype)
        return jnp.einsum('bhqk,bqkhd->bqhd', p, v_g)

    out = lax.map(block, (qb, nbr.reshape(nb, Q_BLOCK, nk), off.reshape(nb, Q_BLOCK, nk)))
    return out.transpose(1, 0, 2, 3, 4).reshape(B, S, H * Dh)


def s5_direction(u, lam_re, lam_im, log_dt, b_re, b_im, c_re, c_im, reverse):
    lam = lax.complex(jnp.minimum(lam_re.astype(jnp.float32), LAM_RE_MAX),
                      lam_im.astype(jnp.float32))
    dt = jnp.exp(log_dt.astype(jnp.float32))[:, None]
    lam_bar = jnp.exp(lam * dt)
    b = lax.complex(b_re.astype(jnp.float32), b_im.astype(jnp.float32))
    b_bar = ((lam_bar - 1.0) / lam)[..., None] * b
    bu = jnp.einsum('bsgc,gpc->bsgp', u.astype(jnp.complex64), b_bar)
    a = jnp.broadcast_to(lam_bar, bu.shape)

    def combine(e1, e2):
        a1, x1 = e1
        a2, x2 = e2
        return a1 * a2, a2 * x1 + x2

    _, h = lax.associative_scan(combine, (a, bu), axis=1, reverse=reverse)
    c = lax.complex(c_re.astype(jnp.float32), c_im.astype(jnp.float32))
    return jnp.einsum('bsgp,gcp->bsgc', h, c).real


def s5_mixer(u_flat, lam_re, lam_im, log_dt, b_re, b_im, c_re, c_im, d, w_glu, b_glu):
    B, S, _ = u_flat.shape
    u = u_flat.reshape(B, S, SSM_GROUPS, SSM_CH).astype(jnp.float32)
    y_f = s5_direction(u, lam_re[0], lam_im[0], log_dt[0], b_re[0], b_im[0], c_re[0], c_im[0], False)
    y_b = s5_direction(u, lam_re[1], lam_im[1], log_dt[1], b_re[1], b_im[1], c_re[1], c_im[1], True)
    y = (y_f + y_b + d.astype(jnp.float32) * u).reshape(B, S, SSM_WIDTH)
    y = jax.nn.gelu(y)
    y = y * jax.nn.sigmoid(y @ w_glu.astype(jnp.float32) + b_glu.astype(jnp.float32))
    return y.astype(u_flat.dtype)


def memory_cross_attention(q_flat, mem, mem_norm, w_mem_kv):
    B, S, _ = q_flat.shape
    q = q_flat.reshape(B, S, MEM_HEADS, MEM_HEAD_DIM)
    kv = rmsnorm(mem, mem_norm) @ w_mem_kv
    M = kv.shape[1]
    k, v = jnp.split(kv, 2, axis=-1)
    k = k.reshape(B, M, MEM_HEADS, MEM_HEAD_DIM)
    v = v.reshape(B, M, MEM_HEADS, MEM_HEAD_DIM)
    s = jnp.einsum('bshd,bmhd->bhsm', q, k, preferred_element_type=jnp.float32) * MEM_HEAD_DIM ** -0.5
    p = jax.nn.softmax(s, axis=-1).astype(v.dtype)
    return jnp.einsum('bhsm,bmhd->bshd', p, v).reshape(B, S, MEM_WIDTH)


def setup_inputs(seed: int = 0) -> dict:
    key = jax.random.key(seed)
    ks = jax.random.split(key, 32)
    L = DEPTH
    f32 = jnp.float32

    def dense(k, shape, fan_in):
        return jax.random.normal(k, shape, f32) * fan_in ** -0.5

    def gain(k, shape):
        return 1.0 + 0.02 * jax.random.normal(k, shape, f32)

    n = jnp.arange(SSM_STATE, dtype=f32)
    G, P, Cg = SSM_GROUPS, SSM_STATE, SSM_CH
    return {
        "x": jax.random.normal(ks[0], (BATCH, SEQ, D_MODEL), f32),
        "mem": jax.random.normal(ks[1], (BATCH, N_MEM, D_MODEL), f32),
        "norm_mix_pre": gain(ks[2], (L, D_MODEL)),
        "w_in": dense(ks[3], (L, D_MODEL, IN_WIDTH), D_MODEL),
        "na_rpb": 0.02 * jax.random.normal(ks[4], (L, NA_HEADS, 2 * WIN_H - 1, 2 * WIN_W - 1), f32),
        "ssm_lam_re": -0.5 + 0.01 * jax.random.normal(ks[5], (L, 2, G, P), f32),
        "ssm_lam_im": math.pi * n + 0.01 * jax.random.normal(ks[6], (L, 2, G, P), f32),
        "ssm_log_dt": jax.random.uniform(ks[7], (L, 2, G), f32, math.log(DT_MIN), math.log(DT_MAX)),
        "ssm_b_re": dense(ks[8], (L, 2, G, P, Cg), 2 * Cg),
        "ssm_b_im": dense(ks[9], (L, 2, G, P, Cg), 2 * Cg),
        "ssm_c_re": dense(ks[10], (L, 2, G, Cg, P), 2 * P),
        "ssm_c_im": dense(ks[11], (L, 2, G, Cg, P), 2 * P),
        "ssm_d": jax.random.normal(ks[12], (L, G, Cg), f32),
        "w_glu": dense(ks[13], (L, SSM_WIDTH, SSM_WIDTH), SSM_WIDTH),
        "b_glu": 0.01 * jax.random.normal(ks[14], (L, SSM_WIDTH), f32),
        "mem_norm": gain(ks[15], (L, D_MODEL)),
        "w_mem_kv": dense(ks[16], (L, D_MODEL, 2 * MEM_WIDTH), D_MODEL),
        "out_norm_na": gain(ks[17], (L, NA_WIDTH)),
        "out_norm_ssm": gain(ks[18], (L, SSM_WIDTH)),
        "out_norm_mem": gain(ks[19], (L, MEM_WIDTH)),
        "w_out": dense(ks[20], (L, MIX_WIDTH, D_MODEL), MIX_WIDTH),
        "norm_mix_post": gain(ks[21], (L, D_MODEL)),
        "norm_mlp_pre": gain(ks[22], (L, D_MODEL)),
        "w_ff1": dense(ks[23], (L, D_MODEL, D_FF), D_MODEL),
        "w_ff2": dense(ks[24], (L, D_FF, D_MODEL), D_FF),
        "norm_mlp_post": gain(ks[25], (L, D_MODEL)),
    }


def reference(x, mem, norm_mix_pre, w_in, na_rpb, ssm_lam_re, ssm_lam_im, ssm_log_dt,
              ssm_b_re, ssm_b_im, ssm_c_re, ssm_c_im, ssm_d, w_glu, b_glu, mem_norm,
              w_mem_kv, out_norm_na, out_norm_ssm, out_norm_mem, w_out, norm_mix_post,
              norm_mlp_pre, w_ff1, w_ff2, norm_mlp_post):
    B, S, _ = x.shape
    for l in range(DEPTH):
        h = rmsnorm(x, norm_mix_pre[l])
        proj = h @ w_in[l]
        q_na, k_na, v_na, u_ssm, q_mem = jnp.split(
            proj, [NA_WIDTH, 2 * NA_WIDTH, 3 * NA_WIDTH, 3 * NA_WIDTH + SSM_WIDTH], axis=-1)
        hs = (B, S, NA_HEADS, HEAD_DIM)
        y_na = neighbourhood_attention(q_na.reshape(hs), k_na.reshape(hs), v_na.reshape(hs), na_rpb[l])
        y_ssm = s5_mixer(u_ssm, ssm_lam_re[l], ssm_lam_im[l], ssm_log_dt[l], ssm_b_re[l], ssm_b_im[l],
                         ssm_c_re[l], ssm_c_im[l], ssm_d[l], w_glu[l], b_glu[l])
        y_mem = memory_cross_attention(q_mem, mem, mem_norm[l], w_mem_kv[l])
        y = jnp.concatenate([rmsnorm(y_na, out_norm_na[l]),
                             rmsnorm(y_ssm, out_norm_ssm[l]),
                             rmsnorm(y_mem, out_norm_mem[l])], axis=-1)
        x = x + rmsnorm(y @ w_out[l], norm_mix_post[l])
        h = rmsnorm(x, norm_mlp_pre[l])
        f = jnp.square(jax.nn.relu(h @ w_ff1[l])) @ w_ff2[l]
        x = x + rmsnorm(f, norm_mlp_post[l])
    return x
```

```python
import numpy as np
import concourse.bass as bass
import concourse.mybir as mybir
from contextlib import ExitStack
from concourse.bass_utils import run_bass_kernel_spmd
import os as _os


F32 = mybir.dt.float32
BF16 = mybir.dt.bfloat16
ALU = mybir.AluOpType
ACT = mybir.ActivationFunctionType
AX = mybir.AxisListType
ENGS = ["tensor", "vector", "scalar", "gpsimd", "sync"]


def apx(base, off, dims):
    return bass.AP(base.tensor, base.offset + off, [list(base.ap[0])] + [list(d) for d in dims])


def _names(aps):
    out = []
    for a in aps:
        if a is None or isinstance(a, (int, float)):
            continue
        out.append(a.tensor.name)
    return out


class Prog:
    def __init__(self, nc):
        self.nc = nc
        self.gs = ExitStack()
        self.sems = {e: self.gs.enter_context(nc.semaphore("s_" + e)) for e in ENGS}
        self.dsems = {}
        self.seq_base = {e: 0 for e in ENGS}
        self.dma_cnt = {}
        self.scopes = [self.gs]
        self._reset()

    def _reset(self):
        self.ops = {e: [] for e in ENGS}
        self.last_w = {}
        self.reads = {}

    def push_scope(self):
        es = ExitStack(); self.scopes.append(es); return es

    def pop_scope(self):
        self.scopes.pop().close()

    def _uniq(self, name):
        self._names = getattr(self, "_names", {})
        n = self._names.get(name, 0)
        self._names[name] = n + 1
        return name if n == 0 else "%s_v%d" % (name, n)

    def sb(self, name, shape, dt=F32):
        return self.scopes[-1].enter_context(self.nc.sbuf_tensor(self._uniq(name), list(shape), dt))

    def ps(self, name, shape, dt=F32):
        return self.scopes[-1].enter_context(self.nc.psum_tensor(self._uniq(name), list(shape), dt))

    def op(self, eng, fn, r=(), w=(), dma=None):
        deps = []
        for x in r:
            if x in self.last_w:
                deps.append((self.last_w[x], "raw"))
        for x in w:
            if x in self.last_w:
                deps.append((self.last_w[x], "waw"))
            for t in self.reads.get(x, []):
                deps.append((t, "war"))
        idx = len(self.ops[eng])
        if dma is not None:
            self.dma_cnt[dma] = self.dma_cnt.get(dma, 0) + 1
            tok = ("dma", dma, self.dma_cnt[dma])
        else:
            tok = ("eng", eng, idx)
        fdeps = []
        for d, kind in deps:
            if d[0] == "eng" and d[1] == eng and dma is None:
                if eng == "tensor" or kind != "raw":
                    continue
            fdeps.append(d)
        self.ops[eng].append(dict(fn=fn, deps=fdeps, tok=tok, dma=dma, signal=False))
        for x in r:
            self.reads.setdefault(x, []).append(tok)
        for x in w:
            self.last_w[x] = tok
            self.reads[x] = []
        return tok

    def dma(self, eng, out, in_, r=None, w=None, key=None):
        rr = _names([in_]) if r is None else r
        ww = _names([out]) if w is None else w
        k = key or ww[0]
        return self.op(eng, lambda e: e.dma_start(out=out, in_=in_), r=rr, w=ww, dma=k)

    def mm(self, out, lhsT, rhs, start=True, stop=True, r=None, w=None, **kw):
        rr = _names([lhsT, rhs]) if r is None else r
        ww = _names([out]) if w is None else w
        return self.op("tensor", lambda e: e.matmul(out, lhsT, rhs, start=start, stop=stop, **kw), r=rr, w=ww)

    def tr(self, out, in_, ident, r=None, w=None):
        rr = _names([in_, ident]) if r is None else r
        ww = _names([out]) if w is None else w
        return self.op("tensor", lambda e: e.transpose(out, in_, ident), r=rr, w=ww)

    def act(self, out, in_, func, scale=None, bias=None, accum_out=None, r=None, w=None):
        rr = _names([in_, scale, bias]) if r is None else r
        ww = _names([out, accum_out]) if w is None else w
        kw = {}
        if scale is not None: kw["scale"] = scale
        if bias is not None: kw["bias"] = bias
        if accum_out is not None: kw["accum_out"] = accum_out
        return self.op("scalar", lambda e: e.activation(out=out, in_=in_, func=func, **kw), r=rr, w=ww)

    def tt(self, eng, out, in0, in1, op, r=None, w=None):
        rr = _names([in0, in1]) if r is None else r
        ww = _names([out]) if w is None else w
        return self.op(eng, lambda e: e.tensor_tensor(out=out, in0=in0, in1=in1, op=op), r=rr, w=ww)

    def ts(self, eng, out, in0, s1, s2, op0, op1=None, accum_out=None, r=None, w=None):
        rr = _names([in0, s1, s2]) if r is None else r
        ww = _names([out, accum_out]) if w is None else w
        kw = {}
        if op1 is not None: kw["op1"] = op1
        if accum_out is not None: kw["accum_out"] = accum_out
        return self.op(eng, lambda e: e.tensor_scalar(out=out, in0=in0, scalar1=s1, scalar2=s2, op0=op0, **kw), r=rr, w=ww)

    def stt(self, out, in0, scalar, in1, op0, op1, accum_out=None, r=None, w=None):
        rr = _names([in0, scalar, in1]) if r is None else r
        ww = _names([out, accum_out]) if w is None else w
        kw = {}
        if accum_out is not None: kw["accum_out"] = accum_out
        return self.op("vector", lambda e: e.scalar_tensor_tensor(out=out, in0=in0, scalar=scalar, in1=in1, op0=op0, op1=op1, **kw), r=rr, w=ww)

    def copy(self, eng, out, in_, r=None, w=None):
        rr = _names([in_]) if r is None else r
        ww = _names([out]) if w is None else w
        if eng == "scalar":
            return self.op(eng, lambda e: e.activation(out=out, in_=in_, func=ACT.Copy), r=rr, w=ww)
        return self.op(eng, lambda e: e.tensor_copy(out=out, in_=in_), r=rr, w=ww)

    def memset(self, eng, ap, val):
        return self.op(eng, lambda e: e.memset(ap, val), r=[], w=_names([ap]))

    def recip(self, out, in_, r=None, w=None):
        rr = _names([in_]) if r is None else r
        ww = _names([out]) if w is None else w
        return self.op("vector", lambda e: e.reciprocal(out=out, in_=in_), r=rr, w=ww)

    def scan(self, out, d0, d1, initial, r=None, w=None):
        rr = _names([d0, d1, initial]) if r is None else r
        ww = _names([out]) if w is None else w
        return self.op("vector", lambda e: e.tensor_tensor_scan(out=out, data0=d0, data1=d1, initial=initial,
                                                                 op0=ALU.mult, op1=ALU.add), r=rr, w=ww)

    def flush(self, final_dma_keys=None):
        nc = self.nc
        ops = self.ops
        prod = set()
        for e in ENGS:
            for o in ops[e]:
                for d in o["deps"]:
                    if d[0] == "eng":
                        prod.add((d[1], d[2]))
        for e in ENGS:
            for i in range(len(ops[e]) - 1, -1, -1):
                if ops[e][i]["fn"] is not None and ops[e][i]["dma"] is None:
                    prod.add((e, i)); break
        seq = {}
        fin = {}
        for e in ENGS:
            c = self.seq_base[e]
            for i, o in enumerate(ops[e]):
                if (e, i) in prod:
                    c += 1
                    o["signal"] = True
                    seq[(e, i)] = c
            fin[e] = c
        for k in self.dma_cnt:
            if k not in self.dsems:
                self.dsems[k] = self.gs.enter_context(nc.semaphore("d_" + k))
        sems, dsems = self.sems, self.dsems
        dma_fin = dict(self.dma_cnt)
        seq_base = dict(self.seq_base)

        def make(e):
            def body(engine):
                waited = {}
                def wait(k, v):
                    if waited.get(k, 0) >= v:
                        return
                    waited[k] = v
                    s = sems[k[1]] if k[0] == "eng" else dsems[k[1]]
                    engine.wait_ge(s, v)
                for i, o in enumerate(ops[e]):
                    need = {}
                    for d in o["deps"]:
                        if d[0] == "eng":
                            k = ("eng", d[1]); v = seq[(d[1], d[2])]
                        else:
                            k = ("dma", d[1]); v = 16 * d[2]
                        if v > need.get(k, 0):
                            need[k] = v
                    for k, v in need.items():
                        wait(k, v)
                    if o["fn"] is None:
                        continue
                    ins = o["fn"](engine)
                    if o["dma"] is not None:
                        ins.then_inc(dsems[o["dma"]], 16)
                    elif o["signal"]:
                        ins.then_inc(sems[e], 1)
                for e2 in ENGS:
                    if fin[e2] > seq_base[e2]:
                        wait(("eng", e2), fin[e2])
                for k, n in dma_fin.items():
                    if n > 0:
                        wait(("dma", k), 16 * n)
            return body

        with nc.Block() as block:
            block.tensor(make("tensor"))
            block.vector(make("vector"))
            block.scalar(make("scalar"))
            block.gpsimd(make("gpsimd"))
            block.sync(make("sync"))
        self.seq_base = fin
        self._reset()

    def close(self):
        while len(self.scopes) > 1:
            self.pop_scope()
        self.gs.close()


TWO_PI = float(2 * np.pi)
MAGIC = 12582912.0
SHRINK = 1.0 - 2e-6


def host_ssm_layout(inp):
    lam_re = inp["ssm_lam_re"][0]; lam_im = inp["ssm_lam_im"][0]; log_dt = inp["ssm_log_dt"][0]
    b_re = inp["ssm_b_re"][0]; b_im = inp["ssm_b_im"][0]; c_re = inp["ssm_c_re"][0]; c_im = inp["ssm_c_im"][0]
    d = inp["ssm_d"][0]
    sc = np.zeros((128, 3, 32), np.float32)
    B = np.zeros((128, 2, 32, 16), np.float32)
    C = np.zeros((128, 2, 32, 16), np.float32)
    for gp in range(16):
        for dr in range(2):
            sl = gp * 2 + dr
            for gi in range(2):
                g = 2 * gp + gi
                rows = slice(gi * 64, gi * 64 + 64)
                sc[rows, 0, sl] = lam_re[dr, g]
                sc[rows, 1, sl] = lam_im[dr, g]
                sc[rows, 2, sl] = log_dt[dr, g]
                B[rows, 0, sl, :] = b_re[dr, g]
                B[rows, 1, sl, :] = b_im[dr, g]
                C[rows, 0, sl, :] = c_re[dr, g].T
                C[rows, 1, sl, :] = c_im[dr, g].T
    dcol = np.zeros((128, 32), np.float32)
    for g in range(32):
        dcol[:, g] = np.tile(d[g], 8)
    return sc, B, C, dcol


def host_consts():
    ident = np.eye(128, dtype=np.float32)
    iota = np.tile(np.arange(129, dtype=np.float32)[None, :], (128, 1))
    s_idx = np.arange(128) // 16
    maskF = (s_idx[None, :] >= s_idx[:, None]).astype(np.float32)
    maskB = (s_idx[:, None] >= s_idx[None, :]).astype(np.float32)
    return ident, iota, maskF, maskB


def cmul(P, eng, o_re, o_im, a_re, a_im, b_re, b_im, t1, t2):
    P.tt(eng, t1, a_im, b_im, ALU.mult)
    P.tt(eng, o_re, a_re, b_re, ALU.mult)
    P.tt(eng, o_re, o_re, t1, ALU.subtract)
    P.tt(eng, t2, a_im, b_re, ALU.mult)
    P.tt(eng, o_im, a_re, b_im, ALU.mult)
    P.tt(eng, o_im, o_im, t2, ALU.add)


def sin_of(P, out, x, tA, tB, shape_ap=None):
    P.ts("vector", tA, x, 1.0 / TWO_PI, MAGIC, ALU.mult, ALU.add)
    P.ts("vector", tA, tA, -MAGIC, -TWO_PI, ALU.add, ALU.mult)
    P.tt("vector", tB, tA, x, ALU.add)
    P.act(out, tB, ACT.Sin, scale=SHRINK)


def phase_s(P, nc, D, G, flush=True):
    es = P.push_scope()
    sc = P.sb("s_sc", [128, 3, 32]); Bt = P.sb("s_B", [128, 2, 32, 16]); Ct = P.sb("s_C", [128, 2, 32, 16])
    dcol = P.sb("s_dcol", [128, 32])
    maskF = P.sb("s_maskF", [128, 128]); maskB = P.sb("s_maskB", [128, 128])
    P.dma("sync", sc[:], D["ssm_sc"]); P.dma("sync", Bt[:], D["ssm_B"]); P.dma("sync", Ct[:], D["ssm_C"])
    P.dma("sync", dcol[:], D["ssm_dcol"])
    P.dma("sync", maskF[:], D["maskF"]); P.dma("sync", maskB[:], D["maskB"])
    identf = G["identf"]
    n = 0
    def T32(nm):
        return P.sb("s_" + nm, [128, 32])
    lre = T32("lre"); dt = T32("dt"); a = T32("a"); th = T32("th"); mag = T32("mag")
    sn = T32("sn"); cs = T32("cs"); tA = T32("tA"); tB = T32("tB"); thc = T32("thc")
    lbr = T32("lbr"); lbi = T32("lbi"); nr = T32("nr"); den = T32("den"); gre = T32("gre"); gim = T32("gim")
    ilr = T32("ilr"); ili = T32("ili"); t1 = T32("t1"); t2 = T32("t2")
    V = "vector"
    P.ts(V, lre[:], sc[:, 0, :], -1e-4, None, ALU.min)
    P.act(dt[:], sc[:, 2, :], ACT.Exp)
    P.tt(V, a[:], lre[:], dt[:], ALU.mult)
    P.tt(V, th[:], sc[:, 1, :], dt[:], ALU.mult)
    P.act(mag[:], a[:], ACT.Exp)
    sin_of(P, sn[:], th[:], tA[:], tB[:])
    P.ts(V, thc[:], th[:], float(np.pi / 2), None, ALU.add)
    sin_of(P, cs[:], thc[:], tA[:], tB[:])
    P.tt(V, lbr[:], mag[:], cs[:], ALU.mult)
    P.tt(V, lbi[:], mag[:], sn[:], ALU.mult)
    P.ts(V, nr[:], lbr[:], -1.0, None, ALU.add)
    P.tt(V, den[:], lre[:], lre[:], ALU.mult)
    P.tt(V, t1[:], sc[:, 1, :], sc[:, 1, :], ALU.mult)
    P.tt(V, den[:], den[:], t1[:], ALU.add)
    P.recip(den[:], den[:])
    P.tt(V, gre[:], nr[:], lre[:], ALU.mult)
    P.tt(V, t1[:], lbi[:], sc[:, 1, :], ALU.mult)
    P.tt(V, gre[:], gre[:], t1[:], ALU.add)
    P.tt(V, gre[:], gre[:], den[:], ALU.mult)
    P.tt(V, gim[:], lbi[:], lre[:], ALU.mult)
    P.tt(V, t1[:], nr[:], sc[:, 1, :], ALU.mult)
    P.tt(V, gim[:], gim[:], t1[:], ALU.subtract)
    P.tt(V, gim[:], gim[:], den[:], ALU.mult)
    P.tt(V, t1[:], mag[:], mag[:], ALU.mult)
    P.recip(t1[:], t1[:])
    P.tt(V, ilr[:], lbr[:], t1[:], ALU.mult)
    P.tt(V, ili[:], lbi[:], t1[:], ALU.mult)
    P.ts(V, ili[:], ili[:], -1.0, None, ALU.mult)
    PWr = P.sb("s_PWr", [128, 16, 32]); PWi = P.sb("s_PWi", [128, 16, 32])
    P.memset(V, PWr[:, 7, :], 1.0); P.memset(V, PWi[:, 7, :], 0.0)
    for k in range(0, 8):
        cmul(P, V, PWr[:, 8 + k, :], PWi[:, 8 + k, :], PWr[:, 7 + k, :], PWi[:, 7 + k, :], lbr[:], lbi[:], t1[:], t2[:])
    for k in range(0, 7):
        cmul(P, V, PWr[:, 6 - k, :], PWi[:, 6 - k, :], PWr[:, 7 - k, :], PWi[:, 7 - k, :], ilr[:], ili[:], t1[:], t2[:])
    LN = G["LN"]
    sqa_r = T32("sqa_r"); sqa_i = T32("sqa_i"); sqb_r = T32("sqb_r"); sqb_i = T32("sqb_i")
    cur = (PWr[:, 15, :], PWi[:, 15, :])
    bufs = [(sqa_r[:], sqa_i[:]), (sqb_r[:], sqb_i[:])]
    for i in range(7):
        o = (LN[:, 0, :], LN[:, 1, :]) if i == 6 else bufs[i % 2]
        cmul(P, V, o[0], o[1], cur[0], cur[1], cur[0], cur[1], t1[:], t2[:])
        cur = o
    P.act(G["R"][:], a[:], ACT.Exp, scale=8.0)
    P.ts(V, G["th8"][:], th[:], 8.0, None, ALU.mult)
    P.ts(V, G["a8"][:], a[:], 8.0, None, ALU.mult)
    PBr = P.sb("s_PBr", [128, 32, 8]); PBi = P.sb("s_PBi", [128, 32, 8])
    PCr = P.sb("s_PCr", [128, 32, 8]); PCi = P.sb("s_PCi", [128, 32, 8])
    PGr = P.sb("s_PGr", [128, 32, 8]); PGi = P.sb("s_PGi", [128, 32, 8])
    t8a = P.sb("s_t8a", [128, 32, 8]); t8b = P.sb("s_t8b", [128, 32, 8])
    Qr = P.sb("s_Qr", [128, 32, 8]); Qi = P.sb("s_Qi", [128, 32, 8])
    def gather(dst, src, k0f, stf, k0b, stb):
        P.copy(V, apx(dst[:], 0, [[16, 16], [1, 8]]), apx(src[:], k0f * 32, [[2, 16], [32 * stf, 8]]))
        P.copy(V, apx(dst[:], 8, [[16, 16], [1, 8]]), apx(src[:], k0b * 32 + 1, [[2, 16], [32 * stb, 8]]))
    gather(Qr, PWr, 14, -1, 7, 1); gather(Qi, PWi, 14, -1, 7, 1)
    gb_r = apx(gre[:], 0, [[1, 32], [0, 8]]); gb_i = apx(gim[:], 0, [[1, 32], [0, 8]])
    cmul(P, V, PBr[:], PBi[:], Qr[:], Qi[:], gb_r, gb_i, t8a[:], t8b[:])
    gather(PCr, PWr, 8, 1, 15, -1); gather(PCi, PWi, 8, 1, 15, -1)
    gather(PGr, PWr, 0, 1, 7, -1); gather(PGi, PWi, 0, 1, 7, -1)
    WB = G["WB"]; WC = G["WC"]; Tm = G["T"]
    NBS = 8
    Wr = P.sb("s_Wr", [128, NBS, 128]); Wi = P.sb("s_Wi", [128, NBS, 128])
    Gr = P.sb("s_Gr", [128, NBS, 128]); Gi = P.sb("s_Gi", [128, NBS, 128])
    X1 = P.sb("s_X1", [128, NBS, 128]); X2 = P.sb("s_X2", [128, NBS, 128])
    tf = P.sb("s_tf", [128, 128]); tb = P.sb("s_tb", [128, 128])
    pst = [P.ps("s_ps%d" % i, [128, 512]) for i in range(4)]
    def bc_coef(t, s0):
        return apx(t[:], s0 * 8, [[8, NBS], [1, 8], [0, 16]])
    def bc_mat(t, ri, s0):
        return apx(t[:], ri * 512 + s0 * 16, [[16, NBS], [0, 8], [1, 16]])
    def v4(t):
        return apx(t[:], 0, [[128, NBS], [16, 8], [1, 16]])
    for b in range(32 // NBS):
        s0 = b * NBS
        E = "vector" if b % 2 == 0 else "gpsimd"
        P.tt(E, v4(X1), bc_coef(PBi, s0), bc_mat(Bt, 1, s0), ALU.mult)
        P.tt(E, v4(Wr), bc_coef(PBr, s0), bc_mat(Bt, 0, s0), ALU.mult)
        P.tt(E, v4(Wr), v4(Wr), v4(X1), ALU.subtract)
        P.tt(E, v4(X2), bc_coef(PBi, s0), bc_mat(Bt, 0, s0), ALU.mult)
        P.tt(E, v4(Wi), bc_coef(PBr, s0), bc_mat(Bt, 1, s0), ALU.mult)
        P.tt(E, v4(Wi), v4(Wi), v4(X2), ALU.add)
        P.tt(E, v4(X1), bc_coef(PGi, s0), bc_mat(Ct, 1, s0), ALU.mult)
        P.tt(E, v4(Gr), bc_coef(PGr, s0), bc_mat(Ct, 0, s0), ALU.mult)
        P.tt(E, v4(Gr), v4(Gr), v4(X1), ALU.subtract)
        P.tt(E, v4(X2), bc_coef(PGi, s0), bc_mat(Ct, 0, s0), ALU.mult)
        P.tt(E, v4(Gi), bc_coef(PGr, s0), bc_mat(Ct, 1, s0), ALU.mult)
        P.tt(E, v4(Gi), v4(Gi), v4(X2), ALU.add)
        P.ts(E, Gi[:], Gi[:], -1.0, None, ALU.mult)
        for j in range(NBS):
            sl = s0 + j
            for ri, Wt in enumerate((Wr, Wi)):
                ps = pst[(j * 2 + ri) % 2]
                P.tr(ps[:, 0:128], Wt[:, j, :], identf[:])
                P.copy("scalar", WB[:, sl, ri, :], ps[:, 0:128])
        for jp in range(NBS // 2):
            gp = (s0 // 2) + jp
            jf = 2 * jp; jb = 2 * jp + 1
            for gi in range(2):
                g = 2 * gp + gi
                rows = slice(gi * 64, gi * 64 + 64)
                psf = pst[2]; psb = pst[3]
                P.mm(psf[:, 0:128], Wr[rows, jf, :], Gr[rows, jf, :], start=True, stop=False)
                P.mm(psf[:, 0:128], Wi[rows, jf, :], Gi[rows, jf, :], start=False, stop=True)
                P.mm(psb[:, 0:128], Wr[rows, jb, :], Gr[rows, jb, :], start=True, stop=False)
                P.mm(psb[:, 0:128], Wi[rows, jb, :], Gi[rows, jb, :], start=False, stop=True)
                P.tt(V, tf[:], psf[:, 0:128], maskF[:], ALU.mult)
                P.tt(V, tb[:], psb[:, 0:128], maskB[:], ALU.mult)
                P.tt(V, tf[:], tf[:], tb[:], ALU.add)
                P.stt(Tm[:, g, :], identf[:], dcol[:, g:g + 1], tf[:], ALU.mult, ALU.add)
        P.tt(E, v4(X1), bc_coef(PCi, s0), bc_mat(Ct, 1, s0), ALU.mult)
        P.tt(E, v4(X2), bc_coef(PCr, s0), bc_mat(Ct, 0, s0), ALU.mult)
        P.tt(E, apx(WC[:], s0 * 256, [[256, NBS], [16, 8], [1, 16]]), v4(X2), v4(X1), ALU.subtract)
        P.tt(E, v4(X1), bc_coef(PCi, s0), bc_mat(Ct, 0, s0), ALU.mult)
        P.tt(E, v4(X2), bc_coef(PCr, s0), bc_mat(Ct, 1, s0), ALU.mult)
        P.tt(E, v4(X2), v4(X2), v4(X1), ALU.add)
        P.ts(E, apx(WC[:], s0 * 256 + 128, [[256, NBS], [1, 128]]), X2[:], -1.0, None, ALU.mult)
    if flush:
        P.flush()
        P.pop_scope()


def make_tables(P, G, kind, flush=True):
    V = "vector"
    n = 129 if kind == "E" else 128
    P.push_scope()
    iota = G["iota"]; th8 = G["th8"]; a8 = G["a8"]
    X = P.sb("mt_X", [128, 4, n]); XA = P.sb("mt_XA", [128, 4, n]); XB = P.sb("mt_XB", [128, 4, n])
    Rp = P.sb("mt_Rp", [128, 4, n])
    for hq in range(8):
        sl = slice(hq * 4, (hq + 1) * 4)
        P.tt(V, X[:], apx(th8[:], hq * 4, [[1, 4], [0, n]]), apx(iota[:], 0, [[0, 4], [1, n]]), ALU.mult)
        if kind == "E":
            sin_of(P, G["TS"][:, sl, :], X[:], XA[:], XB[:])
            P.ts(V, X[:], X[:], float(np.pi / 2), None, ALU.add)
            sin_of(P, G["TC"][:, sl, :], X[:], XA[:], XB[:])
        else:
            P.tt(V, Rp[:], apx(a8[:], hq * 4, [[1, 4], [0, n]]), apx(iota[:], 0, [[0, 4], [1, n]]), ALU.mult)
            P.act(Rp[:], Rp[:], ACT.Exp)
            sin_of(P, XA[:], X[:], XA[:], XB[:])
            P.tt(V, G["Qi"][:, sl, 0:128], XA[:], Rp[:], ALU.mult)
            P.ts(V, X[:], X[:], float(np.pi / 2), None, ALU.add)
            sin_of(P, XA[:], X[:], XA[:], XB[:])
            P.tt(V, G["Qr"][:, sl, 0:128], XA[:], Rp[:], ALU.mult)
    if flush:
        P.flush()
        P.pop_scope()


EPS = 1e-6
GC = float(np.sqrt(2 / np.pi))


def host_carry_masks(core):
    M = np.zeros((128, 16, 2, 32), np.float32)
    for j in range(16):
        for k in range(2):
            u = 2 * core + k
            M[:, j, k, 0::2] = 1.0 if j < u else 0.0
            M[:, j, k, 1::2] = 1.0 if j < 15 - u else 0.0
    return M


def ssm_alloc(P, G):
    G["Wssm"] = P.sb("Wssm", [128, 16, 512], BF16)
    G["Rt"] = P.sb("Rt", [128, 32, 128], BF16)
    G["U"] = P.sb("U", [128, 32, 128], BF16)
    G["Floc"] = P.sb("Floc", [128, 16, 32, 2])
    G["acc"] = P.sb("acc", [128, 32, 4])
    G["ss8"] = P.sb("ss8", [128, 8]); G["rstd8"] = P.sb("rstd8", [128, 8])
    G["junk"] = P.sb("junk", [128, 128])
    G["Tm"] = [P.sb("Tmp%d" % i, [128, 4, 128]) for i in range(2)]
    G["Cin"] = P.sb("Cin", [128, 2, 32, 2])
    if "epst" not in G:
        G["epst"] = P.sb("epst", [128, 1])


def load_wssm(P, G, w_in_ap, gpre):
    stg = [P.sb("wstg%d" % i, [128, 512]) for i in range(2)]
    for k in range(16):
        s = stg[k % 2]
        P.dma("sync", s[:], w_in_ap[k * 128:(k + 1) * 128, 3072:3584])
        P.act(G["Wssm"][:, k, :], s[:], ACT.Copy, scale=gpre[:, k:k + 1])


def ssm_load(P, xsrc, tok0, xbf):
    src = xsrc[:, tok0:tok0 + 1024].rearrange("(k p) t -> p k t", p=128)
    for q in range(4):
        P.dma("gpsimd", xbf[:, 4 * q:4 * q + 4, :], src[:, 4 * q:4 * q + 4, :], key=xbf.name)


def ssm_unit(P, G, xsrc, tok0, xbf, PS, mode, uidx, kown=None, LO=None, preloaded=False):
    V = "vector"
    WB, R = G["WB"], G["R"]
    Rt, U = G["Rt"], G["U"]
    identf, identb = G["identf"], G["identb"]
    if not preloaded:
        ssm_load(P, xsrc, tok0, xbf)
    def xs(k, s):
        return apx(xbf[:], k * 1024 + s, [[8, 128]])
    for hs in range(2):
        psG = PS["f"][hs]
        for s4 in range(4):
            s = hs * 4 + s4
            for k in range(16):
                P.mm(psG[:, s4 * 128:(s4 + 1) * 128], xs(k, s), xs(k, s), start=(k == 0), stop=(k == 15))
        for s4 in range(4):
            s = hs * 4 + s4
            P.stt(G["junk"][:], psG[:, s4 * 128:(s4 + 1) * 128], 1.0, identf[:], ALU.mult, ALU.mult,
                  accum_out=G["ss8"][:, s:s + 1])
    P.act(G["rstd8"][:], G["ss8"][:], ACT.Sqrt, scale=1.0 / 2048, bias=G["epst"][:, 0:1])
    P.recip(G["rstd8"][:], G["rstd8"][:])
    for s in range(8):
        ps = PS["f"][2 + (s % 2)]
        for k in range(16):
            P.mm(ps[:, :], xs(k, s), G["Wssm"][:, k, :], start=(k == 0), stop=(k == 15))
        P.act(apx(Rt[:], s * 16, [[128, 32], [1, 16]]), apx(ps[:], 0, [[16, 32], [1, 16]]), ACT.Copy, scale=G["rstd8"][:, s:s + 1])
    for g8 in range(4):
        psT = PS["b"][g8 % 2]
        for gg in range(8):
            g = g8 * 8 + gg
            P.tr(psT[:, gg * 128:(gg + 1) * 128], Rt[:, g, :], identb[:])
        if g8 % 2 == 0:
            P.copy("scalar", U[:, g8 * 8:(g8 + 1) * 8, :], psT[:, :], w=["U%d" % g8], r=[psT.name])
        else:
            P.copy(V, U[:, g8 * 8:(g8 + 1) * 8, :], psT[:, :], w=["U%d" % g8], r=[psT.name])
    def hc(sl):
        gp = sl // 2
        psH = PS["f"][4 + sl % 2]
        ukey = ["U%d" % (gp // 4)]
        for ri in range(2):
            for gi in range(2):
                P.mm(psH[gi * 64:(gi + 1) * 64, ri * 128:(ri + 1) * 128], WB[:, sl, ri, gi * 64:(gi + 1) * 64], U[:, 2 * gp + gi, :],
                     start=True, stop=True, r=[WB.name] + ukey)
        return psH

    if mode == "G":
        for sl in range(32):
            dr = sl % 2
            psH = hc(sl)
            Qr, Qi = G["Qr"], G["Qi"]
            if dr == 0:
                qr = apx(Qr[:, sl, 0:128], 127, [[-1, 128]]); qi = apx(Qi[:, sl, 0:128], 127, [[-1, 128]])
            else:
                qr = Qr[:, sl, 0:128]; qi = Qi[:, sl, 0:128]
            hre = psH[:, 0:128]; him = psH[:, 128:256]
            acc = G["acc"]
            jk = G["Tm"][sl % 2]
            P.stt(jk[:, 0, :], hre, 1.0, qr, ALU.mult, ALU.mult, accum_out=acc[:, sl, 0:1])
            P.stt(jk[:, 1, :], him, 1.0, qi, ALU.mult, ALU.mult, accum_out=acc[:, sl, 1:2])
            P.stt(jk[:, 2, :], hre, 1.0, qi, ALU.mult, ALU.mult, accum_out=acc[:, sl, 2:3])
            P.stt(jk[:, 3, :], him, 1.0, qr, ALU.mult, ALU.mult, accum_out=acc[:, sl, 3:4])
    else:
        TC, TS = G["TC"], G["TS"]

        def stage_a(sl):
            dr = sl % 2
            psH = hc(sl)
            if dr == 0:
                hre = psH[:, 0:128]; him = psH[:, 128:256]
            else:
                hre = apx(psH[:], 127, [[-1, 128]]); him = apx(psH[:], 255, [[-1, 128]])
            c1 = TC[:, sl, 1:129]; s1 = TS[:, sl, 1:129]
            Tm = G["Tm"][sl % 2]; Dt = G["Dt"][sl % 2]
            P.tt(V, Tm[:, 0, :], hre, c1, ALU.mult)
            P.tt(V, Tm[:, 1, :], him, s1, ALU.mult)
            P.tt(V, Tm[:, 2, :], him, c1, ALU.mult)
            P.tt(V, Tm[:, 3, :], hre, s1, ALU.mult)
            P.tt(V, Dt[:, 0, :], Tm[:, 0, :], Tm[:, 1, :], ALU.add)
            P.tt(V, Dt[:, 1, :], Tm[:, 2, :], Tm[:, 3, :], ALU.subtract)

        def stage_b(sl):
            gp = sl // 2; dr = sl % 2
            Dt = G["Dt"][sl % 2]; Wb = G["Wb"][sl % 2]; Tm = LO["Tm2"][sl % 2]
            Rbc = apx(R[:], sl, [[0, 128]])
            for ri in range(2):
                P.scan(Wb[:, ri, 1:129], Rbc, Dt[:, ri, :], G["Cin"][:, kown, sl, ri:ri + 1])
            P.copy(V, Wb[:, :, 0:1], apx(G["Cin"][:], (kown * 32 + sl) * 2, [[1, 2], [1, 1]]))
            c0 = TC[:, sl, 0:128]; s0 = TS[:, sl, 0:128]
            Zp = LO["Zp"][gp % 2]
            wr = Wb[:, 0, 0:128]; wi = Wb[:, 1, 0:128]
            P.tt(V, Tm[:, 0, :], wr, c0, ALU.mult)
            P.tt(V, Tm[:, 1, :], wi, s0, ALU.mult)
            P.tt(V, Tm[:, 2, :], wr, s0, ALU.mult)
            P.tt(V, Tm[:, 3, :], wi, c0, ALU.mult)
            if dr == 0:
                zo_re = Zp[:, dr, 0, :]; zo_im = Zp[:, dr, 1, :]
            else:
                zo_re = apx(Zp[:], (dr * 2 + 0) * 128 + 127, [[-1, 128]])
                zo_im = apx(Zp[:], (dr * 2 + 1) * 128 + 127, [[-1, 128]])
            P.tt(V, zo_re, Tm[:, 0, :], Tm[:, 1, :], ALU.subtract)
            P.tt(V, zo_im, Tm[:, 2, :], Tm[:, 3, :], ALU.add)

        def stage_c(gp):
            Zp = LO["Zp"][gp % 2]
            WC, T = G["WC"], G["T"]
            psY = PS["f"][6 + ((gp // 2) % 2)]
            for gi in range(2):
                g = 2 * gp + gi
                col = ((gp % 2) * 2 + gi) * 128
                rows = slice(gi * 64, gi * 64 + 64)
                P.mm(psY[:, col:col + 128], T[:, g, :], U[:, g, :], start=True, stop=False, r=[T.name, "U%d" % (gp // 4)])
                n = 0
                for dr in range(2):
                    for ri in range(2):
                        n += 1
                        P.mm(psY[:, col:col + 128], WC[rows, gp * 2 + dr, ri, :], Zp[rows, dr, ri, :],
                             start=False, stop=(n == 4))
            if gp % 2 == 1:
                g0 = 2 * gp - 2
                Ysb = LO["Ysb"][(gp // 2) % 2]
                P.copy("scalar", Ysb[:], psY[:, :])
                psR = PS["f"][(gp // 2) % 2]
                for gg in range(4):
                    P.tr(psR[:, gg * 128:(gg + 1) * 128], Ysb[:, gg, :], identf[:])
                P.copy("scalar", apx(LO["Rout"][:], g0 * 16, [[16, 4], [512, 8], [1, 16]]),
                       apx(psR[:], 0, [[128, 4], [16, 8], [1, 16]]))

        stage_a(0)
        for sl in range(32):
            if sl + 1 < 32:
                stage_a(sl + 1)
            stage_b(sl)
            if sl % 2 == 1:
                stage_c(sl // 2)
    if mode == "G":
        acc = G["acc"]; Fl = G["Floc"]
        for par, ust in ((0, uidx), (1, 15 - uidx)):
            def a(c):
                return apx(acc[:], par * 4 + c, [[8, 16]])
            P.tt(V, apx(Fl[:], (ust * 32 + par) * 2 + 0, [[4, 16]]), a(0), a(1), ALU.subtract)
            P.tt(V, apx(Fl[:], (ust * 32 + par) * 2 + 1, [[4, 16]]), a(2), a(3), ALU.add)
    if mode == "L":
        for t in range(8):
            psR = PS["f"][2 + (t % 2)]
            for ch in range(4):
                P.tr(psR[:, ch * 128:(ch + 1) * 128], LO["Rout"][:, t, ch * 128:(ch + 1) * 128], identf[:])
            P.copy("scalar" if t % 2 == 0 else V, apx(LO["ysT"][:], t, [[1024, 4], [8, 128]]),
                   apx(psR[:], 0, [[128, 4], [1, 128]]))


def carry_compute(P, G, Mt):
    V = "vector"
    c_re = P.sb("cc_re", [128, 2, 32]); c_im = P.sb("cc_im", [128, 2, 32])
    n_re = P.sb("cn_re", [128, 2, 32]); n_im = P.sb("cn_im", [128, 2, 32])
    t1 = P.sb("cc_t1", [128, 2, 32]); t2 = P.sb("cc_t2", [128, 2, 32])
    LN = G["LN"]; Fl = G["Floc"]
    a_re = apx(LN[:], 0, [[0, 2], [1, 32]]); a_im = apx(LN[:], 32, [[0, 2], [1, 32]])
    P.memset(V, c_re[:], 0.0); P.memset(V, c_im[:], 0.0)
    for j in range(16):
        f_re = apx(Fl[:], j * 64 + 0, [[0, 2], [2, 32]]); f_im = apx(Fl[:], j * 64 + 1, [[0, 2], [2, 32]])
        m = Mt[:, j, :, :]
        P.tt(V, t1[:], a_im, c_im[:], ALU.mult)
        P.tt(V, n_re[:], a_re, c_re[:], ALU.mult)
        P.tt(V, n_re[:], n_re[:], t1[:], ALU.subtract)
        P.tt(V, t2[:], a_im, c_re[:], ALU.mult)
        P.tt(V, n_im[:], a_re, c_im[:], ALU.mult)
        P.tt(V, n_im[:], n_im[:], t2[:], ALU.add)
        P.tt(V, n_re[:], n_re[:], f_re, ALU.add)
        P.tt(V, n_im[:], n_im[:], f_im, ALU.add)
        P.tt(V, n_re[:], n_re[:], c_re[:], ALU.subtract)
        P.tt(V, n_im[:], n_im[:], c_im[:], ALU.subtract)
        P.tt(V, n_re[:], n_re[:], m, ALU.mult)
        P.tt(V, n_im[:], n_im[:], m, ALU.mult)
        P.tt(V, c_re[:], c_re[:], n_re[:], ALU.add)
        P.tt(V, c_im[:], c_im[:], n_im[:], ALU.add)
    Cin = G["Cin"]
    P.copy(V, apx(Cin[:], 0, [[64, 2], [2, 32]]), c_re[:])
    P.copy(V, apx(Cin[:], 1, [[64, 2], [2, 32]]), c_im[:])


def ssm_finish(P, G, LO, PS, kown):
    V = "vector"; PL = "gpsimd"
    ysT = LO["ysT"]
    for q in range(4):
        c0 = q * 256
        y = apx(ysT[:], c0, [[1024, 4], [1, 256]])
        a = LO["ga"]; b = LO["gb"]; gl = LO["gl"]; gbf = LO["gbf"]; zt = LO["zt"]; sq = LO["sq"]
        P.tt(PL, a[:], y, y, ALU.mult)
        P.ts(PL, a[:], a[:], 0.044715 * GC, GC, ALU.mult, ALU.add)
        P.tt(PL, a[:], a[:], y, ALU.mult)
        P.act(b[:], a[:], ACT.Tanh)
        P.stt(gl[:], b[:], 1.0, y, ALU.add, ALU.mult)
        P.act(gbf[:], gl[:], ACT.Copy, scale=0.5)
        for co in range(4):
            ps = PS["f"][4 + (co % 2)]
            for ci in range(4):
                P.mm(ps[:, 0:256], LO["Wglu"][:, ci, co * 128:(co + 1) * 128], gbf[:, ci, :], start=(ci == 0), stop=(ci == 3))
            P.act(zt[:, co, :], ps[:, 0:256], ACT.Sigmoid, bias=G["bglu"][:, co:co + 1])
        P.stt(gl[:], gl[:], 0.5, zt[:], ALU.mult, ALU.mult)
        P.tt(PL, sq[:], gl[:], gl[:], ALU.mult)
        pss = PS["f"][6]
        for ci in range(4):
            P.mm(pss[:, 0:256], G["onesb"][:], sq[:, ci, :], start=(ci == 0), stop=(ci == 3))
        rs = LO["rs"]
        P.act(rs[:], pss[:, 0:256], ACT.Ln, scale=1.0 / 512, bias=G["epst"][:, 0:1])
        P.act(rs[:], rs[:], ACT.Exp, scale=-0.5)
        for ci in range(4):
            P.stt(G["yssm_n"][:, ci, kown * 1024 + c0: kown * 1024 + c0 + 256], gl[:, ci, :], G["g_out"][:, 8 + ci:9 + ci],
                  rs[:], ALU.mult, ALU.mult)


EPS = 1e-6
NEG = -30000.0
QSCALE = float(128 ** -0.5)
NA_CLS = {0: (0, 0, 6), 1: (1, 2, 5), 14: (3, 28, 5), 15: (4, 28, 6)}


def na_class(j):
    if j in NA_CLS:
        return NA_CLS[j]
    return (2, 2 * j, 5)


def host_na_tables(rpb, core):
    GW, WH, WW = 64, 8, 16
    rows = 256
    tab = np.full((5, 8, 128, 768), NEG, np.float32)
    rep = {0: 0, 1: 1, 2: 6, 3: 14, 4: 15}
    for cls, j in rep.items():
        _, b, nch = na_class(j)
        q_halo_tok = (4 + 2 * j) * 64 + np.arange(128)
        q_glob = core * 2048 - 256 + q_halo_tok
        r = q_glob // GW; c = q_glob % GW
        rs = np.clip(r - WH // 2, 0, rows - WH); cs = np.clip(c - WW // 2, 0, GW - WW)
        for dr_ in range(WH):
            for dc_ in range(WW):
                kr = rs + dr_; kc = cs + dc_
                k_glob = kr * GW + kc
                k_halo = k_glob - (core * 2048 - 256)
                rel = k_halo - b * 64
                ok = (rel >= 0) & (rel < nch * 128)
                assert ok.all(), (cls, core)
                ch = rel // 128; kk = rel % 128
                oi = kr - r + WH - 1; oj = kc - c + WW - 1
                qi = np.arange(128)
                for h in range(8):
                    tab[cls, h, kk, ch * 128 + qi] = rpb[h, oi, oj]
    return tab


WCONV = {"w_in": [(0, 0), (0, 512), (0, 1024), (0, 1536), (0, 2048), (0, 2560), (0, 3584)],
         "w_out": [(0, c * 512) for c in range(4)],
         "w_ff1": [(0, c * 512) for c in range(16)]}
W_FF2 = [(jp * 2048, cg * 512) for cg in range(4) for jp in range(4)]


def conv_tasks_ff2(P, D):
    tasks = []
    for i, (r0, c0) in enumerate(W_FF2):
        def t(i=i, r0=r0, c0=c0):
            v = D["w_ff2"][r0:r0 + 2048, c0:c0 + 512].rearrange("(k p) c -> p k c", p=128)
            P.dma("gpsimd", D["wc_w_ff2"][i, :, 0:8, :], v[:, 0:8, :], key="wconv", w=["wc_w_ff2" + str(i)])
            P.dma("gpsimd", D["wc_w_ff2"][i, :, 8:16, :], v[:, 8:16, :], key="wconv", w=["wc_w_ff2" + str(i)])
        tasks.append(t)
    return tasks


def conv_tasks(P, D, G=None, S=None):
    tasks = []
    for name, lst in WCONV.items():
        gain = None
        for i, (r0, c0) in enumerate(lst):
            if gain is None:
                def t(name=name, i=i, r0=r0, c0=c0):
                    v = D[name][r0:r0 + 2048, c0:c0 + 512].rearrange("(k p) c -> p k c", p=128)
                    P.dma("gpsimd", D["wc_" + name][i, :, 0:8, :], v[:, 0:8, :], key="wconv", w=["wc_" + name + str(i)])
                    P.dma("gpsimd", D["wc_" + name][i, :, 8:16, :], v[:, 8:16, :], key="wconv", w=["wc_" + name + str(i)])
                tasks.append(t)
            else:
                for kq in range(4):
                    def t(name=name, i=i, r0=r0, c0=c0, kq=kq, gain=gain):
                        for k in range(kq * 4, kq * 4 + 4):
                            n = S["n"]; S["n"] += 1
                            st = S["cst"][n % 2]; sb = S["cbf"][n % 2]
                            P.dma("sync", st[:], D[name][r0 + k * 128:r0 + (k + 1) * 128, c0:c0 + 512])
                            P.act(sb[:], st[:], ACT.Copy, scale=G[gain][:, k:k + 1])
                            P.dma("sync", D["wc_" + name][i, :, k, :], sb[:], key="wconv2", w=["wc_" + name + str(i)])
                    tasks.append(t)
    return tasks


class WStream:
    def __init__(self, P, D, n=3):
        self.P = P; self.D = D
        self.bufs = [P.sb("wbuf%d" % i, [128, 16, 512], BF16) for i in range(n)]
        self.i = 0

    def preload(self, name, idx):
        b = self._load(name, idx)
        self.pre = getattr(self, "pre", [])
        self.pre.append((name, idx, b))

    def next(self, name, idx):
        pre = getattr(self, "pre", [])
        if pre:
            n2, i2, b = pre.pop(0)
            assert (n2, i2) == (name, idx), (n2, i2, name, idx)
            return b
        return self._load(name, idx)

    def _load(self, name, idx):
        b = self.bufs[self.i % len(self.bufs)]
        self.i += 1
        src = self.D["wc_" + name]
        self.P.dma("gpsimd", b[:, 0:8, :], src[idx, :, 0:8, :], key=b.name, r=["wc_" + name + str(idx)])
        self.P.dma("gpsimd", b[:, 8:16, :], src[idx, :, 8:16, :], key=b.name, r=["wc_" + name + str(idx)])
        return b


def xprep_gen(P, G, S, src, col0, gcol, xg, rstd_bc, ps_ss, want_col=None, ps_tr=None, stage=None):
    if stage is not None:
        v = src[:, col0:col0 + 512].rearrange("(k p) t -> p k t", p=128)
        P.dma("sync", stage[:, 0:8, :], v[:, 0:8, :], key=stage.name + "_ld", w=[stage.name])
        P.dma("sync", stage[:, 8:16, :], v[:, 8:16, :], key=stage.name + "_ld", w=[stage.name])
        acc = S["sacc"]; sq = S["ssq"]
        for k in range(16):
            P.act(xg[:, k, :], stage[:, k, :], ACT.Copy, scale=gcol[:, k:k + 1], w=[xg.name + str(k)])
            if k == 0:
                P.tt("gpsimd", acc[:], stage[:, k, :], stage[:, k, :], ALU.mult)
            else:
                P.tt("gpsimd", sq[k % 2][:], stage[:, k, :], stage[:, k, :], ALU.mult)
                P.tt("gpsimd", acc[:], acc[:], sq[k % 2][:], ALU.add)
        yield
        P.mm(ps_ss[:, :], G["onesf"][:], acc[:], start=True, stop=True)
    else:
        for k in range(16):
            st = S["xst"][k % len(S["xst"])]; sq = S["xsq"][k % 2]
            P.dma("sync", st[:], src[k * 128:(k + 1) * 128, col0:col0 + 512])
            P.act(xg[:, k, :], st[:], ACT.Copy, scale=gcol[:, k:k + 1], w=[xg.name + str(k)])
            P.tt("vector", sq[:], st[:], st[:], ALU.mult)
            P.mm(ps_ss[:, :], G["onesb"][:], sq[:], start=(k == 0), stop=(k == 15))
            if k % 4 == 3:
                yield
    P.act(rstd_bc[:], ps_ss[:, :], ACT.Ln, scale=1.0 / 2048, bias=G["epst"][:, 0:1])
    P.act(rstd_bc[:], rstd_bc[:], ACT.Exp, scale=-0.5)
    if want_col is not None:
        for j in range(4):
            P.tr(ps_tr[:, j * 128:(j + 1) * 128], rstd_bc[:, j * 128:(j + 1) * 128], G["identf"][:])
        P.copy("vector", want_col[:], apx(ps_tr[:], 0, [[128, 4]]))
    yield


def xprep(*a, **k):
    for _ in xprep_gen(*a, **k):
        pass


def mem_kv(P, G, D, flush=True):
    P.push_scope()
    Wkv = P.sb("m_wkv", [128, 16, 1024], BF16)
    for q in range(4):
        P.dma("gpsimd", Wkv[:, 4 * q:4 * q + 4, :], D["w_mem_kv"].rearrange("(k p) c -> p k c", p=128)[:, 4 * q:4 * q + 4, :], key="m_wkv")
    mg = P.sb("m_mg", [128, 16, 256], BF16)
    st = [P.sb("m_st%d" % i, [128, 256]) for i in range(2)]
    sq = [P.sb("m_sq%d" % i, [128, 256], BF16) for i in range(2)]
    rs = P.sb("m_rs", [128, 256]); rc = P.sb("m_rc", [128, 2])
    ps = [P.ps("m_ps%d" % i, [128, 512]) for i in range(3)]
    for k in range(16):
        P.dma("sync", st[k % 2][:], D["memT"][k * 128:(k + 1) * 128, :])
        P.act(mg[:, k, :], st[k % 2][:], ACT.Copy, scale=G["g_mem"][:, k:k + 1])
        P.tt("vector", sq[k % 2][:], st[k % 2][:], st[k % 2][:], ALU.mult)
        P.mm(ps[0][:, 0:256], G["onesb"][:], sq[k % 2][:], start=(k == 0), stop=(k == 15))
    P.act(rs[:], ps[0][:, 0:256], ACT.Ln, scale=1.0 / 2048, bias=G["epst"][:, 0:1])
    P.act(rs[:], rs[:], ACT.Exp, scale=-0.5)
    for j in range(2):
        P.tr(ps[1][:, j * 128:(j + 1) * 128], rs[:, j * 128:(j + 1) * 128], G["identf"][:])
    P.copy("vector", rc[:], apx(ps[1][:], 0, [[128, 2]]))
    for h in range(4):
        pp = ps[h % 2 + 1] if False else ps[2]
        for k in range(16):
            P.mm(pp[:, 0:256], Wkv[:, k, h * 128:(h + 1) * 128], mg[:, k, :], start=(k == 0), stop=(k == 15))
        P.tt("vector", G["kmemT"][:, h, :], pp[:, 0:256], rs[:], ALU.mult)
    for c in range(2):
        pp = ps[c]
        for k in range(16):
            P.mm(pp[:, :], mg[:, k, c * 128:(c + 1) * 128], Wkv[:, k, 512:1024], start=(k == 0), stop=(k == 15))
        P.act(G["Vmem"][:, c, :], pp[:, :], ACT.Copy, scale=rc[:, c:c + 1])
    if flush:
        P.flush()
        P.pop_scope()


def sweep1(P, G, D, nq=4, dbg=None):
    V = "vector"
    P.push_scope()
    S = {}
    S["sacc"] = P.sb("sacc", [128, 512]); S["ssq"] = [P.sb("ssq%d" % i, [128, 512]) for i in range(2)]
    S["xst"] = S["ssq"]
    S["xsq"] = [P.sb("xsq%d" % i, [128, 512], BF16) for i in range(2)]
    xg = P.sb("xg", [128, 16, 512], BF16)
    rstd = P.sb("rstd_bc", [128, 512]); rcol = P.sb("rstd_col", [128, 4])
    kT = [P.sb("kT%d" % i, [128, 8, 512], BF16) for i in range(3)]
    Vr = [P.sb("Vr%d" % i, [128, 4, 1024], BF16) for i in range(3)]
    qT = P.sb("qT", [128, 8, 512], BF16); qmT = P.sb("qmT", [128, 4, 512], BF16)
    ot = P.sb("ot", [128, 16, 512])
    yna = ot[:, 0:8, :]; ymem = ot[:, 8:12, :]
    ymix_na = P.sb("ymix_na", [128, 8, 512], BF16); ymix_mem = P.sb("ymix_mem", [128, 4, 512], BF16)
    tabt = [P.sb("tab%d" % i, [128, 768]) for i in range(2)]
    Ssb = [P.sb("Ssb%d" % i, [128, 768]) for i in range(2)]
    PT = [P.sb("PT%d" % i, [128, 768], BF16) for i in range(2)]
    rsum = P.sb("rsum", [128, 512]); sqs = [P.sb("sqs%d" % i, [128, 512], BF16) for i in range(2)]
    Pm = P.sb("Pm", [128, 2, 512], BF16)
    rs2 = P.sb("rs2", [128, 512])
    ps = [P.ps("s1ps%d" % i, [128, 512]) for i in range(8)]
    W = WStream(P, D, 2)
    xsrc = D["xT_own"]
    cnt = {"p": 0}
    def pbank():
        cnt["p"] += 1
        return ps[1 + (cnt["p"] % 2)]

    def kv_tile_gen(m, fast=False):
        slot = m % 3
        for _ in xprep_gen(P, G, S, xsrc, 512 * m, G["gpre"], xg, rstd, ps[0], want_col=rcol, ps_tr=ps[1],
                           stage=(ot if fast else None)):
            yield
        for half in range(2):
            wb = W.next("w_in", 2 + half)
            for hh in range(4):
                pb = pbank()
                for k in range(16):
                    P.mm(pb[:, :], wb[:, k, hh * 128:(hh + 1) * 128], xg[:, k, :], start=(k == 0), stop=(k == 15),
                         r=[wb.name, xg.name + str(k)])
                    if k == 7:
                        yield
                P.tt(V, kT[slot][:, half * 4 + hh, :], pb[:, :], rstd[:], ALU.mult)
                yield
        for half in range(2):
            wb = W.next("w_in", 4 + half)
            for j in range(4):
                pb = pbank()
                for k in range(16):
                    P.mm(pb[:, :], xg[:, k, j * 128:(j + 1) * 128], wb[:, k, :], start=(k == 0), stop=(k == 15),
                         r=[wb.name, xg.name + str(k)])
                    if k == 7:
                        yield
                P.act(Vr[slot][:, j, half * 512:(half + 1) * 512], pb[:, :], ACT.Copy, scale=rcol[:, j:j + 1])
                yield

    def kv_tile(m):
        for _ in kv_tile_gen(m, fast=True):
            pass

    def q_prep_gen(i, fast):
        for _ in xprep_gen(P, G, S, xsrc, 512 * i + 256, G["gpre"], xg, rstd, ps[0] if fast else ps[3],
                           stage=(ot if fast else None)):
            yield

    def q_tile(i):
        for half in range(2):
            wb = W.next("w_in", half)
            for hh in range(4):
                pb = pbank()
                for k in range(16):
                    P.mm(pb[:, :], wb[:, k, hh * 128:(hh + 1) * 128], xg[:, k, :], start=(k == 0), stop=(k == 15),
                         r=[wb.name, xg.name + str(k)])
                P.tt(V, qT[:, half * 4 + hh, :], pb[:, :], rstd[:], ALU.mult)
        wb = W.next("w_in", 6)
        for hh in range(4):
            pb = pbank()
            for k in range(16):
                P.mm(pb[:, :], wb[:, k, hh * 128:(hh + 1) * 128], xg[:, k, :], start=(k == 0), stop=(k == 15),
                     r=[wb.name, xg.name + str(k)])
            P.tt(V, qmT[:, hh, :], pb[:, :], rstd[:], ALU.mult)

    def na(i, filler=None):
        steps = [(jj, h) for jj in range(4) for h in range(8)]
        SA = [(ps[3], ps[4]), (ps[5], ps[6])]

        def scores(n):
            jj, h = steps[n]
            j = 4 * i + jj
            cls, b, nch = na_class(j)
            tb = tabt[n % 2]
            pa, pb_ = SA[n % 2]
            P.dma("sync", tb[:, 0:nch * 128], D["na_tab"][cls, h, :, 0:nch * 128])
            for c in range(nch):
                idx = b // 2 + c
                kt = kT[(idx // 4) % 3]
                pb = pa if c < 4 else pb_
                cc = c % 4
                P.mm(pb[:, cc * 128:(cc + 1) * 128], kt[:, h, (idx % 4) * 128:(idx % 4 + 1) * 128],
                     qT[:, h, jj * 128:(jj + 1) * 128], start=True, stop=True)

        def soft(n):
            jj, h = steps[n]
            j = 4 * i + jj
            cls, b, nch = na_class(j)
            tb = tabt[n % 2]; Sb = Ssb[n % 2]; Pt = PT[n % 2]
            pa, pb_ = SA[n % 2]
            P.stt(Sb[:, 0:512], pa[:, 0:512], QSCALE, tb[:, 0:512], ALU.mult, ALU.add)
            w2 = (nch - 4) * 128
            P.stt(Sb[:, 512:512 + w2], pb_[:, 0:w2], QSCALE, tb[:, 512:512 + w2], ALU.mult, ALU.add)
            P.act(Pt[:, 0:nch * 128], Sb[:, 0:nch * 128], ACT.Exp)

        def rest(n):
            jj, h = steps[n]
            j = 4 * i + jj
            cls, b, nch = na_class(j)
            Pt = PT[n % 2]
            for c in range(nch):
                idx = b // 2 + c
                vt = Vr[(idx // 4) % 3]
                P.mm(ps[7][:, 0:128], vt[:, idx % 4, h * 128:(h + 1) * 128], Pt[:, c * 128:(c + 1) * 128],
                     start=(c == 0), stop=(c == nch - 1))
            for c in range(nch):
                P.mm(ps[7][:, 128:256], G["onesb"][:], Pt[:, c * 128:(c + 1) * 128], start=(c == 0), stop=(c == nch - 1))
            P.act(rsum[:, 0:128], ps[7][:, 128:256], ACT.Ln)
            P.act(rsum[:, 0:128], rsum[:, 0:128], ACT.Exp, scale=-1.0)
            P.tt(V, yna[:, h, jj * 128:(jj + 1) * 128], ps[7][:, 0:128], rsum[:, 0:128], ALU.mult)

        scores(0)
        soft(0)
        for n in range(32):
            if n + 1 < 32:
                scores(n + 1)
                soft(n + 1)
            if filler is not None:
                next(filler, None)
            rest(n)
            if filler is not None and n % 4 == 3:
                next(filler, None)
        if filler is not None:
            for _ in filler:
                pass

    def memattn(i):
        for h in range(4):
            for c in range(2):
                P.mm(ps[3 + c][:, :], G["kmemT"][:, h, c * 128:(c + 1) * 128], qmT[:, h, :], start=True, stop=True)
                P.act(Pm[:, c, :], ps[3 + c][:, :], ACT.Exp, scale=QSCALE)
            pb = pbank()
            for c in range(2):
                P.mm(pb[:, :], G["Vmem"][:, c, h * 128:(h + 1) * 128], Pm[:, c, :], start=(c == 0), stop=(c == 1))
            pb2 = pbank()
            for c in range(2):
                P.mm(pb2[:, :], G["onesb"][:], Pm[:, c, :], start=(c == 0), stop=(c == 1))
            P.act(rsum[:], pb2[:, :], ACT.Ln)
            P.act(rsum[:], rsum[:], ACT.Exp, scale=-1.0)
            P.tt(V, ymem[:, h, :], pb[:, :], rsum[:], ALU.mult)

    def groupnorm(y, nch, gcol0, out, D_):
        for c in range(nch):
            P.tt(V, sqs[c % 2][:], y[:, c, :], y[:, c, :], ALU.mult)
            P.mm(ps[0][:, :], G["onesb"][:], sqs[c % 2][:], start=(c == 0), stop=(c == nch - 1))
        P.act(rs2[:], ps[0][:, :], ACT.Ln, scale=1.0 / D_, bias=G["epst"][:, 0:1])
        P.act(rs2[:], rs2[:], ACT.Exp, scale=-0.5)
        for c in range(nch):
            P.stt(out[:, c, :], y[:, c, :], G["g_out"][:, gcol0 + c:gcol0 + c + 1], rs2[:], ALU.mult, ALU.mult)

    def wout(i, filler=None):
        pend = []
        for bq in range(4):
            if filler is not None:
                next(filler, None); next(filler, None)
            wb = W.next("w_out", bq)
            for dc in range(4):
                pb = pbank()
                for m in range(16):
                    if m < 8:
                        rhs = ymix_na[:, m, :]
                    elif m < 12:
                        rhs = G["yssm_n"][:, m - 8, 512 * i:512 * (i + 1)]
                    else:
                        rhs = ymix_mem[:, m - 12, :]
                    P.mm(pb[:, :], wb[:, m, dc * 128:(dc + 1) * 128], rhs, start=(m == 0), stop=(m == 15))
                d = bq * 4 + dc
                while pend:
                    pend.pop(0)()
                P.copy("scalar", ot[:, d, :], pb[:, :])
                P.tt(V, sqs[d % 2][:], ot[:, d, :], ot[:, d, :], ALU.mult)
                pend.append(lambda d=d: P.mm(ps[0][:, :], G["onesb"][:], sqs[d % 2][:], start=(d == 0), stop=(d == 15)))
        while pend:
            pend.pop(0)()
        P.act(rs2[:], ps[0][:, :], ACT.Ln, scale=1.0 / 2048, bias=G["epst"][:, 0:1])
        P.act(rs2[:], rs2[:], ACT.Exp, scale=-0.5)
        if filler is not None:
            for _ in filler:
                pass
        if i + 1 < nq:
            W.preload("w_in", 0); W.preload("w_in", 1)
        for d in range(16):
            P.stt(ot[:, d, :], ot[:, d, :], G["g_post"][:, d:d + 1], rs2[:], ALU.mult, ALU.mult)
        x1v = D["x1T"][:, 512 * i:512 * (i + 1)].rearrange("(k p) t -> p k t", p=128)
        P.op("gpsimd", lambda e, x1v=x1v: e.dma_start(out=x1v, in_=ot[:], accum_op=ALU.add),
             r=[ot.name, "x1T_%d" % i], w=["x1T_%d" % i], dma="x1acc")

    kv_tile(0)
    kv_tile(1)
    for i in range(nq):
        P.dma("sync", D["x1T"][:, 512 * i:512 * (i + 1)], xsrc[:, 512 * i + 256:512 * i + 768], key="x1cp", w=["x1T_%d" % i])
        if i == 0:
            for _ in q_prep_gen(0, True):
                pass
        q_tile(i)
        na(i, kv_tile_gen(i + 2) if i + 2 <= 4 else None)
        memattn(i)
        groupnorm(yna, 8, 0, ymix_na, 1024)
        groupnorm(ymem, 4, 12, ymix_mem, 512)
        if dbg is not None and i == 0:
            P.dma("sync", dbg["yna"], yna, key="dbg_yna"); P.dma("sync", dbg["ymem"], ymem, key="dbg_ymem")
        wout(i, q_prep_gen(i + 1, False) if i + 1 < nq else None)
    P.flush()
    P.pop_scope()


def sweep2(P, G, D, nq=4):
    V = "vector"
    P.push_scope()
    S = {}
    S["sacc"] = P.sb("fsacc", [128, 512]); S["ssq"] = [P.sb("fssq%d" % i, [128, 512]) for i in range(2)]
    h2s = [P.sb("h2_%d" % i, [128, 16, 512], BF16) for i in range(2)]
    S["xst"] = S["ssq"]
    S["xsq"] = [P.sb("fxsq%d" % i, [128, 512], BF16) for i in range(2)]
    hid = P.sb("hid", [128, 64, 512], BF16)
    ft = P.sb("ft", [128, 16, 512])
    rstds = [P.sb("f_rstd%d" % i, [128, 512]) for i in range(2)]; rs2 = P.sb("f_rs2", [128, 512]); r4 = P.sb("f_r4", [128, 512])
    rl = [P.sb("f_rl%d" % i, [128, 512]) for i in range(2)]
    sqs = [P.sb("f_sqs%d" % i, [128, 512], BF16) for i in range(4)]
    ps = [P.ps("s2ps%d" % i, [128, 512]) for i in range(8)]
    W = WStream(P, D, 2)
    n = 0
    for i in range(nq):
        P.dma("sync", D["outT"][:, 512 * i:512 * (i + 1)], D["x1T"][:, 512 * i:512 * (i + 1)], key="ocp", w=["outT_%d" % i], r=[])
        h2 = h2s[i % 2]; rstd = rstds[i % 2]
        if i == 0:
            xprep(P, G, S, D["x1T"], 0, G["g_pre2"], h2, rstd, ps[0], stage=ft)
        filler = None
        if i + 1 < nq:
            filler = xprep_gen(P, G, S, D["x1T"], 512 * (i + 1), G["g_pre2"], h2s[(i + 1) % 2], rstds[(i + 1) % 2], ps[0])
        for bq in range(16):
            wb = W.next("w_ff1", bq)
            for c4 in range(4):
                pb = ps[1 + (n % 2)]; r = rl[n % 2]; n += 1
                for k in range(16):
                    P.mm(pb[:, :], wb[:, k, c4 * 128:(c4 + 1) * 128], h2[:, k, :], start=(k == 0), stop=(k == 15),
                         r=[wb.name, h2.name + str(k)])
                P.act(r[:], pb[:, :], ACT.Relu)
                P.tt(V, hid[:, bq * 4 + c4, :], r[:], r[:], ALU.mult)
        pend2 = []
        for cg in range(4):
            for jp in range(4):
                wb = W.next("w_ff2", cg * 4 + jp)
                if filler is not None:
                    next(filler, None)
                if jp == 1:
                    while pend2:
                        pend2.pop(0)()
                for jj in range(16):
                    for dc in range(4):
                        P.mm(ps[4 + dc][:, :], wb[:, jj, dc * 128:(dc + 1) * 128], hid[:, jp * 16 + jj, :],
                             start=(jp == 0 and jj == 0), stop=(jp == 3 and jj == 15))
            for dc in range(4):
                d = cg * 4 + dc
                P.copy("scalar", ft[:, d, :], ps[4 + dc][:, :])
                P.tt(V, sqs[d % 4][:], ft[:, d, :], ft[:, d, :], ALU.mult)
                pend2.append(lambda d=d: P.mm(ps[3][:, :], G["onesb"][:], sqs[d % 4][:], start=(d == 0), stop=(d == 15)))
        while pend2:
            pend2.pop(0)()
        if i + 1 < nq:
            W.preload("w_ff1", 0); W.preload("w_ff1", 1)
        P.tt(V, r4[:], rstd[:], rstd[:], ALU.mult)
        P.tt(V, rs2[:], r4[:], r4[:], ALU.mult)
        P.tt(V, rs2[:], rs2[:], ps[3][:, :], ALU.mult)
        P.act(rs2[:], rs2[:], ACT.Ln, scale=1.0 / 2048, bias=G["epst"][:, 0:1])
        P.act(rs2[:], rs2[:], ACT.Exp, scale=-0.5)
        P.tt(V, rs2[:], rs2[:], r4[:], ALU.mult)
        for d in range(16):
            P.stt(ft[:, d, :], ft[:, d, :], G["g_post2"][:, d:d + 1], rs2[:], ALU.mult, ALU.mult)
        ov = D["outT"][:, 512 * i:512 * (i + 1)].rearrange("(k p) t -> p k t", p=128)
        P.op("gpsimd", lambda e, ov=ov: e.dma_start(out=ov, in_=ft[:], accum_op=ALU.add),
             r=[ft.name, "outT_%d" % i], w=["outT_%d" % i], dma="oacc")
    P.flush()
    P.pop_scope()


def colvec(v):
    return np.ascontiguousarray(np.asarray(v, np.float32).reshape(-1, 128).T)


def build_program(shapes, debug=False, stages="SGLM12"):
    nc = bass.Bass("TRN2", target_bir_lowering=False)
    D = {}
    for name, shp in shapes.items():
        D[name] = nc.dram_tensor(name, list(shp), F32, kind="ExternalInput").ap()
    D["x1T"] = nc.dram_tensor("x1T", [2048, 2048], F32, kind="Internal").ap()
    for nm, lst in WCONV.items():
        D["wc_" + nm] = nc.dram_tensor("wc_" + nm, [len(lst), 128, 16, 512], BF16, kind="Internal").ap()
    D["wc_w_ff2"] = nc.dram_tensor("wc_w_ff2", [16, 128, 16, 512], BF16, kind="Internal").ap()
    D["outT"] = nc.dram_tensor("outT", [2048, 2048], F32, kind="ExternalOutput").ap()
    dbg = None
    if debug:
        dbg = {"yna": nc.dram_tensor("dbg_yna", [128, 8, 512], F32, kind="ExternalOutput").ap(),
               "ymem": nc.dram_tensor("dbg_ymem", [128, 4, 512], F32, kind="ExternalOutput").ap(),
               "yssm": nc.dram_tensor("dbg_yssm", [128, 4, 2048], BF16, kind="ExternalOutput").ap(),
               "x1T": nc.dram_tensor("dbg_x1T", [2048, 2048], F32, kind="ExternalOutput").ap()}
    P = Prog(nc)
    G = {}
    G["identf"] = P.sb("identf", [128, 128]); G["identb"] = P.sb("identb", [128, 128], BF16)
    G["onesb"] = P.sb("onesb", [128, 128], BF16); G["onesf"] = P.sb("onesf", [128, 128])
    for nm in ["gpre", "g_out", "g_post", "g_pre2", "g_post2", "g_mem"]:
        G[nm] = P.sb("t_" + nm, [128, 16])
        P.dma("sync", G[nm][:], D[nm])
    G["bglu"] = P.sb("t_bglu", [128, 4]); P.dma("sync", G["bglu"][:], D["bglu"])
    G["epst"] = P.sb("epst", [128, 1])
    G["yssm_n"] = P.sb("yssm_n", [128, 4, 2048], BF16)
    G["kmemT"] = P.sb("kmemT", [128, 4, 256], BF16); G["Vmem"] = P.sb("Vmem", [128, 2, 512], BF16)
    P.dma("sync", G["identf"][:], D["ident"]); P.dma("gpsimd", G["identb"][:], D["ident"])
    P.memset("vector", G["onesb"][:], 1.0); P.memset("vector", G["onesf"][:], 1.0)
    P.memset("vector", G["epst"][:], EPS)
    CS = {"n": 0}
    ctasks = []
    if "S" in stages:
        P.push_scope()
        G["WB"] = P.sb("WB", [128, 32, 2, 128], BF16)
        G["WC"] = P.sb("WC", [128, 32, 2, 128], BF16)
        G["T"] = P.sb("T", [128, 32, 128], BF16)
        G["R"] = P.sb("R", [128, 32]); G["LN"] = P.sb("LN", [128, 2, 32])
        G["th8"] = P.sb("th8", [128, 32]); G["a8"] = P.sb("a8", [128, 32])
        G["iota"] = P.sb("iota_t", [128, 129]); P.dma("sync", G["iota"][:], D["iota"])
        TA = P.sb("TA", [128, 32, 129]); TB = P.sb("TB", [128, 32, 129])
        G["Qr"] = TA; G["Qi"] = TB; G["TC"] = TA; G["TS"] = TB
        ssm_alloc(P, G)
        P.memset("vector", G["Floc"][:], 0.0)
        ctasks = conv_tasks(P, D, G, CS)
        phase_s(P, nc, D, G, flush=False)
        load_wssm(P, G, D["w_in"], G["gpre"])
        make_tables(P, G, "Q", flush=False)
        P.flush()
        P.pop_scope(); P.pop_scope()
        P.push_scope()
        xbf = [P.sb("xbf%d" % i, [128, 16, 1024], BF16) for i in range(2)]
        PS = {"f": [P.ps("psf%d" % i, [128, 512]) for i in range(6)], "b": [P.ps("psb%d" % i, [128, 1024], BF16) for i in range(2)]}
        PS["f"] += [PS["f"][0], PS["f"][1]]
        ssm_load(P, D["xT_all"], 0, xbf[0])
        for u in range(16):
            if u + 1 < 16:
                ssm_load(P, D["xT_all"], (u + 1) * 1024, xbf[(u + 1) % 2])
            for _ in range(2):
                if ctasks:
                    ctasks.pop(0)()
            ssm_unit(P, G, D["xT_all"], u * 1024, xbf[u % 2], PS, "G", u, preloaded=True)
        P.flush()
        P.pop_scope()
        P.push_scope()
        Mt = P.sb("cmask_t", [128, 16, 2, 32])
        P.dma("sync", Mt[:], D["cmask"])
        carry_compute(P, G, Mt)
        make_tables(P, G, "E", flush=False)
        if "M" in stages:
            mem_kv(P, G, D, flush=False)
        P.flush()
        if "M" in stages:
            P.pop_scope()
        P.pop_scope(); P.pop_scope()
        P.push_scope()
        xb = P.sb("xbfL", [128, 16, 1024], BF16)
        G["Dt"] = [P.sb("Dt%d" % i, [128, 2, 128]) for i in range(2)]
        G["Wb"] = [P.sb("Wb%d" % i, [128, 2, 129]) for i in range(2)]
        LO = {}
        LO["Tm2"] = [P.sb("Tm2_%d" % i, [128, 4, 128]) for i in range(2)]
        LO["Zp"] = [P.sb("Zp%d" % i, [128, 2, 2, 128], BF16) for i in range(2)]
        LO["Ysb"] = [P.sb("Ysb%d" % i, [128, 4, 128]) for i in range(2)]
        LO["Rout"] = xb[:, 0:8, :].bitcast(F32)
        _yb = xb[:, 8:16, :].bitcast(F32)
        LO["ysT"] = apx(_yb, 0, [[1024, 4], [1, 1024]])
        for nm in ["ga", "gl", "zt"]:
            LO[nm] = P.sb("L_" + nm, [128, 4, 256])
        LO["gb"] = LO["ga"]
        LO["gbf"] = P.sb("L_gbf", [128, 4, 256], BF16); LO["sq"] = P.sb("L_sq", [128, 4, 256], BF16)
        LO["rs"] = P.sb("L_rs", [128, 256])
        LO["Wglu"] = P.sb("L_wglu", [128, 4, 512], BF16)
        P.dma("gpsimd", LO["Wglu"][:], D["w_glu"].rearrange("(k p) c -> p k c", p=128))
        PS = {"f": [P.ps("psLf%d" % i, [128, 512]) for i in range(6)], "b": [P.ps("psLb%d" % i, [128, 1024], BF16) for i in range(2)]}
        PS["f"] += [PS["f"][0], PS["f"][1]]
        ssm_load(P, D["xT_own"], 256, xb)
        for t in conv_tasks_ff2(P, D):
            t()
        for k in range(2):
            ssm_unit(P, G, D["xT_own"], 256 + k * 1024, xb, PS, "L", None, kown=k, LO=LO, preloaded=(k == 0))
            ssm_finish(P, G, LO, PS, k)
        if debug:
            P.dma("sync", dbg["yssm"], G["yssm_n"][:], key="dbg_yssm")
        P.flush()
        P.pop_scope()
        P.pop_scope()
    else:
        P.memset("vector", G["yssm_n"][:], 0.0)
        if "M" in stages:
            mem_kv(P, G, D)
    assert not ctasks
    if "1" in stages:
        sweep1(P, G, D, dbg=dbg)
        if debug:
            P.dma("sync", dbg["x1T"], D["x1T"], key="dbg_x1T")
    if "2" in stages:
        sweep2(P, G, D)
    P.flush()
    P.close()
    return nc


def host_inputs(inp, core, shared):
    m = dict(shared)
    t0 = core * 2048
    xT = shared["xT_all"]
    own = np.zeros((2048, 2560), np.float32)
    lo = t0 - 256; hi = t0 + 2304
    a = max(lo, 0); b = min(hi, 16384)
    own[:, a - lo:b - lo] = xT[:, a:b]
    m["xT_own"] = own
    m["na_tab"] = host_na_tables(np.asarray(inp["na_rpb"][0], np.float32), core)
    m["cmask"] = host_carry_masks(core)
    return m


def host_shared(inp):
    sc, B, C, dcol = host_ssm_layout(inp)
    ident, iota, maskF, maskB = host_consts()
    f = lambda a: np.ascontiguousarray(np.asarray(a, np.float32))
    sh = {"ssm_sc": sc, "ssm_B": B, "ssm_C": C, "ssm_dcol": dcol, "iota": iota, "maskF": maskF, "maskB": maskB, "ident": ident,
          "xT_all": f(np.asarray(inp["x"][0]).T), "w_in": f(inp["w_in"][0]), "w_out": f(inp["w_out"][0]),
          "w_ff1": f(inp["w_ff1"][0]), "w_ff2": f(inp["w_ff2"][0]), "w_glu": f(inp["w_glu"][0]),
          "w_mem_kv": f(inp["w_mem_kv"][0]), "memT": f(np.asarray(inp["mem"][0]).T),
          "gpre": colvec(inp["norm_mix_pre"][0]), "g_post": colvec(inp["norm_mix_post"][0]),
          "g_pre2": colvec(inp["norm_mlp_pre"][0]), "g_post2": colvec(inp["norm_mlp_post"][0]),
          "g_mem": colvec(inp["mem_norm"][0]),
          "g_out": colvec(np.concatenate([np.asarray(inp["out_norm_na"][0]), np.asarray(inp["out_norm_ssm"][0]),
                                          np.asarray(inp["out_norm_mem"][0])])),
          "bglu": colvec(inp["b_glu"][0])}
    return sh


_CACHE = {}


def kernel(**inputs):
    inp = {k: np.asarray(v) for k, v in inputs.items()}
    sh = host_shared(inp)
    maps = [host_inputs(inp, c, sh) for c in range(8)]
    shapes = {k: v.shape for k, v in maps[0].items()}
    debug = bool(int(_os.environ.get("MK_DEBUG", "0")))
    stages = _os.environ.get("MK_STAGES", "SGLM12")
    ncores = int(_os.environ.get("MK_CORES", "8"))
    nc = build_program(shapes, debug=debug, stages=stages)
    res = run_bass_kernel_spmd(nc, maps[:ncores], core_ids=list(range(ncores)))
    if debug:
        _CACHE["res"] = res.results
    out = np.zeros((1, 16384, 2048), np.float32)
    for c in range(ncores):
        out[0, c * 2048:(c + 1) * 2048, :] = res.results[c]["outT"].T
    return out
```

```python
import numpy as np
import concourse.bass as bass
import concourse.mybir as mybir
from contextlib import ExitStack
from concourse.bass_utils import run_bass_kernel_spmd
import os as _os


F32 = mybir.dt.float32
BF16 = mybir.dt.bfloat16
ALU = mybir.AluOpType
ACT = mybir.ActivationFunctionType
AX = mybir.AxisListType
ENGS = ["tensor", "vector", "scalar", "gpsimd", "sync"]


def apx(base, off, dims):
    return bass.AP(base.tensor, base.offset + off, [list(base.ap[0])] + [list(d) for d in dims])


def _names(aps):
    out = []
    for a in aps:
        if a is None or isinstance(a, (int, float)):
            continue
        out.append(a.tensor.name)
    return out


class Prog:
    def __init__(self, nc):
        self.nc = nc
        self.gs = ExitStack()
        self.sems = {e: self.gs.enter_context(nc.semaphore("s_" + e)) for e in ENGS}
        self.dsems = {}
        self.seq_base = {e: 0 for e in ENGS}
        self.dma_cnt = {}
        self.scopes = [self.gs]
        self._reset()

    def _reset(self):
        self.ops = {e: [] for e in ENGS}
        self.last_w = {}
        self.reads = {}

    def push_scope(self):
        es = ExitStack(); self.scopes.append(es); return es

    def pop_scope(self):
        self.scopes.pop().close()

    def _uniq(self, name):
        self._names = getattr(self, "_names", {})
        n = self._names.get(name, 0)
        self._names[name] = n + 1
        return name if n == 0 else "%s_v%d" % (name, n)

    def sb(self, name, shape, dt=F32):
        return self.scopes[-1].enter_context(self.nc.sbuf_tensor(self._uniq(name), list(shape), dt))

    def ps(self, name, shape, dt=F32):
        return self.scopes[-1].enter_context(self.nc.psum_tensor(self._uniq(name), list(shape), dt))

    def op(self, eng, fn, r=(), w=(), dma=None):
        deps = []
        for x in r:
            if x in self.last_w:
                deps.append((self.last_w[x], "raw"))
        for x in w:
            if x in self.last_w:
                deps.append((self.last_w[x], "waw"))
            for t in self.reads.get(x, []):
                deps.append((t, "war"))
        idx = len(self.ops[eng])
        if dma is not None:
            self.dma_cnt[dma] = self.dma_cnt.get(dma, 0) + 1
            tok = ("dma", dma, self.dma_cnt[dma])
        else:
            tok = ("eng", eng, idx)
        fdeps = []
        for d, kind in deps:
            if d[0] == "eng" and d[1] == eng and dma is None:
                if eng == "tensor" or kind != "raw":
                    continue
            fdeps.append(d)
        self.ops[eng].append(dict(fn=fn, deps=fdeps, tok=tok, dma=dma, signal=False))
        for x in r:
            self.reads.setdefault(x, []).append(tok)
        for x in w:
            self.last_w[x] = tok
            self.reads[x] = []
        return tok

    def dma(self, eng, out, in_, r=None, w=None, key=None):
        rr = _names([in_]) if r is None else r
        ww = _names([out]) if w is None else w
        k = key or ww[0]
        return self.op(eng, lambda e: e.dma_start(out=out, in_=in_), r=rr, w=ww, dma=k)

    def mm(self, out, lhsT, rhs, start=True, stop=True, r=None, w=None, **kw):
        rr = _names([lhsT, rhs]) if r is None else r
        ww = _names([out]) if w is None else w
        return self.op("tensor", lambda e: e.matmul(out, lhsT, rhs, start=start, stop=stop, **kw), r=rr, w=ww)

    def tr(self, out, in_, ident, r=None, w=None):
        rr = _names([in_, ident]) if r is None else r
        ww = _names([out]) if w is None else w
        return self.op("tensor", lambda e: e.transpose(out, in_, ident), r=rr, w=ww)

    def act(self, out, in_, func, scale=None, bias=None, accum_out=None, r=None, w=None):
        rr = _names([in_, scale, bias]) if r is None else r
        ww = _names([out, accum_out]) if w is None else w
        kw = {}
        if scale is not None: kw["scale"] = scale
        if bias is not None: kw["bias"] = bias
        if accum_out is not None: kw["accum_out"] = accum_out
        return self.op("scalar", lambda e: e.activation(out=out, in_=in_, func=func, **kw), r=rr, w=ww)

    def tt(self, eng, out, in0, in1, op, r=None, w=None):
        rr = _names([in0, in1]) if r is None else r
        ww = _names([out]) if w is None else w
        return self.op(eng, lambda e: e.tensor_tensor(out=out, in0=in0, in1=in1, op=op), r=rr, w=ww)

    def ts(self, eng, out, in0, s1, s2, op0, op1=None, accum_out=None, r=None, w=None):
        rr = _names([in0, s1, s2]) if r is None else r
        ww = _names([out, accum_out]) if w is None else w
        kw = {}
        if op1 is not None: kw["op1"] = op1
        if accum_out is not None: kw["accum_out"] = accum_out
        return self.op(eng, lambda e: e.tensor_scalar(out=out, in0=in0, scalar1=s1, scalar2=s2, op0=op0, **kw), r=rr, w=ww)

    def stt(self, out, in0, scalar, in1, op0, op1, accum_out=None, r=None, w=None):
        rr = _names([in0, scalar, in1]) if r is None else r
        ww = _names([out, accum_out]) if w is None else w
        kw = {}
        if accum_out is not None: kw["accum_out"] = accum_out
        return self.op("vector", lambda e: e.scalar_tensor_tensor(out=out, in0=in0, scalar=scalar, in1=in1, op0=op0, op1=op1, **kw), r=rr, w=ww)

    def copy(self, eng, out, in_, r=None, w=None):
        rr = _names([in_]) if r is None else r
        ww = _names([out]) if w is None else w
        if eng == "scalar":
            return self.op(eng, lambda e: e.activation(out=out, in_=in_, func=ACT.Copy), r=rr, w=ww)
        return self.op(eng, lambda e: e.tensor_copy(out=out, in_=in_), r=rr, w=ww)

    def memset(self, eng, ap, val):
        return self.op(eng, lambda e: e.memset(ap, val), r=[], w=_names([ap]))

    def recip(self, out, in_, r=None, w=None):
        rr = _names([in_]) if r is None else r
        ww = _names([out]) if w is None else w
        return self.op("vector", lambda e: e.reciprocal(out=out, in_=in_), r=rr, w=ww)

    def scan(self, out, d0, d1, initial, r=None, w=None):
        rr = _names([d0, d1, initial]) if r is None else r
        ww = _names([out]) if w is None else w
        return self.op("vector", lambda e: e.tensor_tensor_scan(out=out, data0=d0, data1=d1, initial=initial,
                                                                 op0=ALU.mult, op1=ALU.add), r=rr, w=ww)

    def flush(self, final_dma_keys=None):
        nc = self.nc
        ops = self.ops
        prod = set()
        for e in ENGS:
            for o in ops[e]:
                for d in o["deps"]:
                    if d[0] == "eng":
                        prod.add((d[1], d[2]))
        for e in ENGS:
            for i in range(len(ops[e]) - 1, -1, -1):
                if ops[e][i]["fn"] is not None and ops[e][i]["dma"] is None:
                    prod.add((e, i)); break
        seq = {}
        fin = {}
        for e in ENGS:
            c = self.seq_base[e]
            for i, o in enumerate(ops[e]):
                if (e, i) in prod:
                    c += 1
                    o["signal"] = True
                    seq[(e, i)] = c
            fin[e] = c
        for k in self.dma_cnt:
            if k not in self.dsems:
                self.dsems[k] = self.gs.enter_context(nc.semaphore("d_" + k))
        sems, dsems = self.sems, self.dsems
        dma_fin = dict(self.dma_cnt)
        seq_base = dict(self.seq_base)

        def make(e):
            def body(engine):
                waited = {}
                def wait(k, v):
                    if waited.get(k, 0) >= v:
                        return
                    waited[k] = v
                    s = sems[k[1]] if k[0] == "eng" else dsems[k[1]]
                    engine.wait_ge(s, v)
                for i, o in enumerate(ops[e]):
                    need = {}
                    for d in o["deps"]:
                        if d[0] == "eng":
                            k = ("eng", d[1]); v = seq[(d[1], d[2])]
                        else:
                            k = ("dma", d[1]); v = 16 * d[2]
                        if v > need.get(k, 0):
                            need[k] = v
                    for k, v in need.items():
                        wait(k, v)
                    if o["fn"] is None:
                        continue
                    ins = o["fn"](engine)
                    if o["dma"] is not None:
                        ins.then_inc(dsems[o["dma"]], 16)
                    elif o["signal"]:
                        ins.then_inc(sems[e], 1)
                for e2 in ENGS:
                    if fin[e2] > seq_base[e2]:
                        wait(("eng", e2), fin[e2])
                for k, n in dma_fin.items():
                    if n > 0:
                        wait(("dma", k), 16 * n)
            return body

        with nc.Block() as block:
            block.tensor(make("tensor"))
            block.vector(make("vector"))
            block.scalar(make("scalar"))
            block.gpsimd(make("gpsimd"))
            block.sync(make("sync"))
        self.seq_base = fin
        self._reset()

    def close(self):
        while len(self.scopes) > 1:
            self.pop_scope()
        self.gs.close()


TWO_PI = float(2 * np.pi)
MAGIC = 12582912.0
SHRINK = 1.0 - 2e-6


def host_ssm_layout(inp):
    lam_re = inp["ssm_lam_re"][0]; lam_im = inp["ssm_lam_im"][0]; log_dt = inp["ssm_log_dt"][0]
    b_re = inp["ssm_b_re"][0]; b_im = inp["ssm_b_im"][0]; c_re = inp["ssm_c_re"][0]; c_im = inp["ssm_c_im"][0]
    d = inp["ssm_d"][0]
    sc = np.zeros((128, 3, 32), np.float32)
    B = np.zeros((128, 2, 32, 16), np.float32)
    C = np.zeros((128, 2, 32, 16), np.float32)
    for gp in range(16):
        for dr in range(2):
            sl = gp * 2 + dr
            for gi in range(2):
                g = 2 * gp + gi
                rows = slice(gi * 64, gi * 64 + 64)
                sc[rows, 0, sl] = lam_re[dr, g]
                sc[rows, 1, sl] = lam_im[dr, g]
                sc[rows, 2, sl] = log_dt[dr, g]
                B[rows, 0, sl, :] = b_re[dr, g]
                B[rows, 1, sl, :] = b_im[dr, g]
                C[rows, 0, sl, :] = c_re[dr, g].T
                C[rows, 1, sl, :] = c_im[dr, g].T
    dcol = np.zeros((128, 32), np.float32)
    for g in range(32):
        dcol[:, g] = np.tile(d[g], 8)
    return sc, B, C, dcol


def host_consts():
    ident = np.eye(128, dtype=np.float32)
    iota = np.tile(np.arange(129, dtype=np.float32)[None, :], (128, 1))
    s_idx = np.arange(128) // 16
    maskF = (s_idx[None, :] >= s_idx[:, None]).astype(np.float32)
    maskB = (s_idx[:, None] >= s_idx[None, :]).astype(np.float32)
    return ident, iota, maskF, maskB


def cmul(P, eng, o_re, o_im, a_re, a_im, b_re, b_im, t1, t2):
    P.tt(eng, t1, a_im, b_im, ALU.mult)
    P.tt(eng, o_re, a_re, b_re, ALU.mult)
    P.tt(eng, o_re, o_re, t1, ALU.subtract)
    P.tt(eng, t2, a_im, b_re, ALU.mult)
    P.tt(eng, o_im, a_re, b_im, ALU.mult)
    P.tt(eng, o_im, o_im, t2, ALU.add)


def sin_of(P, out, x, tA, tB, shape_ap=None):
    P.ts("vector", tA, x, 1.0 / TWO_PI, MAGIC, ALU.mult, ALU.add)
    P.ts("vector", tA, tA, -MAGIC, -TWO_PI, ALU.add, ALU.mult)
    P.tt("vector", tB, tA, x, ALU.add)
    P.act(out, tB, ACT.Sin, scale=SHRINK)


def phase_s(P, nc, D, G, flush=True):
    es = P.push_scope()
    sc = P.sb("s_sc", [128, 3, 32]); Bt = P.sb("s_B", [128, 2, 32, 16]); Ct = P.sb("s_C", [128, 2, 32, 16])
    dcol = P.sb("s_dcol", [128, 32])
    maskF = P.sb("s_maskF", [128, 128]); maskB = P.sb("s_maskB", [128, 128])
    P.dma("sync", sc[:], D["ssm_sc"]); P.dma("sync", Bt[:], D["ssm_B"]); P.dma("sync", Ct[:], D["ssm_C"])
    P.dma("sync", dcol[:], D["ssm_dcol"])
    P.dma("sync", maskF[:], D["maskF"]); P.dma("sync", maskB[:], D["maskB"])
    identf = G["identf"]
    n = 0
    def T32(nm):
        return P.sb("s_" + nm, [128, 32])
    lre = T32("lre"); dt = T32("dt"); a = T32("a"); th = T32("th"); mag = T32("mag")
    sn = T32("sn"); cs = T32("cs"); tA = T32("tA"); tB = T32("tB"); thc = T32("thc")
    lbr = T32("lbr"); lbi = T32("lbi"); nr = T32("nr"); den = T32("den"); gre = T32("gre"); gim = T32("gim")
    ilr = T32("ilr"); ili = T32("ili"); t1 = T32("t1"); t2 = T32("t2")
    V = "vector"
    P.ts(V, lre[:], sc[:, 0, :], -1e-4, None, ALU.min)
    P.act(dt[:], sc[:, 2, :], ACT.Exp)
    P.tt(V, a[:], lre[:], dt[:], ALU.mult)
    P.tt(V, th[:], sc[:, 1, :], dt[:], ALU.mult)
    P.act(mag[:], a[:], ACT.Exp)
    sin_of(P, sn[:], th[:], tA[:], tB[:])
    P.ts(V, thc[:], th[:], float(np.pi / 2), None, ALU.add)
    sin_of(P, cs[:], thc[:], tA[:], tB[:])
    P.tt(V, lbr[:], mag[:], cs[:], ALU.mult)
    P.tt(V, lbi[:], mag[:], sn[:], ALU.mult)
    P.ts(V, nr[:], lbr[:], -1.0, None, ALU.add)
    P.tt(V, den[:], lre[:], lre[:], ALU.mult)
    P.tt(V, t1[:], sc[:, 1, :], sc[:, 1, :], ALU.mult)
    P.tt(V, den[:], den[:], t1[:], ALU.add)
    P.recip(den[:], den[:])
    P.tt(V, gre[:], nr[:], lre[:], ALU.mult)
    P.tt(V, t1[:], lbi[:], sc[:, 1, :], ALU.mult)
    P.tt(V, gre[:], gre[:], t1[:], ALU.add)
    P.tt(V, gre[:], gre[:], den[:], ALU.mult)
    P.tt(V, gim[:], lbi[:], lre[:], ALU.mult)
    P.tt(V, t1[:], nr[:], sc[:, 1, :], ALU.mult)
    P.tt(V, gim[:], gim[:], t1[:], ALU.subtract)
    P.tt(V, gim[:], gim[:], den[:], ALU.mult)
    P.tt(V, t1[:], mag[:], mag[:], ALU.mult)
    P.recip(t1[:], t1[:])
    P.tt(V, ilr[:], lbr[:], t1[:], ALU.mult)
    P.tt(V, ili[:], lbi[:], t1[:], ALU.mult)
    P.ts(V, ili[:], ili[:], -1.0, None, ALU.mult)
    PWr = P.sb("s_PWr", [128, 16, 32]); PWi = P.sb("s_PWi", [128, 16, 32])
    P.memset(V, PWr[:, 7, :], 1.0); P.memset(V, PWi[:, 7, :], 0.0)
    for k in range(0, 8):
        cmul(P, V, PWr[:, 8 + k, :], PWi[:, 8 + k, :], PWr[:, 7 + k, :], PWi[:, 7 + k, :], lbr[:], lbi[:], t1[:], t2[:])
    for k in range(0, 7):
        cmul(P, V, PWr[:, 6 - k, :], PWi[:, 6 - k, :], PWr[:, 7 - k, :], PWi[:, 7 - k, :], ilr[:], ili[:], t1[:], t2[:])
    LN = G["LN"]
    sqa_r = T32("sqa_r"); sqa_i = T32("sqa_i"); sqb_r = T32("sqb_r"); sqb_i = T32("sqb_i")
    cur = (PWr[:, 15, :], PWi[:, 15, :])
    bufs = [(sqa_r[:], sqa_i[:]), (sqb_r[:], sqb_i[:])]
    for i in range(7):
        o = (LN[:, 0, :], LN[:, 1, :]) if i == 6 else bufs[i % 2]
        cmul(P, V, o[0], o[1], cur[0], cur[1], cur[0], cur[1], t1[:], t2[:])
        cur = o
    P.act(G["R"][:], a[:], ACT.Exp, scale=8.0)
    P.ts(V, G["th8"][:], th[:], 8.0, None, ALU.mult)
    P.ts(V, G["a8"][:], a[:], 8.0, None, ALU.mult)
    PBr = P.sb("s_PBr", [128, 32, 8]); PBi = P.sb("s_PBi", [128, 32, 8])
    PCr = P.sb("s_PCr", [128, 32, 8]); PCi = P.sb("s_PCi", [128, 32, 8])
    PGr = P.sb("s_PGr", [128, 32, 8]); PGi = P.sb("s_PGi", [128, 32, 8])
    t8a = P.sb("s_t8a", [128, 32, 8]); t8b = P.sb("s_t8b", [128, 32, 8])
    Qr = P.sb("s_Qr", [128, 32, 8]); Qi = P.sb("s_Qi", [128, 32, 8])
    def gather(dst, src, k0f, stf, k0b, stb):
        P.copy(V, apx(dst[:], 0, [[16, 16], [1, 8]]), apx(src[:], k0f * 32, [[2, 16], [32 * stf, 8]]))
        P.copy(V, apx(dst[:], 8, [[16, 16], [1, 8]]), apx(src[:], k0b * 32 + 1, [[2, 16], [32 * stb, 8]]))
    gather(Qr, PWr, 14, -1, 7, 1); gather(Qi, PWi, 14, -1, 7, 1)
    gb_r = apx(gre[:], 0, [[1, 32], [0, 8]]); gb_i = apx(gim[:], 0, [[1, 32], [0, 8]])
    cmul(P, V, PBr[:], PBi[:], Qr[:], Qi[:], gb_r, gb_i, t8a[:], t8b[:])
    gather(PCr, PWr, 8, 1, 15, -1); gather(PCi, PWi, 8, 1, 15, -1)
    gather(PGr, PWr, 0, 1, 7, -1); gather(PGi, PWi, 0, 1, 7, -1)
    WB = G["WB"]; WC = G["WC"]; Tm = G["T"]
    NBS = 8
    Wr = P.sb("s_Wr", [128, NBS, 128]); Wi = P.sb("s_Wi", [128, NBS, 128])
    Gr = P.sb("s_Gr", [128, NBS, 128]); Gi = P.sb("s_Gi", [128, NBS, 128])
    X1 = P.sb("s_X1", [128, NBS, 128]); X2 = P.sb("s_X2", [128, NBS, 128])
    tf = P.sb("s_tf", [128, 128]); tb = P.sb("s_tb", [128, 128])
    pst = [P.ps("s_ps%d" % i, [128, 512]) for i in range(4)]
    def bc_coef(t, s0):
        return apx(t[:], s0 * 8, [[8, NBS], [1, 8], [0, 16]])
    def bc_mat(t, ri, s0):
        return apx(t[:], ri * 512 + s0 * 16, [[16, NBS], [0, 8], [1, 16]])
    def v4(t):
        return apx(t[:], 0, [[128, NBS], [16, 8], [1, 16]])
    for b in range(32 // NBS):
        s0 = b * NBS
        E = "vector" if b % 2 == 0 else "gpsimd"
        P.tt(E, v4(X1), bc_coef(PBi, s0), bc_mat(Bt, 1, s0), ALU.mult)
        P.tt(E, v4(Wr), bc_coef(PBr, s0), bc_mat(Bt, 0, s0), ALU.mult)
        P.tt(E, v4(Wr), v4(Wr), v4(X1), ALU.subtract)
        P.tt(E, v4(X2), bc_coef(PBi, s0), bc_mat(Bt, 0, s0), ALU.mult)
        P.tt(E, v4(Wi), bc_coef(PBr, s0), bc_mat(Bt, 1, s0), ALU.mult)
        P.tt(E, v4(Wi), v4(Wi), v4(X2), ALU.add)
        P.tt(E, v4(X1), bc_coef(PGi, s0), bc_mat(Ct, 1, s0), ALU.mult)
        P.tt(E, v4(Gr), bc_coef(PGr, s0), bc_mat(Ct, 0, s0), ALU.mult)
        P.tt(E, v4(Gr), v4(Gr), v4(X1), ALU.subtract)
        P.tt(E, v4(X2), bc_coef(PGi, s0), bc_mat(Ct, 0, s0), ALU.mult)
        P.tt(E, v4(Gi), bc_coef(PGr, s0), bc_mat(Ct, 1, s0), ALU.mult)
        P.tt(E, v4(Gi), v4(Gi), v4(X2), ALU.add)
        P.ts(E, Gi[:], Gi[:], -1.0, None, ALU.mult)
        for j in range(NBS):
            sl = s0 + j
            for ri, Wt in enumerate((Wr, Wi)):
                ps = pst[(j * 2 + ri) % 2]
                P.tr(ps[:, 0:128], Wt[:, j, :], identf[:])
                P.copy("scalar", WB[:, sl, ri, :], ps[:, 0:128])
        for jp in range(NBS // 2):
            gp = (s0 // 2) + jp
            jf = 2 * jp; jb = 2 * jp + 1
            for gi in range(2):
                g = 2 * gp + gi
                rows = slice(gi * 64, gi * 64 + 64)
                psf = pst[2]; psb = pst[3]
                P.mm(psf[:, 0:128], Wr[rows, jf, :], Gr[rows, jf, :], start=True, stop=False)
                P.mm(psf[:, 0:128], Wi[rows, jf, :], Gi[rows, jf, :], start=False, stop=True)
                P.mm(psb[:, 0:128], Wr[rows, jb, :], Gr[rows, jb, :], start=True, stop=False)
                P.mm(psb[:, 0:128], Wi[rows, jb, :], Gi[rows, jb, :], start=False, stop=True)
                P.tt(V, tf[:], psf[:, 0:128], maskF[:], ALU.mult)
                P.tt(V, tb[:], psb[:, 0:128], maskB[:], ALU.mult)
                P.tt(V, tf[:], tf[:], tb[:], ALU.add)
                P.stt(Tm[:, g, :], identf[:], dcol[:, g:g + 1], tf[:], ALU.mult, ALU.add)
        P.tt(E, v4(X1), bc_coef(PCi, s0), bc_mat(Ct, 1, s0), ALU.mult)
        P.tt(E, v4(X2), bc_coef(PCr, s0), bc_mat(Ct, 0, s0), ALU.mult)
        P.tt(E, apx(WC[:], s0 * 256, [[256, NBS], [16, 8], [1, 16]]), v4(X2), v4(X1), ALU.subtract)
        P.tt(E, v4(X1), bc_coef(PCi, s0), bc_mat(Ct, 0, s0), ALU.mult)
        P.tt(E, v4(X2), bc_coef(PCr, s0), bc_mat(Ct, 1, s0), ALU.mult)
        P.tt(E, v4(X2), v4(X2), v4(X1), ALU.add)
        P.ts(E, apx(WC[:], s0 * 256 + 128, [[256, NBS], [1, 128]]), X2[:], -1.0, None, ALU.mult)
    if flush:
        P.flush()
        P.pop_scope()


def make_tables(P, G, kind, flush=True):
    V = "vector"
    n = 129 if kind == "E" else 128
    P.push_scope()
    iota = G["iota"]; th8 = G["th8"]; a8 = G["a8"]
    X = P.sb("mt_X", [128, 4, n]); XA = P.sb("mt_XA", [128, 4, n]); XB = P.sb("mt_XB", [128, 4, n])
    Rp = P.sb("mt_Rp", [128, 4, n])
    for hq in range(8):
        sl = slice(hq * 4, (hq + 1) * 4)
        P.tt(V, X[:], apx(th8[:], hq * 4, [[1, 4], [0, n]]), apx(iota[:], 0, [[0, 4], [1, n]]), ALU.mult)
        if kind == "E":
            sin_of(P, G["TS"][:, sl, :], X[:], XA[:], XB[:])
            P.ts(V, X[:], X[:], float(np.pi / 2), None, ALU.add)
            sin_of(P, G["TC"][:, sl, :], X[:], XA[:], XB[:])
        else:
            P.tt(V, Rp[:], apx(a8[:], hq * 4, [[1, 4], [0, n]]), apx(iota[:], 0, [[0, 4], [1, n]]), ALU.mult)
            P.act(Rp[:], Rp[:], ACT.Exp)
            sin_of(P, XA[:], X[:], XA[:], XB[:])
            P.tt(V, G["Qi"][:, sl, 0:128], XA[:], Rp[:], ALU.mult)
            P.ts(V, X[:], X[:], float(np.pi / 2), None, ALU.add)
            sin_of(P, XA[:], X[:], XA[:], XB[:])
            P.tt(V, G["Qr"][:, sl, 0:128], XA[:], Rp[:], ALU.mult)
    if flush:
        P.flush()
        P.pop_scope()


EPS = 1e-6
GC = float(np.sqrt(2 / np.pi))


def host_carry_masks(core):
    M = np.zeros((128, 16, 2, 32), np.float32)
    for j in range(16):
        for k in range(2):
            u = 2 * core + k
            M[:, j, k, 0::2] = 1.0 if j < u else 0.0
            M[:, j, k, 1::2] = 1.0 if j < 15 - u else 0.0
    return M


def ssm_alloc(P, G):
    G["Wssm"] = P.sb("Wssm", [128, 16, 512], BF16)
    G["Rt"] = P.sb("Rt", [128, 32, 128], BF16)
    G["U"] = P.sb("U", [128, 32, 128], BF16)
    G["Floc"] = P.sb("Floc", [128, 16, 32, 2])
    G["acc"] = P.sb("acc", [128, 32, 4])
    G["ss8"] = P.sb("ss8", [128, 8]); G["rstd8"] = P.sb("rstd8", [128, 8])
    G["junk"] = P.sb("junk", [128, 128])
    G["Tm"] = [P.sb("Tmp%d" % i, [128, 4, 128]) for i in range(2)]
    G["Cin"] = P.sb("Cin", [128, 2, 32, 2])
    if "epst" not in G:
        G["epst"] = P.sb("epst", [128, 1])


def load_wssm(P, G, w_in_ap, gpre):
    stg = [P.sb("wstg%d" % i, [128, 512]) for i in range(2)]
    for k in range(16):
        s = stg[k % 2]
        P.dma("sync", s[:], w_in_ap[k * 128:(k + 1) * 128, 3072:3584])
        P.act(G["Wssm"][:, k, :], s[:], ACT.Copy, scale=gpre[:, k:k + 1])


def ssm_load(P, xsrc, tok0, xbf):
    src = xsrc[:, tok0:tok0 + 1024].rearrange("(k p) t -> p k t", p=128)
    for q in range(4):
        P.dma("gpsimd", xbf[:, 4 * q:4 * q + 4, :], src[:, 4 * q:4 * q + 4, :], key=xbf.name)


def ssm_unit(P, G, xsrc, tok0, xbf, PS, mode, uidx, kown=None, LO=None, preloaded=False):
    V = "vector"
    WB, R = G["WB"], G["R"]
    Rt, U = G["Rt"], G["U"]
    identf, identb = G["identf"], G["identb"]
    if not preloaded:
        ssm_load(P, xsrc, tok0, xbf)
    def xs(k, s):
        return apx(xbf[:], k * 1024 + s, [[8, 128]])
    for hs in range(2):
        psG = PS["f"][hs]
        for s4 in range(4):
            s = hs * 4 + s4
            for k in range(16):
                P.mm(psG[:, s4 * 128:(s4 + 1) * 128], xs(k, s), xs(k, s), start=(k == 0), stop=(k == 15))
        for s4 in range(4):
            s = hs * 4 + s4
            P.stt(G["junk"][:], psG[:, s4 * 128:(s4 + 1) * 128], 1.0, identf[:], ALU.mult, ALU.mult,
                  accum_out=G["ss8"][:, s:s + 1])
    P.act(G["rstd8"][:], G["ss8"][:], ACT.Sqrt, scale=1.0 / 2048, bias=G["epst"][:, 0:1])
    P.recip(G["rstd8"][:], G["rstd8"][:])
    for s in range(8):
        ps = PS["f"][2 + (s % 2)]
        for k in range(16):
            P.mm(ps[:, :], xs(k, s), G["Wssm"][:, k, :], start=(k == 0), stop=(k == 15))
        P.act(apx(Rt[:], s * 16, [[128, 32], [1, 16]]), apx(ps[:], 0, [[16, 32], [1, 16]]), ACT.Copy, scale=G["rstd8"][:, s:s + 1])
    for g8 in range(4):
        psT = PS["b"][g8 % 2]
        for gg in range(8):
            g = g8 * 8 + gg
            P.tr(psT[:, gg * 128:(gg + 1) * 128], Rt[:, g, :], identb[:])
        if g8 % 2 == 0:
            P.copy("scalar", U[:, g8 * 8:(g8 + 1) * 8, :], psT[:, :], w=["U%d" % g8], r=[psT.name])
        else:
            P.copy(V, U[:, g8 * 8:(g8 + 1) * 8, :], psT[:, :], w=["U%d" % g8], r=[psT.name])
    def hc(sl):
        gp = sl // 2
        psH = PS["f"][4 + sl % 2]
        ukey = ["U%d" % (gp // 4)]
        for ri in range(2):
            for gi in range(2):
                P.mm(psH[gi * 64:(gi + 1) * 64, ri * 128:(ri + 1) * 128], WB[:, sl, ri, gi * 64:(gi + 1) * 64], U[:, 2 * gp + gi, :],
                     start=True, stop=True, r=[WB.name] + ukey)
        return psH

    if mode == "G":
        for sl in range(32):
            dr = sl % 2
            psH = hc(sl)
            Qr, Qi = G["Qr"], G["Qi"]
            if dr == 0:
                qr = apx(Qr[:, sl, 0:128], 127, [[-1, 128]]); qi = apx(Qi[:, sl, 0:128], 127, [[-1, 128]])
            else:
                qr = Qr[:, sl, 0:128]; qi = Qi[:, sl, 0:128]
            hre = psH[:, 0:128]; him = psH[:, 128:256]
            acc = G["acc"]
            jk = G["Tm"][sl % 2]
            P.stt(jk[:, 0, :], hre, 1.0, qr, ALU.mult, ALU.mult, accum_out=acc[:, sl, 0:1])
            P.stt(jk[:, 1, :], him, 1.0, qi, ALU.mult, ALU.mult, accum_out=acc[:, sl, 1:2])
            P.stt(jk[:, 2, :], hre, 1.0, qi, ALU.mult, ALU.mult, accum_out=acc[:, sl, 2:3])
            P.stt(jk[:, 3, :], him, 1.0, qr, ALU.mult, ALU.mult, accum_out=acc[:, sl, 3:4])
    else:
        TC, TS = G["TC"], G["TS"]

        def stage_a(sl):
            dr = sl % 2
            psH = hc(sl)
            if dr == 0:
                hre = psH[:, 0:128]; him = psH[:, 128:256]
            else:
                hre = apx(psH[:], 127, [[-1, 128]]); him = apx(psH[:], 255, [[-1, 128]])
            c1 = TC[:, sl, 1:129]; s1 = TS[:, sl, 1:129]
            Tm = G["Tm"][sl % 2]; Dt = G["Dt"][sl % 2]
            P.tt(V, Tm[:, 0, :], hre, c1, ALU.mult)
            P.tt(V, Tm[:, 1, :], him, s1, ALU.mult)
            P.tt(V, Tm[:, 2, :], him, c1, ALU.mult)
            P.tt(V, Tm[:, 3, :], hre, s1, ALU.mult)
            P.tt(V, Dt[:, 0, :], Tm[:, 0, :], Tm[:, 1, :], ALU.add)
            P.tt(V, Dt[:, 1, :], Tm[:, 2, :], Tm[:, 3, :], ALU.subtract)

        def stage_b(sl):
            gp = sl // 2; dr = sl % 2
            Dt = G["Dt"][sl % 2]; Wb = G["Wb"][sl % 2]; Tm = LO["Tm2"][sl % 2]
            Rbc = apx(R[:], sl, [[0, 128]])
            for ri in range(2):
                P.scan(Wb[:, ri, 1:129], Rbc, Dt[:, ri, :], G["Cin"][:, kown, sl, ri:ri + 1])
            P.copy(V, Wb[:, :, 0:1], apx(G["Cin"][:], (kown * 32 + sl) * 2, [[1, 2], [1, 1]]))
            c0 = TC[:, sl, 0:128]; s0 = TS[:, sl, 0:128]
            Zp = LO["Zp"][gp % 2]
            wr = Wb[:, 0, 0:128]; wi = Wb[:, 1, 0:128]
            P.tt(V, Tm[:, 0, :], wr, c0, ALU.mult)
            P.tt(V, Tm[:, 1, :], wi, s0, ALU.mult)
            P.tt(V, Tm[:, 2, :], wr, s0, ALU.mult)
            P.tt(V, Tm[:, 3, :], wi, c0, ALU.mult)
            if dr == 0:
                zo_re = Zp[:, dr, 0, :]; zo_im = Zp[:, dr, 1, :]
            else:
                zo_re = apx(Zp[:], (dr * 2 + 0) * 128 + 127, [[-1, 128]])
                zo_im = apx(Zp[:], (dr * 2 + 1) * 128 + 127, [[-1, 128]])
            P.tt(V, zo_re, Tm[:, 0, :], Tm[:, 1, :], ALU.subtract)
            P.tt(V, zo_im, Tm[:, 2, :], Tm[:, 3, :], ALU.add)

        def stage_c(gp):
            Zp = LO["Zp"][gp % 2]
            WC, T = G["WC"], G["T"]
            psY = PS["f"][6 + ((gp // 2) % 2)]
            for gi in range(2):
                g = 2 * gp + gi
                col = ((gp % 2) * 2 + gi) * 128
                rows = slice(gi * 64, gi * 64 + 64)
                P.mm(psY[:, col:col + 128], T[:, g, :], U[:, g, :], start=True, stop=False, r=[T.name, "U%d" % (gp // 4)])
                n = 0
                for dr in range(2):
                    for ri in range(2):
                        n += 1
                        P.mm(psY[:, col:col + 128], WC[rows, gp * 2 + dr, ri, :], Zp[rows, dr, ri, :],
                             start=False, stop=(n == 4))
            if gp % 2 == 1:
                g0 = 2 * gp - 2
                Ysb = LO["Ysb"][(gp // 2) % 2]
                P.copy("scalar", Ysb[:], psY[:, :])
                psR = PS["f"][(gp // 2) % 2]
                for gg in range(4):
                    P.tr(psR[:, gg * 128:(gg + 1) * 128], Ysb[:, gg, :], identf[:])
                P.copy("scalar", apx(LO["Rout"][:], g0 * 16, [[16, 4], [512, 8], [1, 16]]),
                       apx(psR[:], 0, [[128, 4], [16, 8], [1, 16]]))

        stage_a(0)
        for sl in range(32):
            if sl + 1 < 32:
                stage_a(sl + 1)
            stage_b(sl)
            if sl % 2 == 1:
                stage_c(sl // 2)
    if mode == "G":
        acc = G["acc"]; Fl = G["Floc"]
        for par, ust in ((0, uidx), (1, 15 - uidx)):
            def a(c):
                return apx(acc[:], par * 4 + c, [[8, 16]])
            P.tt(V, apx(Fl[:], (ust * 32 + par) * 2 + 0, [[4, 16]]), a(0), a(1), ALU.subtract)
            P.tt(V, apx(Fl[:], (ust * 32 + par) * 2 + 1, [[4, 16]]), a(2), a(3), ALU.add)
    if mode == "L":
        for t in range(8):
            psR = PS["f"][2 + (t % 2)]
            for ch in range(4):
                P.tr(psR[:, ch * 128:(ch + 1) * 128], LO["Rout"][:, t, ch * 128:(ch + 1) * 128], identf[:])
            P.copy("scalar" if t % 2 == 0 else V, apx(LO["ysT"][:], t, [[1024, 4], [8, 128]]),
                   apx(psR[:], 0, [[128, 4], [1, 128]]))


def carry_compute(P, G, Mt):
    V = "vector"
    c_re = P.sb("cc_re", [128, 2, 32]); c_im = P.sb("cc_im", [128, 2, 32])
    n_re = P.sb("cn_re", [128, 2, 32]); n_im = P.sb("cn_im", [128, 2, 32])
    t1 = P.sb("cc_t1", [128, 2, 32]); t2 = P.sb("cc_t2", [128, 2, 32])
    LN = G["LN"]; Fl = G["Floc"]
    a_re = apx(LN[:], 0, [[0, 2], [1, 32]]); a_im = apx(LN[:], 32, [[0, 2], [1, 32]])
    P.memset(V, c_re[:], 0.0); P.memset(V, c_im[:], 0.0)
    for j in range(16):
        f_re = apx(Fl[:], j * 64 + 0, [[0, 2], [2, 32]]); f_im = apx(Fl[:], j * 64 + 1, [[0, 2], [2, 32]])
        m = Mt[:, j, :, :]
        P.tt(V, t1[:], a_im, c_im[:], ALU.mult)
        P.tt(V, n_re[:], a_re, c_re[:], ALU.mult)
        P.tt(V, n_re[:], n_re[:], t1[:], ALU.subtract)
        P.tt(V, t2[:], a_im, c_re[:], ALU.mult)
        P.tt(V, n_im[:], a_re, c_im[:], ALU.mult)
        P.tt(V, n_im[:], n_im[:], t2[:], ALU.add)
        P.tt(V, n_re[:], n_re[:], f_re, ALU.add)
        P.tt(V, n_im[:], n_im[:], f_im, ALU.add)
        P.tt(V, n_re[:], n_re[:], c_re[:], ALU.subtract)
        P.tt(V, n_im[:], n_im[:], c_im[:], ALU.subtract)
        P.tt(V, n_re[:], n_re[:], m, ALU.mult)
        P.tt(V, n_im[:], n_im[:], m, ALU.mult)
        P.tt(V, c_re[:], c_re[:], n_re[:], ALU.add)
        P.tt(V, c_im[:], c_im[:], n_im[:], ALU.add)
    Cin = G["Cin"]
    P.copy(V, apx(Cin[:], 0, [[64, 2], [2, 32]]), c_re[:])
    P.copy(V, apx(Cin[:], 1, [[64, 2], [2, 32]]), c_im[:])


def ssm_finish(P, G, LO, PS, kown):
    V = "vector"; PL = "gpsimd"
    ysT = LO["ysT"]
    for q in range(4):
        c0 = q * 256
        y = apx(ysT[:], c0, [[1024, 4], [1, 256]])
        a = LO["ga"]; b = LO["gb"]; gl = LO["gl"]; gbf = LO["gbf"]; zt = LO["zt"]; sq = LO["sq"]
        P.tt(PL, a[:], y, y, ALU.mult)
        P.ts(PL, a[:], a[:], 0.044715 * GC, GC, ALU.mult, ALU.add)
        P.tt(PL, a[:], a[:], y, ALU.mult)
        P.act(b[:], a[:], ACT.Tanh)
        P.stt(gl[:], b[:], 1.0, y, ALU.add, ALU.mult)
        P.act(gbf[:], gl[:], ACT.Copy, scale=0.5)
        for co in range(4):
            ps = PS["f"][4 + (co % 2)]
            for ci in range(4):
                P.mm(ps[:, 0:256], LO["Wglu"][:, ci, co * 128:(co + 1) * 128], gbf[:, ci, :], start=(ci == 0), stop=(ci == 3))
            P.act(zt[:, co, :], ps[:, 0:256], ACT.Sigmoid, bias=G["bglu"][:, co:co + 1])
        P.stt(gl[:], gl[:], 0.5, zt[:], ALU.mult, ALU.mult)
        P.tt(PL, sq[:], gl[:], gl[:], ALU.mult)
        pss = PS["f"][6]
        for ci in range(4):
            P.mm(pss[:, 0:256], G["onesb"][:], sq[:, ci, :], start=(ci == 0), stop=(ci == 3))
        rs = LO["rs"]
        P.act(rs[:], pss[:, 0:256], ACT.Ln, scale=1.0 / 512, bias=G["epst"][:, 0:1])
        P.act(rs[:], rs[:], ACT.Exp, scale=-0.5)
        for ci in range(4):
            P.stt(G["yssm_n"][:, ci, kown * 1024 + c0: kown * 1024 + c0 + 256], gl[:, ci, :], G["g_out"][:, 8 + ci:9 + ci],
                  rs[:], ALU.mult, ALU.mult)


EPS = 1e-6
NEG = -30000.0
QSCALE = float(128 ** -0.5)
NA_CLS = {0: (0, 0, 6), 1: (1, 2, 5), 14: (3, 28, 5), 15: (4, 28, 6)}


def na_class(j):
    if j in NA_CLS:
        return NA_CLS[j]
    return (2, 2 * j, 5)


def host_na_tables(rpb, core):
    GW, WH, WW = 64, 8, 16
    rows = 256
    tab = np.full((5, 8, 128, 768), NEG, np.float32)
    rep = {0: 0, 1: 1, 2: 6, 3: 14, 4: 15}
    for cls, j in rep.items():
        _, b, nch = na_class(j)
        q_halo_tok = (4 + 2 * j) * 64 + np.arange(128)
        q_glob = core * 2048 - 256 + q_halo_tok
        r = q_glob // GW; c = q_glob % GW
        rs = np.clip(r - WH // 2, 0, rows - WH); cs = np.clip(c - WW // 2, 0, GW - WW)
        for dr_ in range(WH):
            for dc_ in range(WW):
                kr = rs + dr_; kc = cs + dc_
                k_glob = kr * GW + kc
                k_halo = k_glob - (core * 2048 - 256)
                rel = k_halo - b * 64
                ok = (rel >= 0) & (rel < nch * 128)
                assert ok.all(), (cls, core)
                ch = rel // 128; kk = rel % 128
                oi = kr - r + WH - 1; oj = kc - c + WW - 1
                qi = np.arange(128)
                for h in range(8):
                    tab[cls, h, kk, ch * 128 + qi] = rpb[h, oi, oj]
    return tab


WCONV = {"w_in": [(0, 0), (0, 512), (0, 1024), (0, 1536), (0, 2048), (0, 2560), (0, 3584)],
         "w_out": [(0, c * 512) for c in range(4)],
         "w_ff1": [(0, c * 512) for c in range(16)]}
W_FF2 = [(jp * 2048, cg * 512) for cg in range(4) for jp in range(4)]


def conv_tasks_ff2(P, D):
    tasks = []
    for i, (r0, c0) in enumerate(W_FF2):
        def t(i=i, r0=r0, c0=c0):
            v = D["w_ff2"][r0:r0 + 2048, c0:c0 + 512].rearrange("(k p) c -> p k c", p=128)
            P.dma("gpsimd", D["wc_w_ff2"][i, :, 0:8, :], v[:, 0:8, :], key="wconv", w=["wc_w_ff2" + str(i)])
            P.dma("gpsimd", D["wc_w_ff2"][i, :, 8:16, :], v[:, 8:16, :], key="wconv", w=["wc_w_ff2" + str(i)])
        tasks.append(t)
    return tasks


def conv_tasks(P, D, G=None, S=None):
    tasks = []
    for name, lst in WCONV.items():
        gain = None
        for i, (r0, c0) in enumerate(lst):
            if gain is None:
                def t(name=name, i=i, r0=r0, c0=c0):
                    v = D[name][r0:r0 + 2048, c0:c0 + 512].rearrange("(k p) c -> p k c", p=128)
                    P.dma("gpsimd", D["wc_" + name][i, :, 0:8, :], v[:, 0:8, :], key="wconv", w=["wc_" + name + str(i)])
                    P.dma("gpsimd", D["wc_" + name][i, :, 8:16, :], v[:, 8:16, :], key="wconv", w=["wc_" + name + str(i)])
                tasks.append(t)
            else:
                for kq in range(4):
                    def t(name=name, i=i, r0=r0, c0=c0, kq=kq, gain=gain):
                        for k in range(kq * 4, kq * 4 + 4):
                            n = S["n"]; S["n"] += 1
                            st = S["cst"][n % 2]; sb = S["cbf"][n % 2]
                            P.dma("sync", st[:], D[name][r0 + k * 128:r0 + (k + 1) * 128, c0:c0 + 512])
                            P.act(sb[:], st[:], ACT.Copy, scale=G[gain][:, k:k + 1])
                            P.dma("sync", D["wc_" + name][i, :, k, :], sb[:], key="wconv2", w=["wc_" + name + str(i)])
                    tasks.append(t)
    return tasks


class WStream:
    def __init__(self, P, D, n=3):
        self.P = P; self.D = D
        self.bufs = [P.sb("wbuf%d" % i, [128, 16, 512], BF16) for i in range(n)]
        self.i = 0

    def preload(self, name, idx):
        b = self._load(name, idx)
        self.pre = getattr(self, "pre", [])
        self.pre.append((name, idx, b))

    def next(self, name, idx):
        pre = getattr(self, "pre", [])
        if pre:
            n2, i2, b = pre.pop(0)
            assert (n2, i2) == (name, idx), (n2, i2, name, idx)
            return b
        return self._load(name, idx)

    def _load(self, name, idx):
        b = self.bufs[self.i % len(self.bufs)]
        self.i += 1
        src = self.D["wc_" + name]
        self.P.dma("gpsimd", b[:, 0:8, :], src[idx, :, 0:8, :], key=b.name, r=["wc_" + name + str(idx)])
        self.P.dma("gpsimd", b[:, 8:16, :], src[idx, :, 8:16, :], key=b.name, r=["wc_" + name + str(idx)])
        return b


def xprep_gen(P, G, S, src, col0, gcol, xg, rstd_bc, ps_ss, want_col=None, ps_tr=None, stage=None):
    if stage is not None:
        v = src[:, col0:col0 + 512].rearrange("(k p) t -> p k t", p=128)
        P.dma("sync", stage[:, 0:8, :], v[:, 0:8, :], key=stage.name + "_ld", w=[stage.name])
        P.dma("sync", stage[:, 8:16, :], v[:, 8:16, :], key=stage.name + "_ld", w=[stage.name])
        acc = S["sacc"]; sq = S["ssq"]
        for k in range(16):
            P.act(xg[:, k, :], stage[:, k, :], ACT.Copy, scale=gcol[:, k:k + 1], w=[xg.name + str(k)])
            if k == 0:
                P.tt("gpsimd", acc[:], stage[:, k, :], stage[:, k, :], ALU.mult)
            else:
                P.tt("gpsimd", sq[k % 2][:], stage[:, k, :], stage[:, k, :], ALU.mult)
                P.tt("gpsimd", acc[:], acc[:], sq[k % 2][:], ALU.add)
        yield
        P.mm(ps_ss[:, :], G["onesf"][:], acc[:], start=True, stop=True)
    else:
        for k in range(16):
            st = S["xst"][k % len(S["xst"])]; sq = S["xsq"][k % 2]
            P.dma("sync", st[:], src[k * 128:(k + 1) * 128, col0:col0 + 512])
            P.act(xg[:, k, :], st[:], ACT.Copy, scale=gcol[:, k:k + 1], w=[xg.name + str(k)])
            P.tt("vector", sq[:], st[:], st[:], ALU.mult)
            P.mm(ps_ss[:, :], G["onesb"][:], sq[:], start=(k == 0), stop=(k == 15))
            if k % 4 == 3:
                yield
    P.act(rstd_bc[:], ps_ss[:, :], ACT.Ln, scale=1.0 / 2048, bias=G["epst"][:, 0:1])
    P.act(rstd_bc[:], rstd_bc[:], ACT.Exp, scale=-0.5)
    if want_col is not None:
        for j in range(4):
            P.tr(ps_tr[:, j * 128:(j + 1) * 128], rstd_bc[:, j * 128:(j + 1) * 128], G["identf"][:])
        P.copy("vector", want_col[:], apx(ps_tr[:], 0, [[128, 4]]))
    yield


def xprep(*a, **k):
    for _ in xprep_gen(*a, **k):
        pass


def mem_kv(P, G, D, flush=True):
    P.push_scope()
    Wkv = P.sb("m_wkv", [128, 16, 1024], BF16)
    for q in range(4):
        P.dma("gpsimd", Wkv[:, 4 * q:4 * q + 4, :], D["w_mem_kv"].rearrange("(k p) c -> p k c", p=128)[:, 4 * q:4 * q + 4, :], key="m_wkv")
    mg = P.sb("m_mg", [128, 16, 256], BF16)
    st = [P.sb("m_st%d" % i, [128, 256]) for i in range(2)]
    sq = [P.sb("m_sq%d" % i, [128, 256], BF16) for i in range(2)]
    rs = P.sb("m_rs", [128, 256]); rc = P.sb("m_rc", [128, 2])
    ps = [P.ps("m_ps%d" % i, [128, 512]) for i in range(3)]
    for k in range(16):
        P.dma("sync", st[k % 2][:], D["memT"][k * 128:(k + 1) * 128, :])
        P.act(mg[:, k, :], st[k % 2][:], ACT.Copy, scale=G["g_mem"][:, k:k + 1])
        P.tt("vector", sq[k % 2][:], st[k % 2][:], st[k % 2][:], ALU.mult)
        P.mm(ps[0][:, 0:256], G["onesb"][:], sq[k % 2][:], start=(k == 0), stop=(k == 15))
    P.act(rs[:], ps[0][:, 0:256], ACT.Ln, scale=1.0 / 2048, bias=G["epst"][:, 0:1])
    P.act(rs[:], rs[:], ACT.Exp, scale=-0.5)
    for j in range(2):
        P.tr(ps[1][:, j * 128:(j + 1) * 128], rs[:, j * 128:(j + 1) * 128], G["identf"][:])
    P.copy("vector", rc[:], apx(ps[1][:], 0, [[128, 2]]))
    for h in range(4):
        pp = ps[h % 2 + 1] if False else ps[2]
        for k in range(16):
            P.mm(pp[:, 0:256], Wkv[:, k, h * 128:(h + 1) * 128], mg[:, k, :], start=(k == 0), stop=(k == 15))
        P.tt("vector", G["kmemT"][:, h, :], pp[:, 0:256], rs[:], ALU.mult)
    for c in range(2):
        pp = ps[c]
        for k in range(16):
            P.mm(pp[:, :], mg[:, k, c * 128:(c + 1) * 128], Wkv[:, k, 512:1024], start=(k == 0), stop=(k == 15))
        P.act(G["Vmem"][:, c, :], pp[:, :], ACT.Copy, scale=rc[:, c:c + 1])
    if flush:
        P.flush()
        P.pop_scope()


def sweep1(P, G, D, nq=4, dbg=None):
    V = "vector"
    P.push_scope()
    S = {}
    S["sacc"] = P.sb("sacc", [128, 512]); S["ssq"] = [P.sb("ssq%d" % i, [128, 512]) for i in range(2)]
    S["xst"] = S["ssq"]
    S["xsq"] = [P.sb("xsq%d" % i, [128, 512], BF16) for i in range(2)]
    xg = P.sb("xg", [128, 16, 512], BF16)
    rstd = P.sb("rstd_bc", [128, 512]); rcol = P.sb("rstd_col", [128, 4])
    kT = [P.sb("kT%d" % i, [128, 8, 512], BF16) for i in range(3)]
    Vr = [P.sb("Vr%d" % i, [128, 4, 1024], BF16) for i in range(3)]
    qT = P.sb("qT", [128, 8, 512], BF16); qmT = P.sb("qmT", [128, 4, 512], BF16)
    ot = P.sb("ot", [128, 16, 512])
    yna = ot[:, 0:8, :]; ymem = ot[:, 8:12, :]
    ymix_na = P.sb("ymix_na", [128, 8, 512], BF16); ymix_mem = P.sb("ymix_mem", [128, 4, 512], BF16)
    tabt = [P.sb("tab%d" % i, [128, 768]) for i in range(2)]
    Ssb = [P.sb("Ssb%d" % i, [128, 768]) for i in range(2)]
    PT = [P.sb("PT%d" % i, [128, 768], BF16) for i in range(2)]
    rsum = P.sb("rsum", [128, 512]); sqs = [P.sb("sqs%d" % i, [128, 512], BF16) for i in range(2)]
    Pm = P.sb("Pm", [128, 2, 512], BF16)
    rs2 = P.sb("rs2", [128, 512])
    ps = [P.ps("s1ps%d" % i, [128, 512]) for i in range(8)]
    W = WStream(P, D, 2)
    xsrc = D["xT_own"]
    cnt = {"p": 0}
    def pbank():
        cnt["p"] += 1
        return ps[1 + (cnt["p"] % 2)]

    def kv_tile_gen(m, fast=False):
        slot = m % 3
        for _ in xprep_gen(P, G, S, xsrc, 512 * m, G["gpre"], xg, rstd, ps[0], want_col=rcol, ps_tr=ps[1],
                           stage=(ot if fast else None)):
            yield
        for half in range(2):
            wb = W.next("w_in", 2 + half)
            for hh in range(4):
                pb = pbank()
                for k in range(16):
                    P.mm(pb[:, :], wb[:, k, hh * 128:(hh + 1) * 128], xg[:, k, :], start=(k == 0), stop=(k == 15),
                         r=[wb.name, xg.name + str(k)])
                    if k == 7:
                        yield
                P.tt(V, kT[slot][:, half * 4 + hh, :], pb[:, :], rstd[:], ALU.mult)
                yield
        for half in range(2):
            wb = W.next("w_in", 4 + half)
            for j in range(4):
                pb = pbank()
                for k in range(16):
                    P.mm(pb[:, :], xg[:, k, j * 128:(j + 1) * 128], wb[:, k, :], start=(k == 0), stop=(k == 15),
                         r=[wb.name, xg.name + str(k)])
                    if k == 7:
                        yield
                P.act(Vr[slot][:, j, half * 512:(half + 1) * 512], pb[:, :], ACT.Copy, scale=rcol[:, j:j + 1])
                yield

    def kv_tile(m):
        for _ in kv_tile_gen(m, fast=True):
            pass

    def q_prep_gen(i, fast):
        for _ in xprep_gen(P, G, S, xsrc, 512 * i + 256, G["gpre"], xg, rstd, ps[0] if fast else ps[3],
                           stage=(ot if fast else None)):
            yield

    def q_tile(i):
        for half in range(2):
            wb = W.next("w_in", half)
            for hh in range(4):
                pb = pbank()
                for k in range(16):
                    P.mm(pb[:, :], wb[:, k, hh * 128:(hh + 1) * 128], xg[:, k, :], start=(k == 0), stop=(k == 15),
                         r=[wb.name, xg.name + str(k)])
                P.tt(V, qT[:, half * 4 + hh, :], pb[:, :], rstd[:], ALU.mult)
        wb = W.next("w_in", 6)
        for hh in range(4):
            pb = pbank()
            for k in range(16):
                P.mm(pb[:, :], wb[:, k, hh * 128:(hh + 1) * 128], xg[:, k, :], start=(k == 0), stop=(k == 15),
                     r=[wb.name, xg.name + str(k)])
            P.tt(V, qmT[:, hh, :], pb[:, :], rstd[:], ALU.mult)

    def na(i, filler=None, ff2t=()):
        steps = [(jj, h) for jj in range(4) for h in range(8)]
        SA = [(ps[3], ps[4]), (ps[5], ps[6])]

        def scores(n):
            jj, h = steps[n]
            j = 4 * i + jj
            cls, b, nch = na_class(j)
            tb = tabt[n % 2]
            pa, pb_ = SA[n % 2]
            P.dma("sync", tb[:, 0:nch * 128], D["na_tab"][cls, h, :, 0:nch * 128])
            for c in range(nch):
                idx = b // 2 + c
                kt = kT[(idx // 4) % 3]
                pb = pa if c < 4 else pb_
                cc = c % 4
                P.mm(pb[:, cc * 128:(cc + 1) * 128], kt[:, h, (idx % 4) * 128:(idx % 4 + 1) * 128],
                     qT[:, h, jj * 128:(jj + 1) * 128], start=True, stop=True)

        def soft(n):
            jj, h = steps[n]
            j = 4 * i + jj
            cls, b, nch = na_class(j)
            tb = tabt[n % 2]; Sb = Ssb[n % 2]; Pt = PT[n % 2]
            pa, pb_ = SA[n % 2]
            P.stt(Sb[:, 0:512], pa[:, 0:512], QSCALE, tb[:, 0:512], ALU.mult, ALU.add)
            w2 = (nch - 4) * 128
            P.stt(Sb[:, 512:512 + w2], pb_[:, 0:w2], QSCALE, tb[:, 512:512 + w2], ALU.mult, ALU.add)
            P.act(Pt[:, 0:nch * 128], Sb[:, 0:nch * 128], ACT.Exp)

        def rest(n):
            jj, h = steps[n]
            j = 4 * i + jj
            cls, b, nch = na_class(j)
            Pt = PT[n % 2]
            for c in range(nch):
                idx = b // 2 + c
                vt = Vr[(idx // 4) % 3]
                P.mm(ps[7][:, 0:128], vt[:, idx % 4, h * 128:(h + 1) * 128], Pt[:, c * 128:(c + 1) * 128],
                     start=(c == 0), stop=(c == nch - 1))
            for c in range(nch):
                P.mm(ps[7][:, 128:256], G["onesb"][:], Pt[:, c * 128:(c + 1) * 128], start=(c == 0), stop=(c == nch - 1))
            P.act(rsum[:, 0:128], ps[7][:, 128:256], ACT.Ln)
            P.act(rsum[:, 0:128], rsum[:, 0:128], ACT.Exp, scale=-1.0)
            P.tt(V, yna[:, h, jj * 128:(jj + 1) * 128], ps[7][:, 0:128], rsum[:, 0:128], ALU.mult)

        scores(0)
        soft(0)
        for n in range(32):
            if n + 1 < 32:
                scores(n + 1)
                soft(n + 1)
            if filler is not None:
                next(filler, None)
            rest(n)
            if filler is not None and n % 4 == 3:
                next(filler, None)
            if n % 8 == 4 and ff2t:
                ff2t.pop(0)()
        if filler is not None:
            for _ in filler:
                pass

    def memattn(i):
        for h in range(4):
            for c in range(2):
                P.mm(ps[3 + c][:, :], G["kmemT"][:, h, c * 128:(c + 1) * 128], qmT[:, h, :], start=True, stop=True)
                P.act(Pm[:, c, :], ps[3 + c][:, :], ACT.Exp, scale=QSCALE)
            pb = pbank()
            for c in range(2):
                P.mm(pb[:, :], G["Vmem"][:, c, h * 128:(h + 1) * 128], Pm[:, c, :], start=(c == 0), stop=(c == 1))
            pb2 = pbank()
            for c in range(2):
                P.mm(pb2[:, :], G["onesb"][:], Pm[:, c, :], start=(c == 0), stop=(c == 1))
            P.act(rsum[:], pb2[:, :], ACT.Ln)
            P.act(rsum[:], rsum[:], ACT.Exp, scale=-1.0)
            P.tt(V, ymem[:, h, :], pb[:, :], rsum[:], ALU.mult)

    def groupnorm(y, nch, gcol0, out, D_):
        for c in range(nch):
            P.tt(V, sqs[c % 2][:], y[:, c, :], y[:, c, :], ALU.mult)
            P.mm(ps[0][:, :], G["onesb"][:], sqs[c % 2][:], start=(c == 0), stop=(c == nch - 1))
        P.act(rs2[:], ps[0][:, :], ACT.Ln, scale=1.0 / D_, bias=G["epst"][:, 0:1])
        P.act(rs2[:], rs2[:], ACT.Exp, scale=-0.5)
        for c in range(nch):
            P.stt(out[:, c, :], y[:, c, :], G["g_out"][:, gcol0 + c:gcol0 + c + 1], rs2[:], ALU.mult, ALU.mult)

    def wout(i, filler=None):
        pend = []
        for bq in range(4):
            if filler is not None:
                next(filler, None); next(filler, None)
            wb = W.next("w_out", bq)
            for dc in range(4):
                pb = pbank()
                for m in range(16):
                    if m < 8:
                        rhs = ymix_na[:, m, :]
                    elif m < 12:
                        rhs = G["yssm_n"][:, m - 8, 512 * i:512 * (i + 1)]
                    else:
                        rhs = ymix_mem[:, m - 12, :]
                    P.mm(pb[:, :], wb[:, m, dc * 128:(dc + 1) * 128], rhs, start=(m == 0), stop=(m == 15))
                d = bq * 4 + dc
                while pend:
                    pend.pop(0)()
                P.copy("scalar", ot[:, d, :], pb[:, :])
                P.tt(V, sqs[d % 2][:], ot[:, d, :], ot[:, d, :], ALU.mult)
                pend.append(lambda d=d: P.mm(ps[0][:, :], G["onesb"][:], sqs[d % 2][:], start=(d == 0), stop=(d == 15)))
        while pend:
            pend.pop(0)()
        P.act(rs2[:], ps[0][:, :], ACT.Ln, scale=1.0 / 2048, bias=G["epst"][:, 0:1])
        P.act(rs2[:], rs2[:], ACT.Exp, scale=-0.5)
        if filler is not None:
            for _ in filler:
                pass
        if i + 1 < nq:
            W.preload("w_in", 0); W.preload("w_in", 1)
        for d in range(16):
            P.stt(ot[:, d, :], ot[:, d, :], G["g_post"][:, d:d + 1], rs2[:], ALU.mult, ALU.mult)
        x1v = D["x1T"][:, 512 * i:512 * (i + 1)].rearrange("(k p) t -> p k t", p=128)
        P.op("gpsimd", lambda e, x1v=x1v: e.dma_start(out=x1v, in_=ot[:], accum_op=ALU.add),
             r=[ot.name, "x1T_%d" % i], w=["x1T_%d" % i], dma="x1acc")

    kv_tile(0)
    kv_tile(1)
    ff2t = conv_tasks_ff2(P, D)
    for i in range(nq):
        P.dma("sync", D["x1T"][:, 512 * i:512 * (i + 1)], xsrc[:, 512 * i + 256:512 * i + 768], key="x1cp", w=["x1T_%d" % i])
        if i == 0:
            for _ in q_prep_gen(0, True):
                pass
        q_tile(i)
        na(i, kv_tile_gen(i + 2) if i + 2 <= 4 else None, ff2t)
        memattn(i)
        groupnorm(yna, 8, 0, ymix_na, 1024)
        groupnorm(ymem, 4, 12, ymix_mem, 512)
        if dbg is not None and i == 0:
            P.dma("sync", dbg["yna"], yna, key="dbg_yna"); P.dma("sync", dbg["ymem"], ymem, key="dbg_ymem")
        wout(i, q_prep_gen(i + 1, False) if i + 1 < nq else None)
    P.flush()
    P.pop_scope()


def sweep2(P, G, D, nq=4):
    V = "vector"
    P.push_scope()
    S = {}
    S["sacc"] = P.sb("fsacc", [128, 512]); S["ssq"] = [P.sb("fssq%d" % i, [128, 512]) for i in range(2)]
    h2s = [P.sb("h2_%d" % i, [128, 16, 512], BF16) for i in range(2)]
    S["xst"] = S["ssq"]
    S["xsq"] = [P.sb("fxsq%d" % i, [128, 512], BF16) for i in range(2)]
    hid = P.sb("hid", [128, 64, 512], BF16)
    ft = P.sb("ft", [128, 16, 512])
    rstds = [P.sb("f_rstd%d" % i, [128, 512]) for i in range(2)]; rs2 = P.sb("f_rs2", [128, 512]); r4 = P.sb("f_r4", [128, 512])
    rl = [P.sb("f_rl%d" % i, [128, 512]) for i in range(2)]
    sqs = [P.sb("f_sqs%d" % i, [128, 512], BF16) for i in range(4)]
    ps = [P.ps("s2ps%d" % i, [128, 512]) for i in range(8)]
    W = WStream(P, D, 2)
    n = 0
    for i in range(nq):
        P.dma("sync", D["outT"][:, 512 * i:512 * (i + 1)], D["x1T"][:, 512 * i:512 * (i + 1)], key="ocp", w=["outT_%d" % i], r=[])
        h2 = h2s[i % 2]; rstd = rstds[i % 2]
        if i == 0:
            xprep(P, G, S, D["x1T"], 0, G["g_pre2"], h2, rstd, ps[0], stage=ft)
        filler = None
        if i + 1 < nq:
            filler = xprep_gen(P, G, S, D["x1T"], 512 * (i + 1), G["g_pre2"], h2s[(i + 1) % 2], rstds[(i + 1) % 2], ps[0])
        for bq in range(16):
            wb = W.next("w_ff1", bq)
            for c4 in range(4):
                pb = ps[1 + (n % 2)]; r = rl[n % 2]; n += 1
                for k in range(16):
                    P.mm(pb[:, :], wb[:, k, c4 * 128:(c4 + 1) * 128], h2[:, k, :], start=(k == 0), stop=(k == 15),
                         r=[wb.name, h2.name + str(k)])
                P.act(r[:], pb[:, :], ACT.Relu)
                P.tt(V, hid[:, bq * 4 + c4, :], r[:], r[:], ALU.mult)
        pend2 = []
        for cg in range(4):
            for jp in range(4):
                wb = W.next("w_ff2", cg * 4 + jp)
                if filler is not None:
                    next(filler, None)
                if jp == 1:
                    while pend2:
                        pend2.pop(0)()
                for jj in range(16):
                    for dc in range(4):
                        P.mm(ps[4 + dc][:, :], wb[:, jj, dc * 128:(dc + 1) * 128], hid[:, jp * 16 + jj, :],
                             start=(jp == 0 and jj == 0), stop=(jp == 3 and jj == 15))
            for dc in range(4):
                d = cg * 4 + dc
                P.copy("scalar", ft[:, d, :], ps[4 + dc][:, :])
                P.tt(V, sqs[d % 4][:], ft[:, d, :], ft[:, d, :], ALU.mult)
                pend2.append(lambda d=d: P.mm(ps[3][:, :], G["onesb"][:], sqs[d % 4][:], start=(d == 0), stop=(d == 15)))
        while pend2:
            pend2.pop(0)()
        if i + 1 < nq:
            W.preload("w_ff1", 0); W.preload("w_ff1", 1)
        P.tt(V, r4[:], rstd[:], rstd[:], ALU.mult)
        P.tt(V, rs2[:], r4[:], r4[:], ALU.mult)
        P.tt(V, rs2[:], rs2[:], ps[3][:, :], ALU.mult)
        P.act(rs2[:], rs2[:], ACT.Ln, scale=1.0 / 2048, bias=G["epst"][:, 0:1])
        P.act(rs2[:], rs2[:], ACT.Exp, scale=-0.5)
        P.tt(V, rs2[:], rs2[:], r4[:], ALU.mult)
        for d in range(16):
            P.stt(ft[:, d, :], ft[:, d, :], G["g_post2"][:, d:d + 1], rs2[:], ALU.mult, ALU.mult)
        ov = D["outT"][:, 512 * i:512 * (i + 1)].rearrange("(k p) t -> p k t", p=128)
        P.op("gpsimd", lambda e, ov=ov: e.dma_start(out=ov, in_=ft[:], accum_op=ALU.add),
             r=[ft.name, "outT_%d" % i], w=["outT_%d" % i], dma="oacc")
    P.flush()
    P.pop_scope()


def colvec(v):
    return np.ascontiguousarray(np.asarray(v, np.float32).reshape(-1, 128).T)


def build_program(shapes, debug=False, stages="SGLM12"):
    nc = bass.Bass("TRN2", target_bir_lowering=False)
    D = {}
    for name, shp in shapes.items():
        D[name] = nc.dram_tensor(name, list(shp), F32, kind="ExternalInput").ap()
    D["x1T"] = nc.dram_tensor("x1T", [2048, 2048], F32, kind="Internal").ap()
    for nm, lst in WCONV.items():
        D["wc_" + nm] = nc.dram_tensor("wc_" + nm, [len(lst), 128, 16, 512], BF16, kind="Internal").ap()
    D["wc_w_ff2"] = nc.dram_tensor("wc_w_ff2", [16, 128, 16, 512], BF16, kind="Internal").ap()
    D["outT"] = nc.dram_tensor("outT", [2048, 2048], F32, kind="ExternalOutput").ap()
    dbg = None
    if debug:
        dbg = {"yna": nc.dram_tensor("dbg_yna", [128, 8, 512], F32, kind="ExternalOutput").ap(),
               "ymem": nc.dram_tensor("dbg_ymem", [128, 4, 512], F32, kind="ExternalOutput").ap(),
               "yssm": nc.dram_tensor("dbg_yssm", [128, 4, 2048], BF16, kind="ExternalOutput").ap(),
               "x1T": nc.dram_tensor("dbg_x1T", [2048, 2048], F32, kind="ExternalOutput").ap()}
    P = Prog(nc)
    G = {}
    G["identf"] = P.sb("identf", [128, 128]); G["identb"] = P.sb("identb", [128, 128], BF16)
    G["onesb"] = P.sb("onesb", [128, 128], BF16); G["onesf"] = P.sb("onesf", [128, 128])
    for nm in ["gpre", "g_out", "g_post", "g_pre2", "g_post2", "g_mem"]:
        G[nm] = P.sb("t_" + nm, [128, 16])
        P.dma("sync", G[nm][:], D[nm])
    G["bglu"] = P.sb("t_bglu", [128, 4]); P.dma("sync", G["bglu"][:], D["bglu"])
    G["epst"] = P.sb("epst", [128, 1])
    G["yssm_n"] = P.sb("yssm_n", [128, 4, 2048], BF16)
    G["kmemT"] = P.sb("kmemT", [128, 4, 256], BF16); G["Vmem"] = P.sb("Vmem", [128, 2, 512], BF16)
    P.dma("sync", G["identf"][:], D["ident"]); P.dma("gpsimd", G["identb"][:], D["ident"])
    P.memset("vector", G["onesb"][:], 1.0); P.memset("vector", G["onesf"][:], 1.0)
    P.memset("vector", G["epst"][:], EPS)
    CS = {"n": 0}
    ctasks = []
    if "S" in stages:
        P.push_scope()
        G["WB"] = P.sb("WB", [128, 32, 2, 128], BF16)
        G["WC"] = P.sb("WC", [128, 32, 2, 128], BF16)
        G["T"] = P.sb("T", [128, 32, 128], BF16)
        G["R"] = P.sb("R", [128, 32]); G["LN"] = P.sb("LN", [128, 2, 32])
        G["th8"] = P.sb("th8", [128, 32]); G["a8"] = P.sb("a8", [128, 32])
        G["iota"] = P.sb("iota_t", [128, 129]); P.dma("sync", G["iota"][:], D["iota"])
        TA = P.sb("TA", [128, 32, 129]); TB = P.sb("TB", [128, 32, 129])
        G["Qr"] = TA; G["Qi"] = TB; G["TC"] = TA; G["TS"] = TB
        ssm_alloc(P, G)
        P.memset("vector", G["Floc"][:], 0.0)
        ctasks = conv_tasks(P, D, G, CS)
        phase_s(P, nc, D, G, flush=False)
        load_wssm(P, G, D["w_in"], G["gpre"])
        make_tables(P, G, "Q", flush=False)
        P.flush()
        P.pop_scope(); P.pop_scope()
        P.push_scope()
        xbf = [P.sb("xbf%d" % i, [128, 16, 1024], BF16) for i in range(2)]
        PS = {"f": [P.ps("psf%d" % i, [128, 512]) for i in range(6)], "b": [P.ps("psb%d" % i, [128, 1024], BF16) for i in range(2)]}
        PS["f"] += [PS["f"][0], PS["f"][1]]
        ssm_load(P, D["xT_all"], 0, xbf[0])
        for u in range(16):
            if u + 1 < 16:
                ssm_load(P, D["xT_all"], (u + 1) * 1024, xbf[(u + 1) % 2])
            for _ in range(2):
                if ctasks:
                    ctasks.pop(0)()
            ssm_unit(P, G, D["xT_all"], u * 1024, xbf[u % 2], PS, "G", u, preloaded=True)
        P.flush()
        P.pop_scope()
        P.push_scope()
        Mt = P.sb("cmask_t", [128, 16, 2, 32])
        P.dma("sync", Mt[:], D["cmask"])
        carry_compute(P, G, Mt)
        make_tables(P, G, "E", flush=False)
        if "M" in stages:
            mem_kv(P, G, D, flush=False)
        P.flush()
        if "M" in stages:
            P.pop_scope()
        P.pop_scope(); P.pop_scope()
        P.push_scope()
        xb = P.sb("xbfL", [128, 16, 1024], BF16)
        G["Dt"] = [P.sb("Dt%d" % i, [128, 2, 128]) for i in range(2)]
        G["Wb"] = [P.sb("Wb%d" % i, [128, 2, 129]) for i in range(2)]
        LO = {}
        LO["Tm2"] = [P.sb("Tm2_%d" % i, [128, 4, 128]) for i in range(2)]
        LO["Zp"] = [P.sb("Zp%d" % i, [128, 2, 2, 128], BF16) for i in range(2)]
        LO["Ysb"] = [P.sb("Ysb%d" % i, [128, 4, 128]) for i in range(2)]
        LO["Rout"] = xb[:, 0:8, :].bitcast(F32)
        _yb = xb[:, 8:16, :].bitcast(F32)
        LO["ysT"] = apx(_yb, 0, [[1024, 4], [1, 1024]])
        for nm in ["ga", "gl", "zt"]:
            LO[nm] = P.sb("L_" + nm, [128, 4, 256])
        LO["gb"] = LO["ga"]
        LO["gbf"] = P.sb("L_gbf", [128, 4, 256], BF16); LO["sq"] = P.sb("L_sq", [128, 4, 256], BF16)
        LO["rs"] = P.sb("L_rs", [128, 256])
        LO["Wglu"] = P.sb("L_wglu", [128, 4, 512], BF16)
        P.dma("gpsimd", LO["Wglu"][:], D["w_glu"].rearrange("(k p) c -> p k c", p=128))
        PS = {"f": [P.ps("psLf%d" % i, [128, 512]) for i in range(6)], "b": [P.ps("psLb%d" % i, [128, 1024], BF16) for i in range(2)]}
        PS["f"] += [PS["f"][0], PS["f"][1]]
        for k in range(2):
            ssm_unit(P, G, D["xT_own"], 256 + k * 1024, xb, PS, "L", None, kown=k, LO=LO)
            ssm_finish(P, G, LO, PS, k)
        if debug:
            P.dma("sync", dbg["yssm"], G["yssm_n"][:], key="dbg_yssm")
        P.flush()
        P.pop_scope()
        P.pop_scope()
    else:
        P.memset("vector", G["yssm_n"][:], 0.0)
        if "M" in stages:
            mem_kv(P, G, D)
    assert not ctasks
    if "1" in stages:
        sweep1(P, G, D, dbg=dbg)
        if debug:
            P.dma("sync", dbg["x1T"], D["x1T"], key="dbg_x1T")
    if "2" in stages:
        sweep2(P, G, D)
    P.flush()
    P.close()
    return nc


def host_inputs(inp, core, shared):
    m = dict(shared)
    t0 = core * 2048
    xT = shared["xT_all"]
    own = np.zeros((2048, 2560), np.float32)
    lo = t0 - 256; hi = t0 + 2304
    a = max(lo, 0); b = min(hi, 16384)
    own[:, a - lo:b - lo] = xT[:, a:b]
    m["xT_own"] = own
    m["na_tab"] = host_na_tables(np.asarray(inp["na_rpb"][0], np.float32), core)
    m["cmask"] = host_carry_masks(core)
    return m


def host_shared(inp):
    sc, B, C, dcol = host_ssm_layout(inp)
    ident, iota, maskF, maskB = host_consts()
    f = lambda a: np.ascontiguousarray(np.asarray(a, np.float32))
    sh = {"ssm_sc": sc, "ssm_B": B, "ssm_C": C, "ssm_dcol": dcol, "iota": iota, "maskF": maskF, "maskB": maskB, "ident": ident,
          "xT_all": f(np.asarray(inp["x"][0]).T), "w_in": f(inp["w_in"][0]), "w_out": f(inp["w_out"][0]),
          "w_ff1": f(inp["w_ff1"][0]), "w_ff2": f(inp["w_ff2"][0]), "w_glu": f(inp["w_glu"][0]),
          "w_mem_kv": f(inp["w_mem_kv"][0]), "memT": f(np.asarray(inp["mem"][0]).T),
          "gpre": colvec(inp["norm_mix_pre"][0]), "g_post": colvec(inp["norm_mix_post"][0]),
          "g_pre2": colvec(inp["norm_mlp_pre"][0]), "g_post2": colvec(inp["norm_mlp_post"][0]),
          "g_mem": colvec(inp["mem_norm"][0]),
          "g_out": colvec(np.concatenate([np.asarray(inp["out_norm_na"][0]), np.asarray(inp["out_norm_ssm"][0]),
                                          np.asarray(inp["out_norm_mem"][0])])),
          "bglu": colvec(inp["b_glu"][0])}
    return sh


_CACHE = {}


def kernel(**inputs):
    inp = {k: np.asarray(v) for k, v in inputs.items()}
    sh = host_shared(inp)
    maps = [host_inputs(inp, c, sh) for c in range(8)]
    shapes = {k: v.shape for k, v in maps[0].items()}
    debug = bool(int(_os.environ.get("MK_DEBUG", "0")))
    stages = _os.environ.get("MK_STAGES", "SGLM12")
    ncores = int(_os.environ.get("MK_CORES", "8"))
    nc = build_program(shapes, debug=debug, stages=stages)
    res = run_bass_kernel_spmd(nc, maps[:ncores], core_ids=list(range(ncores)))
    if debug:
        _CACHE["res"] = res.results
    out = np.zeros((1, 16384, 2048), np.float32)
    for c in range(ncores):
        out[0, c * 2048:(c + 1) * 2048, :] = res.results[c]["outT"].T
    return out
```

```python
import numpy as np
import concourse.bass as bass
import concourse.mybir as mybir
from contextlib import ExitStack
from concourse.bass_utils import run_bass_kernel_spmd
import os as _os


F32 = mybir.dt.float32
BF16 = mybir.dt.bfloat16
ALU = mybir.AluOpType
ACT = mybir.ActivationFunctionType
AX = mybir.AxisListType
ENGS = ["tensor", "vector", "scalar", "gpsimd", "sync"]


def apx(base, off, dims):
    return bass.AP(base.tensor, base.offset + off, [list(base.ap[0])] + [list(d) for d in dims])


def _names(aps):
    out = []
    for a in aps:
        if a is None or isinstance(a, (int, float)):
            continue
        out.append(a.tensor.name)
    return out


class Prog:
    def __init__(self, nc):
        self.nc = nc
        self.gs = ExitStack()
        self.sems = {e: self.gs.enter_context(nc.semaphore("s_" + e)) for e in ENGS}
        self.dsems = {}
        self.seq_base = {e: 0 for e in ENGS}
        self.dma_cnt = {}
        self.scopes = [self.gs]
        self._reset()

    def _reset(self):
        self.ops = {e: [] for e in ENGS}
        self.last_w = {}
        self.reads = {}

    def push_scope(self):
        es = ExitStack(); self.scopes.append(es); return es

    def pop_scope(self):
        self.scopes.pop().close()

    def _uniq(self, name):
        self._names = getattr(self, "_names", {})
        n = self._names.get(name, 0)
        self._names[name] = n + 1
        return name if n == 0 else "%s_v%d" % (name, n)

    def sb(self, name, shape, dt=F32):
        return self.scopes[-1].enter_context(self.nc.sbuf_tensor(self._uniq(name), list(shape), dt))

    def ps(self, name, shape, dt=F32):
        return self.scopes[-1].enter_context(self.nc.psum_tensor(self._uniq(name), list(shape), dt))

    def op(self, eng, fn, r=(), w=(), dma=None):
        deps = []
        for x in r:
            if x in self.last_w:
                deps.append((self.last_w[x], "raw"))
        for x in w:
            if x in self.last_w:
                deps.append((self.last_w[x], "waw"))
            for t in self.reads.get(x, []):
                deps.append((t, "war"))
        idx = len(self.ops[eng])
        if dma is not None:
            self.dma_cnt[dma] = self.dma_cnt.get(dma, 0) + 1
            tok = ("dma", dma, self.dma_cnt[dma])
        else:
            tok = ("eng", eng, idx)
        fdeps = []
        for d, kind in deps:
            if d[0] == "eng" and d[1] == eng and dma is None:
                if eng == "tensor" or kind != "raw":
                    continue
            fdeps.append(d)
        self.ops[eng].append(dict(fn=fn, deps=fdeps, tok=tok, dma=dma, signal=False))
        for x in r:
            self.reads.setdefault(x, []).append(tok)
        for x in w:
            self.last_w[x] = tok
            self.reads[x] = []
        return tok

    def dma(self, eng, out, in_, r=None, w=None, key=None):
        rr = _names([in_]) if r is None else r
        ww = _names([out]) if w is None else w
        k = key or ww[0]
        return self.op(eng, lambda e: e.dma_start(out=out, in_=in_), r=rr, w=ww, dma=k)

    def mm(self, out, lhsT, rhs, start=True, stop=True, r=None, w=None, **kw):
        rr = _names([lhsT, rhs]) if r is None else r
        ww = _names([out]) if w is None else w
        return self.op("tensor", lambda e: e.matmul(out, lhsT, rhs, start=start, stop=stop, **kw), r=rr, w=ww)

    def tr(self, out, in_, ident, r=None, w=None):
        rr = _names([in_, ident]) if r is None else r
        ww = _names([out]) if w is None else w
        return self.op("tensor", lambda e: e.transpose(out, in_, ident), r=rr, w=ww)

    def act(self, out, in_, func, scale=None, bias=None, accum_out=None, r=None, w=None):
        rr = _names([in_, scale, bias]) if r is None else r
        ww = _names([out, accum_out]) if w is None else w
        kw = {}
        if scale is not None: kw["scale"] = scale
        if bias is not None: kw["bias"] = bias
        if accum_out is not None: kw["accum_out"] = accum_out
        return self.op("scalar", lambda e: e.activation(out=out, in_=in_, func=func, **kw), r=rr, w=ww)

    def tt(self, eng, out, in0, in1, op, r=None, w=None):
        rr = _names([in0, in1]) if r is None else r
        ww = _names([out]) if w is None else w
        return self.op(eng, lambda e: e.tensor_tensor(out=out, in0=in0, in1=in1, op=op), r=rr, w=ww)

    def ts(self, eng, out, in0, s1, s2, op0, op1=None, accum_out=None, r=None, w=None):
        rr = _names([in0, s1, s2]) if r is None else r
        ww = _names([out, accum_out]) if w is None else w
        kw = {}
        if op1 is not None: kw["op1"] = op1
        if accum_out is not None: kw["accum_out"] = accum_out
        return self.op(eng, lambda e: e.tensor_scalar(out=out, in0=in0, scalar1=s1, scalar2=s2, op0=op0, **kw), r=rr, w=ww)

    def stt(self, out, in0, scalar, in1, op0, op1, accum_out=None, r=None, w=None):
        rr = _names([in0, scalar, in1]) if r is None else r
        ww = _names([out, accum_out]) if w is None else w
        kw = {}
        if accum_out is not None: kw["accum_out"] = accum_out
        return self.op("vector", lambda e: e.scalar_tensor_tensor(out=out, in0=in0, scalar=scalar, in1=in1, op0=op0, op1=op1, **kw), r=rr, w=ww)

    def copy(self, eng, out, in_, r=None, w=None):
        rr = _names([in_]) if r is None else r
        ww = _names([out]) if w is None else w
        if eng == "scalar":
            return self.op(eng, lambda e: e.activation(out=out, in_=in_, func=ACT.Copy), r=rr, w=ww)
        return self.op(eng, lambda e: e.tensor_copy(out=out, in_=in_), r=rr, w=ww)

    def memset(self, eng, ap, val):
        return self.op(eng, lambda e: e.memset(ap, val), r=[], w=_names([ap]))

    def recip(self, out, in_, r=None, w=None):
        rr = _names([in_]) if r is None else r
        ww = _names([out]) if w is None else w
        return self.op("vector", lambda e: e.reciprocal(out=out, in_=in_), r=rr, w=ww)

    def scan(self, out, d0, d1, initial, r=None, w=None):
        rr = _names([d0, d1, initial]) if r is None else r
        ww = _names([out]) if w is None else w
        return self.op("vector", lambda e: e.tensor_tensor_scan(out=out, data0=d0, data1=d1, initial=initial,
                                                                 op0=ALU.mult, op1=ALU.add), r=rr, w=ww)

    def flush(self, final_dma_keys=None):
        nc = self.nc
        ops = self.ops
        prod = set()
        for e in ENGS:
            for o in ops[e]:
                for d in o["deps"]:
                    if d[0] == "eng":
                        prod.add((d[1], d[2]))
        for e in ENGS:
            for i in range(len(ops[e]) - 1, -1, -1):
                if ops[e][i]["fn"] is not None and ops[e][i]["dma"] is None:
                    prod.add((e, i)); break
        seq = {}
        fin = {}
        for e in ENGS:
            c = self.seq_base[e]
            for i, o in enumerate(ops[e]):
                if (e, i) in prod:
                    c += 1
                    o["signal"] = True
                    seq[(e, i)] = c
            fin[e] = c
        for k in self.dma_cnt:
            if k not in self.dsems:
                self.dsems[k] = self.gs.enter_context(nc.semaphore("d_" + k))
        sems, dsems = self.sems, self.dsems
        dma_fin = dict(self.dma_cnt)
        seq_base = dict(self.seq_base)

        def make(e):
            def body(engine):
                waited = {}
                def wait(k, v):
                    if waited.get(k, 0) >= v:
                        return
                    waited[k] = v
                    s = sems[k[1]] if k[0] == "eng" else dsems[k[1]]
                    engine.wait_ge(s, v)
                for i, o in enumerate(ops[e]):
                    need = {}
                    for d in o["deps"]:
                        if d[0] == "eng":
                            k = ("eng", d[1]); v = seq[(d[1], d[2])]
                        else:
                            k = ("dma", d[1]); v = 16 * d[2]
                        if v > need.get(k, 0):
                            need[k] = v
                    for k, v in need.items():
                        wait(k, v)
                    if o["fn"] is None:
                        continue
                    ins = o["fn"](engine)
                    if o["dma"] is not None:
                        ins.then_inc(dsems[o["dma"]], 16)
                    elif o["signal"]:
                        ins.then_inc(sems[e], 1)
                for e2 in ENGS:
                    if fin[e2] > seq_base[e2]:
                        wait(("eng", e2), fin[e2])
                for k, n in dma_fin.items():
                    if n > 0:
                        wait(("dma", k), 16 * n)
            return body

        with nc.Block() as block:
            block.tensor(make("tensor"))
            block.vector(make("vector"))
            block.scalar(make("scalar"))
            block.gpsimd(make("gpsimd"))
            block.sync(make("sync"))
        self.seq_base = fin
        self._reset()

    def close(self):
        while len(self.scopes) > 1:
            self.pop_scope()
        self.gs.close()


TWO_PI = float(2 * np.pi)
MAGIC = 12582912.0
SHRINK = 1.0 - 2e-6


def host_ssm_layout(inp):
    lam_re = inp["ssm_lam_re"][0]; lam_im = inp["ssm_lam_im"][0]; log_dt = inp["ssm_log_dt"][0]
    b_re = inp["ssm_b_re"][0]; b_im = inp["ssm_b_im"][0]; c_re = inp["ssm_c_re"][0]; c_im = inp["ssm_c_im"][0]
    d = inp["ssm_d"][0]
    sc = np.zeros((128, 3, 32), np.float32)
    B = np.zeros((128, 2, 32, 16), np.float32)
    C = np.zeros((128, 2, 32, 16), np.float32)
    for gp in range(16):
        for dr in range(2):
            sl = gp * 2 + dr
            for gi in range(2):
                g = 2 * gp + gi
                rows = slice(gi * 64, gi * 64 + 64)
                sc[rows, 0, sl] = lam_re[dr, g]
                sc[rows, 1, sl] = lam_im[dr, g]
                sc[rows, 2, sl] = log_dt[dr, g]
                B[rows, 0, sl, :] = b_re[dr, g]
                B[rows, 1, sl, :] = b_im[dr, g]
                C[rows, 0, sl, :] = c_re[dr, g].T
                C[rows, 1, sl, :] = c_im[dr, g].T
    dcol = np.zeros((128, 32), np.float32)
    for g in range(32):
        dcol[:, g] = np.tile(d[g], 8)
    return sc, B, C, dcol


def host_consts():
    ident = np.eye(128, dtype=np.float32)
    iota = np.tile(np.arange(129, dtype=np.float32)[None, :], (128, 1))
    s_idx = np.arange(128) // 16
    maskF = (s_idx[None, :] >= s_idx[:, None]).astype(np.float32)
    maskB = (s_idx[:, None] >= s_idx[None, :]).astype(np.float32)
    return ident, iota, maskF, maskB


def cmul(P, eng, o_re, o_im, a_re, a_im, b_re, b_im, t1, t2):
    P.tt(eng, t1, a_im, b_im, ALU.mult)
    P.tt(eng, o_re, a_re, b_re, ALU.mult)
    P.tt(eng, o_re, o_re, t1, ALU.subtract)
    P.tt(eng, t2, a_im, b_re, ALU.mult)
    P.tt(eng, o_im, a_re, b_im, ALU.mult)
    P.tt(eng, o_im, o_im, t2, ALU.add)


def sin_of(P, out, x, tA, tB, shape_ap=None):
    P.ts("vector", tA, x, 1.0 / TWO_PI, MAGIC, ALU.mult, ALU.add)
    P.ts("vector", tA, tA, -MAGIC, -TWO_PI, ALU.add, ALU.mult)
    P.tt("vector", tB, tA, x, ALU.add)
    P.act(out, tB, ACT.Sin, scale=SHRINK)


def phase_s(P, nc, D, G, flush=True):
    es = P.push_scope()
    sc = P.sb("s_sc", [128, 3, 32]); Bt = P.sb("s_B", [128, 2, 32, 16]); Ct = P.sb("s_C", [128, 2, 32, 16])
    dcol = P.sb("s_dcol", [128, 32])
    maskF = P.sb("s_maskF", [128, 128]); maskB = P.sb("s_maskB", [128, 128])
    P.dma("sync", sc[:], D["ssm_sc"]); P.dma("sync", Bt[:], D["ssm_B"]); P.dma("sync", Ct[:], D["ssm_C"])
    P.dma("sync", dcol[:], D["ssm_dcol"])
    P.dma("sync", maskF[:], D["maskF"]); P.dma("sync", maskB[:], D["maskB"])
    identf = G["identf"]
    n = 0
    def T32(nm):
        return P.sb("s_" + nm, [128, 32])
    lre = T32("lre"); dt = T32("dt"); a = T32("a"); th = T32("th"); mag = T32("mag")
    sn = T32("sn"); cs = T32("cs"); tA = T32("tA"); tB = T32("tB"); thc = T32("thc")
    lbr = T32("lbr"); lbi = T32("lbi"); nr = T32("nr"); den = T32("den"); gre = T32("gre"); gim = T32("gim")
    ilr = T32("ilr"); ili = T32("ili"); t1 = T32("t1"); t2 = T32("t2")
    V = "vector"
    P.ts(V, lre[:], sc[:, 0, :], -1e-4, None, ALU.min)
    P.act(dt[:], sc[:, 2, :], ACT.Exp)
    P.tt(V, a[:], lre[:], dt[:], ALU.mult)
    P.tt(V, th[:], sc[:, 1, :], dt[:], ALU.mult)
    P.act(mag[:], a[:], ACT.Exp)
    sin_of(P, sn[:], th[:], tA[:], tB[:])
    P.ts(V, thc[:], th[:], float(np.pi / 2), None, ALU.add)
    sin_of(P, cs[:], thc[:], tA[:], tB[:])
    P.tt(V, lbr[:], mag[:], cs[:], ALU.mult)
    P.tt(V, lbi[:], mag[:], sn[:], ALU.mult)
    P.ts(V, nr[:], lbr[:], -1.0, None, ALU.add)
    P.tt(V, den[:], lre[:], lre[:], ALU.mult)
    P.tt(V, t1[:], sc[:, 1, :], sc[:, 1, :], ALU.mult)
    P.tt(V, den[:], den[:], t1[:], ALU.add)
    P.recip(den[:], den[:])
    P.tt(V, gre[:], nr[:], lre[:], ALU.mult)
    P.tt(V, t1[:], lbi[:], sc[:, 1, :], ALU.mult)
    P.tt(V, gre[:], gre[:], t1[:], ALU.add)
    P.tt(V, gre[:], gre[:], den[:], ALU.mult)
    P.tt(V, gim[:], lbi[:], lre[:], ALU.mult)
    P.tt(V, t1[:], nr[:], sc[:, 1, :], ALU.mult)
    P.tt(V, gim[:], gim[:], t1[:], ALU.subtract)
    P.tt(V, gim[:], gim[:], den[:], ALU.mult)
    P.tt(V, t1[:], mag[:], mag[:], ALU.mult)
    P.recip(t1[:], t1[:])
    P.tt(V, ilr[:], lbr[:], t1[:], ALU.mult)
    P.tt(V, ili[:], lbi[:], t1[:], ALU.mult)
    P.ts(V, ili[:], ili[:], -1.0, None, ALU.mult)
    PWr = P.sb("s_PWr", [128, 16, 32]); PWi = P.sb("s_PWi", [128, 16, 32])
    P.memset(V, PWr[:, 7, :], 1.0); P.memset(V, PWi[:, 7, :], 0.0)
    for k in range(0, 8):
        cmul(P, V, PWr[:, 8 + k, :], PWi[:, 8 + k, :], PWr[:, 7 + k, :], PWi[:, 7 + k, :], lbr[:], lbi[:], t1[:], t2[:])
    for k in range(0, 7):
        cmul(P, V, PWr[:, 6 - k, :], PWi[:, 6 - k, :], PWr[:, 7 - k, :], PWi[:, 7 - k, :], ilr[:], ili[:], t1[:], t2[:])
    LN = G["LN"]
    sqa_r = T32("sqa_r"); sqa_i = T32("sqa_i"); sqb_r = T32("sqb_r"); sqb_i = T32("sqb_i")
    cur = (PWr[:, 15, :], PWi[:, 15, :])
    bufs = [(sqa_r[:], sqa_i[:]), (sqb_r[:], sqb_i[:])]
    for i in range(7):
        o = (LN[:, 0, :], LN[:, 1, :]) if i == 6 else bufs[i % 2]
        cmul(P, V, o[0], o[1], cur[0], cur[1], cur[0], cur[1], t1[:], t2[:])
        cur = o
    P.act(G["R"][:], a[:], ACT.Exp, scale=8.0)
    P.ts(V, G["th8"][:], th[:], 8.0, None, ALU.mult)
    P.ts(V, G["a8"][:], a[:], 8.0, None, ALU.mult)
    PBr = P.sb("s_PBr", [128, 32, 8]); PBi = P.sb("s_PBi", [128, 32, 8])
    PCr = P.sb("s_PCr", [128, 32, 8]); PCi = P.sb("s_PCi", [128, 32, 8])
    PGr = P.sb("s_PGr", [128, 32, 8]); PGi = P.sb("s_PGi", [128, 32, 8])
    t8a = P.sb("s_t8a", [128, 32, 8]); t8b = P.sb("s_t8b", [128, 32, 8])
    Qr = P.sb("s_Qr", [128, 32, 8]); Qi = P.sb("s_Qi", [128, 32, 8])
    def gather(dst, src, k0f, stf, k0b, stb):
        P.copy(V, apx(dst[:], 0, [[16, 16], [1, 8]]), apx(src[:], k0f * 32, [[2, 16], [32 * stf, 8]]))
        P.copy(V, apx(dst[:], 8, [[16, 16], [1, 8]]), apx(src[:], k0b * 32 + 1, [[2, 16], [32 * stb, 8]]))
    gather(Qr, PWr, 14, -1, 7, 1); gather(Qi, PWi, 14, -1, 7, 1)
    gb_r = apx(gre[:], 0, [[1, 32], [0, 8]]); gb_i = apx(gim[:], 0, [[1, 32], [0, 8]])
    cmul(P, V, PBr[:], PBi[:], Qr[:], Qi[:], gb_r, gb_i, t8a[:], t8b[:])
    gather(PCr, PWr, 8, 1, 15, -1); gather(PCi, PWi, 8, 1, 15, -1)
    gather(PGr, PWr, 0, 1, 7, -1); gather(PGi, PWi, 0, 1, 7, -1)
    WB = G["WB"]; WC = G["WC"]; Tm = G["T"]
    NBS = 8
    Wr = P.sb("s_Wr", [128, NBS, 128]); Wi = P.sb("s_Wi", [128, NBS, 128])
    Gr = P.sb("s_Gr", [128, NBS, 128]); Gi = P.sb("s_Gi", [128, NBS, 128])
    X1 = P.sb("s_X1", [128, NBS, 128]); X2 = P.sb("s_X2", [128, NBS, 128])
    tf = P.sb("s_tf", [128, 128]); tb = P.sb("s_tb", [128, 128])
    pst = [P.ps("s_ps%d" % i, [128, 512]) for i in range(4)]
    def bc_coef(t, s0):
        return apx(t[:], s0 * 8, [[8, NBS], [1, 8], [0, 16]])
    def bc_mat(t, ri, s0):
        return apx(t[:], ri * 512 + s0 * 16, [[16, NBS], [0, 8], [1, 16]])
    def v4(t):
        return apx(t[:], 0, [[128, NBS], [16, 8], [1, 16]])
    for b in range(32 // NBS):
        s0 = b * NBS
        E = "vector" if b % 2 == 0 else "gpsimd"
        P.tt(E, v4(X1), bc_coef(PBi, s0), bc_mat(Bt, 1, s0), ALU.mult)
        P.tt(E, v4(Wr), bc_coef(PBr, s0), bc_mat(Bt, 0, s0), ALU.mult)
        P.tt(E, v4(Wr), v4(Wr), v4(X1), ALU.subtract)
        P.tt(E, v4(X2), bc_coef(PBi, s0), bc_mat(Bt, 0, s0), ALU.mult)
        P.tt(E, v4(Wi), bc_coef(PBr, s0), bc_mat(Bt, 1, s0), ALU.mult)
        P.tt(E, v4(Wi), v4(Wi), v4(X2), ALU.add)
        P.tt(E, v4(X1), bc_coef(PGi, s0), bc_mat(Ct, 1, s0), ALU.mult)
        P.tt(E, v4(Gr), bc_coef(PGr, s0), bc_mat(Ct, 0, s0), ALU.mult)
        P.tt(E, v4(Gr), v4(Gr), v4(X1), ALU.subtract)
        P.tt(E, v4(X2), bc_coef(PGi, s0), bc_mat(Ct, 0, s0), ALU.mult)
        P.tt(E, v4(Gi), bc_coef(PGr, s0), bc_mat(Ct, 1, s0), ALU.mult)
        P.tt(E, v4(Gi), v4(Gi), v4(X2), ALU.add)
        P.ts(E, Gi[:], Gi[:], -1.0, None, ALU.mult)
        for j in range(NBS):
            sl = s0 + j
            for ri, Wt in enumerate((Wr, Wi)):
                ps = pst[(j * 2 + ri) % 2]
                P.tr(ps[:, 0:128], Wt[:, j, :], identf[:])
                P.copy("scalar", WB[:, sl, ri, :], ps[:, 0:128])
        for jp in range(NBS // 2):
            gp = (s0 // 2) + jp
            jf = 2 * jp; jb = 2 * jp + 1
            for gi in range(2):
                g = 2 * gp + gi
                rows = slice(gi * 64, gi * 64 + 64)
                psf = pst[2]; psb = pst[3]
                P.mm(psf[:, 0:128], Wr[rows, jf, :], Gr[rows, jf, :], start=True, stop=False)
                P.mm(psf[:, 0:128], Wi[rows, jf, :], Gi[rows, jf, :], start=False, stop=True)
                P.mm(psb[:, 0:128], Wr[rows, jb, :], Gr[rows, jb, :], start=True, stop=False)
                P.mm(psb[:, 0:128], Wi[rows, jb, :], Gi[rows, jb, :], start=False, stop=True)
                P.tt(V, tf[:], psf[:, 0:128], maskF[:], ALU.mult)
                P.tt(V, tb[:], psb[:, 0:128], maskB[:], ALU.mult)
                P.tt(V, tf[:], tf[:], tb[:], ALU.add)
                P.stt(Tm[:, g, :], identf[:], dcol[:, g:g + 1], tf[:], ALU.mult, ALU.add)
        P.tt(E, v4(X1), bc_coef(PCi, s0), bc_mat(Ct, 1, s0), ALU.mult)
        P.tt(E, v4(X2), bc_coef(PCr, s0), bc_mat(Ct, 0, s0), ALU.mult)
        P.tt(E, apx(WC[:], s0 * 256, [[256, NBS], [16, 8], [1, 16]]), v4(X2), v4(X1), ALU.subtract)
        P.tt(E, v4(X1), bc_coef(PCi, s0), bc_mat(Ct, 0, s0), ALU.mult)
        P.tt(E, v4(X2), bc_coef(PCr, s0), bc_mat(Ct, 1, s0), ALU.mult)
        P.tt(E, v4(X2), v4(X2), v4(X1), ALU.add)
        P.ts(E, apx(WC[:], s0 * 256 + 128, [[256, NBS], [1, 128]]), X2[:], -1.0, None, ALU.mult)
    if flush:
        P.flush()
        P.pop_scope()


def make_tables(P, G, kind, flush=True):
    V = "vector"
    n = 129 if kind == "E" else 128
    P.push_scope()
    iota = G["iota"]; th8 = G["th8"]; a8 = G["a8"]
    X = P.sb("mt_X", [128, 4, n]); XA = P.sb("mt_XA", [128, 4, n]); XB = P.sb("mt_XB", [128, 4, n])
    Rp = P.sb("mt_Rp", [128, 4, n])
    for hq in range(8):
        sl = slice(hq * 4, (hq + 1) * 4)
        P.tt(V, X[:], apx(th8[:], hq * 4, [[1, 4], [0, n]]), apx(iota[:], 0, [[0, 4], [1, n]]), ALU.mult)
        if kind == "E":
            sin_of(P, G["TS"][:, sl, :], X[:], XA[:], XB[:])
            P.ts(V, X[:], X[:], float(np.pi / 2), None, ALU.add)
            sin_of(P, G["TC"][:, sl, :], X[:], XA[:], XB[:])
        else:
            P.tt(V, Rp[:], apx(a8[:], hq * 4, [[1, 4], [0, n]]), apx(iota[:], 0, [[0, 4], [1, n]]), ALU.mult)
            P.act(Rp[:], Rp[:], ACT.Exp)
            sin_of(P, XA[:], X[:], XA[:], XB[:])
            P.tt(V, G["Qi"][:, sl, 0:128], XA[:], Rp[:], ALU.mult)
            P.ts(V, X[:], X[:], float(np.pi / 2), None, ALU.add)
            sin_of(P, XA[:], X[:], XA[:], XB[:])
            P.tt(V, G["Qr"][:, sl, 0:128], XA[:], Rp[:], ALU.mult)
    if flush:
        P.flush()
        P.pop_scope()


EPS = 1e-6
GC = float(np.sqrt(2 / np.pi))


def host_carry_masks(core):
    M = np.zeros((128, 16, 2, 32), np.float32)
    for j in range(16):
        for k in range(2):
            u = 2 * core + k
            M[:, j, k, 0::2] = 1.0 if j < u else 0.0
            M[:, j, k, 1::2] = 1.0 if j < 15 - u else 0.0
    return M


def ssm_alloc(P, G):
    G["Wssm"] = P.sb("Wssm", [128, 16, 512], BF16)
    G["Rt"] = P.sb("Rt", [128, 32, 128], BF16)
    G["U"] = P.sb("U", [128, 32, 128], BF16)
    G["Floc"] = P.sb("Floc", [128, 16, 32, 2])
    G["acc"] = P.sb("acc", [128, 32, 4])
    G["ss8"] = P.sb("ss8", [128, 8]); G["rstd8"] = P.sb("rstd8", [128, 8])
    G["junk"] = P.sb("junk", [128, 128])
    G["xsq"] = [P.sb("xsq%d" % i, [128, 1024], BF16) for i in range(2)]
    G["Tm"] = [P.sb("Tmp%d" % i, [128, 4, 128]) for i in range(2)]
    G["Cin"] = P.sb("Cin", [128, 2, 32, 2])
    if "epst" not in G:
        G["epst"] = P.sb("epst", [128, 1])


def load_wssm(P, G, w_in_ap, gpre):
    stg = [P.sb("wstg%d" % i, [128, 512]) for i in range(2)]
    for k in range(16):
        s = stg[k % 2]
        P.dma("sync", s[:], w_in_ap[k * 128:(k + 1) * 128, 3072:3584])
        P.act(G["Wssm"][:, k, :], s[:], ACT.Copy, scale=gpre[:, k:k + 1])


def ssm_load(P, xsrc, tok0, xbf):
    src = xsrc[:, tok0:tok0 + 1024].rearrange("(k p) t -> p k t", p=128)
    for q in range(4):
        P.dma("gpsimd", xbf[:, 4 * q:4 * q + 4, :], src[:, 4 * q:4 * q + 4, :], key=xbf.name)


def ssm_unit(P, G, xsrc, tok0, xbf, PS, mode, uidx, kown=None, LO=None, preloaded=False):
    V = "vector"
    WB, R = G["WB"], G["R"]
    Rt, U = G["Rt"], G["U"]
    identf, identb = G["identf"], G["identb"]
    if not preloaded:
        ssm_load(P, xsrc, tok0, xbf)
    def xs(k, s):
        return apx(xbf[:], k * 1024 + s, [[8, 128]])
    PSG = PS["g"]
    for k in range(16):
        xq = G["xsq"][k % 2]
        P.act(xq[:], xbf[:, k, :], ACT.Square)
        for h in range(2):
            P.mm(PSG[:, h * 512:(h + 1) * 512], G["onesb"][:], xq[:, h * 512:(h + 1) * 512], start=(k == 0), stop=(k == 15))
    for s in range(8):
        P.stt(G["junk"][:], apx(PSG[:], s, [[8, 128]]), 1.0, identf[:], ALU.mult, ALU.mult,
              accum_out=G["ss8"][:, s:s + 1])
    P.act(G["rstd8"][:], G["ss8"][:], ACT.Sqrt, scale=1.0 / 2048, bias=G["epst"][:, 0:1])
    P.recip(G["rstd8"][:], G["rstd8"][:])
    for s in range(8):
        ps = PS["f"][2 + (s % 2)]
        for k in range(16):
            P.mm(ps[:, :], xs(k, s), G["Wssm"][:, k, :], start=(k == 0), stop=(k == 15))
        P.act(apx(Rt[:], s * 16, [[128, 32], [1, 16]]), apx(ps[:], 0, [[16, 32], [1, 16]]), ACT.Copy, scale=G["rstd8"][:, s:s + 1])
    for g8 in range(4):
        psT = PS["b"][g8 % 2]
        for gg in range(8):
            g = g8 * 8 + gg
            P.tr(psT[:, gg * 128:(gg + 1) * 128], Rt[:, g, :], identb[:])
        if g8 % 2 == 0:
            P.copy("scalar", U[:, g8 * 8:(g8 + 1) * 8, :], psT[:, :], w=["U%d" % g8], r=[psT.name])
        else:
            P.copy(V, U[:, g8 * 8:(g8 + 1) * 8, :], psT[:, :], w=["U%d" % g8], r=[psT.name])
    def hc(sl):
        gp = sl // 2
        psH = PS["f"][4 + sl % 2]
        ukey = ["U%d" % (gp // 4)]
        for ri in range(2):
            for gi in range(2):
                P.mm(psH[gi * 64:(gi + 1) * 64, ri * 128:(ri + 1) * 128], WB[:, sl, ri, gi * 64:(gi + 1) * 64], U[:, 2 * gp + gi, :],
                     start=True, stop=True, r=[WB.name] + ukey)
        return psH

    if mode == "G":
        for sl in range(32):
            dr = sl % 2
            psH = hc(sl)
            Qr, Qi = G["Qr"], G["Qi"]
            if dr == 0:
                qr = apx(Qr[:, sl, 0:128], 127, [[-1, 128]]); qi = apx(Qi[:, sl, 0:128], 127, [[-1, 128]])
            else:
                qr = Qr[:, sl, 0:128]; qi = Qi[:, sl, 0:128]
            hre = psH[:, 0:128]; him = psH[:, 128:256]
            acc = G["acc"]
            jk = G["Tm"][sl % 2]
            P.stt(jk[:, 0, :], hre, 1.0, qr, ALU.mult, ALU.mult, accum_out=acc[:, sl, 0:1])
            P.stt(jk[:, 1, :], him, 1.0, qi, ALU.mult, ALU.mult, accum_out=acc[:, sl, 1:2])
            P.stt(jk[:, 2, :], hre, 1.0, qi, ALU.mult, ALU.mult, accum_out=acc[:, sl, 2:3])
            P.stt(jk[:, 3, :], him, 1.0, qr, ALU.mult, ALU.mult, accum_out=acc[:, sl, 3:4])
    else:
        TC, TS = G["TC"], G["TS"]

        def stage_a(sl):
            dr = sl % 2
            psH = hc(sl)
            if dr == 0:
                hre = psH[:, 0:128]; him = psH[:, 128:256]
            else:
                hre = apx(psH[:], 127, [[-1, 128]]); him = apx(psH[:], 255, [[-1, 128]])
            c1 = TC[:, sl, 1:129]; s1 = TS[:, sl, 1:129]
            Tm = G["Tm"][sl % 2]; Dt = G["Dt"][sl % 2]
            P.tt(V, Tm[:, 0, :], hre, c1, ALU.mult)
            P.tt(V, Tm[:, 1, :], him, s1, ALU.mult)
            P.tt(V, Tm[:, 2, :], him, c1, ALU.mult)
            P.tt(V, Tm[:, 3, :], hre, s1, ALU.mult)
            P.tt(V, Dt[:, 0, :], Tm[:, 0, :], Tm[:, 1, :], ALU.add)
            P.tt(V, Dt[:, 1, :], Tm[:, 2, :], Tm[:, 3, :], ALU.subtract)

        def stage_b(sl):
            gp = sl // 2; dr = sl % 2
            Dt = G["Dt"][sl % 2]; Wb = G["Wb"][sl % 2]; Tm = LO["Tm2"][sl % 2]
            Rbc = apx(R[:], sl, [[0, 128]])
            for ri in range(2):
                P.scan(Wb[:, ri, 1:129], Rbc, Dt[:, ri, :], G["Cin"][:, kown, sl, ri:ri + 1])
            P.copy(V, Wb[:, :, 0:1], apx(G["Cin"][:], (kown * 32 + sl) * 2, [[1, 2], [1, 1]]))
            c0 = TC[:, sl, 0:128]; s0 = TS[:, sl, 0:128]
            Zp = LO["Zp"][gp % 2]
            wr = Wb[:, 0, 0:128]; wi = Wb[:, 1, 0:128]
            P.tt(V, Tm[:, 0, :], wr, c0, ALU.mult)
            P.tt(V, Tm[:, 1, :], wi, s0, ALU.mult)
            P.tt(V, Tm[:, 2, :], wr, s0, ALU.mult)
            P.tt(V, Tm[:, 3, :], wi, c0, ALU.mult)
            if dr == 0:
                zo_re = Zp[:, dr, 0, :]; zo_im = Zp[:, dr, 1, :]
            else:
                zo_re = apx(Zp[:], (dr * 2 + 0) * 128 + 127, [[-1, 128]])
                zo_im = apx(Zp[:], (dr * 2 + 1) * 128 + 127, [[-1, 128]])
            P.tt(V, zo_re, Tm[:, 0, :], Tm[:, 1, :], ALU.subtract)
            P.tt(V, zo_im, Tm[:, 2, :], Tm[:, 3, :], ALU.add)

        def stage_c(gp):
            Zp = LO["Zp"][gp % 2]
            WC, T = G["WC"], G["T"]
            psY = PS["f"][6 + ((gp // 2) % 2)]
            for gi in range(2):
                g = 2 * gp + gi
                col = ((gp % 2) * 2 + gi) * 128
                rows = slice(gi * 64, gi * 64 + 64)
                P.mm(psY[:, col:col + 128], T[:, g, :], U[:, g, :], start=True, stop=False, r=[T.name, "U%d" % (gp // 4)])
                n = 0
                for dr in range(2):
                    for ri in range(2):
                        n += 1
                        P.mm(psY[:, col:col + 128], WC[rows, gp * 2 + dr, ri, :], Zp[rows, dr, ri, :],
                             start=False, stop=(n == 4))
            if gp % 2 == 1:
                g0 = 2 * gp - 2
                Ysb = LO["Ysb"][(gp // 2) % 2]
                P.copy("scalar", Ysb[:], psY[:, :])
                psR = PS["f"][(gp // 2) % 2]
                for gg in range(4):
                    P.tr(psR[:, gg * 128:(gg + 1) * 128], Ysb[:, gg, :], identf[:])
                P.copy("scalar", apx(LO["Rout"][:], g0 * 16, [[16, 4], [512, 8], [1, 16]]),
                       apx(psR[:], 0, [[128, 4], [16, 8], [1, 16]]))

        stage_a(0)
        for sl in range(32):
            if sl + 1 < 32:
                stage_a(sl + 1)
            stage_b(sl)
            if sl % 2 == 1:
                stage_c(sl // 2)
    if mode == "G":
        acc = G["acc"]; Fl = G["Floc"]
        for par, ust in ((0, uidx), (1, 15 - uidx)):
            def a(c):
                return apx(acc[:], par * 4 + c, [[8, 16]])
            P.tt(V, apx(Fl[:], (ust * 32 + par) * 2 + 0, [[4, 16]]), a(0), a(1), ALU.subtract)
            P.tt(V, apx(Fl[:], (ust * 32 + par) * 2 + 1, [[4, 16]]), a(2), a(3), ALU.add)
    if mode == "L":
        for t in range(8):
            psR = PS["f"][2 + (t % 2)]
            for ch in range(4):
                P.tr(psR[:, ch * 128:(ch + 1) * 128], LO["Rout"][:, t, ch * 128:(ch + 1) * 128], identf[:])
            P.copy("scalar" if t % 2 == 0 else V, apx(LO["ysT"][:], t, [[1024, 4], [8, 128]]),
                   apx(psR[:], 0, [[128, 4], [1, 128]]))


def carry_compute(P, G, Mt):
    V = "vector"
    c_re = P.sb("cc_re", [128, 2, 32]); c_im = P.sb("cc_im", [128, 2, 32])
    n_re = P.sb("cn_re", [128, 2, 32]); n_im = P.sb("cn_im", [128, 2, 32])
    t1 = P.sb("cc_t1", [128, 2, 32]); t2 = P.sb("cc_t2", [128, 2, 32])
    LN = G["LN"]; Fl = G["Floc"]
    a_re = apx(LN[:], 0, [[0, 2], [1, 32]]); a_im = apx(LN[:], 32, [[0, 2], [1, 32]])
    P.memset(V, c_re[:], 0.0); P.memset(V, c_im[:], 0.0)
    for j in range(16):
        f_re = apx(Fl[:], j * 64 + 0, [[0, 2], [2, 32]]); f_im = apx(Fl[:], j * 64 + 1, [[0, 2], [2, 32]])
        m = Mt[:, j, :, :]
        P.tt(V, t1[:], a_im, c_im[:], ALU.mult)
        P.tt(V, n_re[:], a_re, c_re[:], ALU.mult)
        P.tt(V, n_re[:], n_re[:], t1[:], ALU.subtract)
        P.tt(V, t2[:], a_im, c_re[:], ALU.mult)
        P.tt(V, n_im[:], a_re, c_im[:], ALU.mult)
        P.tt(V, n_im[:], n_im[:], t2[:], ALU.add)
        P.tt(V, n_re[:], n_re[:], f_re, ALU.add)
        P.tt(V, n_im[:], n_im[:], f_im, ALU.add)
        P.tt(V, n_re[:], n_re[:], c_re[:], ALU.subtract)
        P.tt(V, n_im[:], n_im[:], c_im[:], ALU.subtract)
        P.tt(V, n_re[:], n_re[:], m, ALU.mult)
        P.tt(V, n_im[:], n_im[:], m, ALU.mult)
        P.tt(V, c_re[:], c_re[:], n_re[:], ALU.add)
        P.tt(V, c_im[:], c_im[:], n_im[:], ALU.add)
    Cin = G["Cin"]
    P.copy(V, apx(Cin[:], 0, [[64, 2], [2, 32]]), c_re[:])
    P.copy(V, apx(Cin[:], 1, [[64, 2], [2, 32]]), c_im[:])


def ssm_finish(P, G, LO, PS, kown):
    V = "vector"; PL = "gpsimd"
    ysT = LO["ysT"]
    for q in range(4):
        c0 = q * 256
        y = apx(ysT[:], c0, [[1024, 4], [1, 256]])
        a = LO["ga"]; b = LO["gb"]; gl = LO["gl"]; gbf = LO["gbf"]; zt = LO["zt"]; sq = LO["sq"]
        P.tt(PL, a[:], y, y, ALU.mult)
        P.ts(PL, a[:], a[:], 0.044715 * GC, GC, ALU.mult, ALU.add)
        P.tt(PL, a[:], a[:], y, ALU.mult)
        P.act(b[:], a[:], ACT.Tanh)
        P.stt(gl[:], b[:], 1.0, y, ALU.add, ALU.mult)
        P.act(gbf[:], gl[:], ACT.Copy, scale=0.5)
        for co in range(4):
            ps = PS["f"][4 + (co % 2)]
            for ci in range(4):
                P.mm(ps[:, 0:256], LO["Wglu"][:, ci, co * 128:(co + 1) * 128], gbf[:, ci, :], start=(ci == 0), stop=(ci == 3))
            P.act(zt[:, co, :], ps[:, 0:256], ACT.Sigmoid, bias=G["bglu"][:, co:co + 1])
        P.stt(gl[:], gl[:], 0.5, zt[:], ALU.mult, ALU.mult)
        P.tt(PL, sq[:], gl[:], gl[:], ALU.mult)
        pss = PS["f"][6]
        for ci in range(4):
            P.mm(pss[:, 0:256], G["onesb"][:], sq[:, ci, :], start=(ci == 0), stop=(ci == 3))
        rs = LO["rs"]
        P.act(rs[:], pss[:, 0:256], ACT.Ln, scale=1.0 / 512, bias=G["epst"][:, 0:1])
        P.act(rs[:], rs[:], ACT.Exp, scale=-0.5)
        for ci in range(4):
            P.stt(G["yssm_n"][:, ci, kown * 1024 + c0: kown * 1024 + c0 + 256], gl[:, ci, :], G["g_out"][:, 8 + ci:9 + ci],
                  rs[:], ALU.mult, ALU.mult)


EPS = 1e-6
NEG = -30000.0
QSCALE = float(128 ** -0.5)
NA_CLS = {0: (0, 0, 6), 1: (1, 2, 5), 14: (3, 28, 5), 15: (4, 28, 6)}


def na_class(j):
    if j in NA_CLS:
        return NA_CLS[j]
    return (2, 2 * j, 5)


def host_na_tables(rpb, core):
    GW, WH, WW = 64, 8, 16
    rows = 256
    tab = np.full((5, 8, 128, 768), NEG, np.float32)
    rep = {0: 0, 1: 1, 2: 6, 3: 14, 4: 15}
    for cls, j in rep.items():
        _, b, nch = na_class(j)
        q_halo_tok = (4 + 2 * j) * 64 + np.arange(128)
        q_glob = core * 2048 - 256 + q_halo_tok
        r = q_glob // GW; c = q_glob % GW
        rs = np.clip(r - WH // 2, 0, rows - WH); cs = np.clip(c - WW // 2, 0, GW - WW)
        for dr_ in range(WH):
            for dc_ in range(WW):
                kr = rs + dr_; kc = cs + dc_
                k_glob = kr * GW + kc
                k_halo = k_glob - (core * 2048 - 256)
                rel = k_halo - b * 64
                ok = (rel >= 0) & (rel < nch * 128)
                assert ok.all(), (cls, core)
                ch = rel // 128; kk = rel % 128
                oi = kr - r + WH - 1; oj = kc - c + WW - 1
                qi = np.arange(128)
                for h in range(8):
                    tab[cls, h, kk, ch * 128 + qi] = rpb[h, oi, oj]
    return tab


WCONV = {"w_in": [(0, 0), (0, 512), (0, 1024), (0, 1536), (0, 2048), (0, 2560), (0, 3584)],
         "w_out": [(0, c * 512) for c in range(4)],
         "w_ff1": [(0, c * 512) for c in range(16)]}
W_FF2 = [(jp * 2048, cg * 512) for cg in range(4) for jp in range(4)]


def conv_tasks_ff2(P, D):
    tasks = []
    for i, (r0, c0) in enumerate(W_FF2):
        def t(i=i, r0=r0, c0=c0):
            v = D["w_ff2"][r0:r0 + 2048, c0:c0 + 512].rearrange("(k p) c -> p k c", p=128)
            P.dma("gpsimd", D["wc_w_ff2"][i, :, 0:8, :], v[:, 0:8, :], key="wconv", w=["wc_w_ff2" + str(i)])
            P.dma("gpsimd", D["wc_w_ff2"][i, :, 8:16, :], v[:, 8:16, :], key="wconv", w=["wc_w_ff2" + str(i)])
        tasks.append(t)
    return tasks


def conv_tasks(P, D, G=None, S=None):
    tasks = []
    for name, lst in WCONV.items():
        gain = None
        for i, (r0, c0) in enumerate(lst):
            if gain is None:
                def t(name=name, i=i, r0=r0, c0=c0):
                    v = D[name][r0:r0 + 2048, c0:c0 + 512].rearrange("(k p) c -> p k c", p=128)
                    P.dma("gpsimd", D["wc_" + name][i, :, 0:8, :], v[:, 0:8, :], key="wconv", w=["wc_" + name + str(i)])
                    P.dma("gpsimd", D["wc_" + name][i, :, 8:16, :], v[:, 8:16, :], key="wconv", w=["wc_" + name + str(i)])
                tasks.append(t)
            else:
                for kq in range(4):
                    def t(name=name, i=i, r0=r0, c0=c0, kq=kq, gain=gain):
                        for k in range(kq * 4, kq * 4 + 4):
                            n = S["n"]; S["n"] += 1
                            st = S["cst"][n % 2]; sb = S["cbf"][n % 2]
                            P.dma("sync", st[:], D[name][r0 + k * 128:r0 + (k + 1) * 128, c0:c0 + 512])
                            P.act(sb[:], st[:], ACT.Copy, scale=G[gain][:, k:k + 1])
                            P.dma("sync", D["wc_" + name][i, :, k, :], sb[:], key="wconv2", w=["wc_" + name + str(i)])
                    tasks.append(t)
    return tasks


class WStream:
    def __init__(self, P, D, n=3):
        self.P = P; self.D = D
        self.bufs = [P.sb("wbuf%d" % i, [128, 16, 512], BF16) for i in range(n)]
        self.i = 0

    def preload(self, name, idx):
        b = self._load(name, idx)
        self.pre = getattr(self, "pre", [])
        self.pre.append((name, idx, b))

    def next(self, name, idx):
        pre = getattr(self, "pre", [])
        if pre:
            n2, i2, b = pre.pop(0)
            assert (n2, i2) == (name, idx), (n2, i2, name, idx)
            return b
        return self._load(name, idx)

    def _load(self, name, idx):
        b = self.bufs[self.i % len(self.bufs)]
        self.i += 1
        src = self.D["wc_" + name]
        self.P.dma("gpsimd", b[:, 0:8, :], src[idx, :, 0:8, :], key=b.name, r=["wc_" + name + str(idx)])
        self.P.dma("gpsimd", b[:, 8:16, :], src[idx, :, 8:16, :], key=b.name, r=["wc_" + name + str(idx)])
        return b


def xprep_gen(P, G, S, src, col0, gcol, xg, rstd_bc, ps_ss, want_col=None, ps_tr=None, stage=None):
    if stage is not None:
        v = src[:, col0:col0 + 512].rearrange("(k p) t -> p k t", p=128)
        P.dma("sync", stage[:, 0:8, :], v[:, 0:8, :], key=stage.name + "_ld", w=[stage.name])
        P.dma("sync", stage[:, 8:16, :], v[:, 8:16, :], key=stage.name + "_ld", w=[stage.name])
        acc = S["sacc"]; sq = S["ssq"]
        for k in range(16):
            P.act(xg[:, k, :], stage[:, k, :], ACT.Copy, scale=gcol[:, k:k + 1], w=[xg.name + str(k)])
            if k == 0:
                P.tt("gpsimd", acc[:], stage[:, k, :], stage[:, k, :], ALU.mult)
            else:
                P.tt("gpsimd", sq[k % 2][:], stage[:, k, :], stage[:, k, :], ALU.mult)
                P.tt("gpsimd", acc[:], acc[:], sq[k % 2][:], ALU.add)
        yield
        P.mm(ps_ss[:, :], G["onesf"][:], acc[:], start=True, stop=True)
    else:
        for k in range(16):
            st = S["xst"][k % len(S["xst"])]; sq = S["xsq"][k % 2]
            P.dma("sync", st[:], src[k * 128:(k + 1) * 128, col0:col0 + 512])
            P.act(xg[:, k, :], st[:], ACT.Copy, scale=gcol[:, k:k + 1], w=[xg.name + str(k)])
            P.tt("vector", sq[:], st[:], st[:], ALU.mult)
            P.mm(ps_ss[:, :], G["onesb"][:], sq[:], start=(k == 0), stop=(k == 15))
            if k % 4 == 3:
                yield
    P.act(rstd_bc[:], ps_ss[:, :], ACT.Ln, scale=1.0 / 2048, bias=G["epst"][:, 0:1])
    P.act(rstd_bc[:], rstd_bc[:], ACT.Exp, scale=-0.5)
    if want_col is not None:
        for j in range(4):
            P.tr(ps_tr[:, j * 128:(j + 1) * 128], rstd_bc[:, j * 128:(j + 1) * 128], G["identf"][:])
        P.copy("vector", want_col[:], apx(ps_tr[:], 0, [[128, 4]]))
    yield


def xprep(*a, **k):
    for _ in xprep_gen(*a, **k):
        pass


def mem_kv(P, G, D, flush=True):
    P.push_scope()
    Wkv = P.sb("m_wkv", [128, 16, 1024], BF16)
    for q in range(4):
        P.dma("gpsimd", Wkv[:, 4 * q:4 * q + 4, :], D["w_mem_kv"].rearrange("(k p) c -> p k c", p=128)[:, 4 * q:4 * q + 4, :], key="m_wkv")
    mg = P.sb("m_mg", [128, 16, 256], BF16)
    st = [P.sb("m_st%d" % i, [128, 256]) for i in range(2)]
    sq = [P.sb("m_sq%d" % i, [128, 256], BF16) for i in range(2)]
    rs = P.sb("m_rs", [128, 256]); rc = P.sb("m_rc", [128, 2])
    ps = [P.ps("m_ps%d" % i, [128, 512]) for i in range(3)]
    for k in range(16):
        P.dma("sync", st[k % 2][:], D["memT"][k * 128:(k + 1) * 128, :])
        P.act(mg[:, k, :], st[k % 2][:], ACT.Copy, scale=G["g_mem"][:, k:k + 1])
        P.tt("vector", sq[k % 2][:], st[k % 2][:], st[k % 2][:], ALU.mult)
        P.mm(ps[0][:, 0:256], G["onesb"][:], sq[k % 2][:], start=(k == 0), stop=(k == 15))
    P.act(rs[:], ps[0][:, 0:256], ACT.Ln, scale=1.0 / 2048, bias=G["epst"][:, 0:1])
    P.act(rs[:], rs[:], ACT.Exp, scale=-0.5)
    for j in range(2):
        P.tr(ps[1][:, j * 128:(j + 1) * 128], rs[:, j * 128:(j + 1) * 128], G["identf"][:])
    P.copy("vector", rc[:], apx(ps[1][:], 0, [[128, 2]]))
    for h in range(4):
        pp = ps[h % 2 + 1] if False else ps[2]
        for k in range(16):
            P.mm(pp[:, 0:256], Wkv[:, k, h * 128:(h + 1) * 128], mg[:, k, :], start=(k == 0), stop=(k == 15))
        P.tt("vector", G["kmemT"][:, h, :], pp[:, 0:256], rs[:], ALU.mult)
    for c in range(2):
        pp = ps[c]
        for k in range(16):
            P.mm(pp[:, :], mg[:, k, c * 128:(c + 1) * 128], Wkv[:, k, 512:1024], start=(k == 0), stop=(k == 15))
        P.act(G["Vmem"][:, c, :], pp[:, :], ACT.Copy, scale=rc[:, c:c + 1])
    if flush:
        P.flush()
        P.pop_scope()


def sweep1(P, G, D, nq=4, dbg=None):
    V = "vector"
    P.push_scope()
    S = {}
    S["sacc"] = P.sb("sacc", [128, 512]); S["ssq"] = [P.sb("ssq%d" % i, [128, 512]) for i in range(2)]
    S["xst"] = S["ssq"]
    S["xsq"] = [P.sb("xsq%d" % i, [128, 512], BF16) for i in range(2)]
    xg = P.sb("xg", [128, 16, 512], BF16)
    rstd = P.sb("rstd_bc", [128, 512]); rcol = P.sb("rstd_col", [128, 4])
    kT = [P.sb("kT%d" % i, [128, 8, 512], BF16) for i in range(3)]
    Vr = [P.sb("Vr%d" % i, [128, 4, 1024], BF16) for i in range(3)]
    qT = P.sb("qT", [128, 8, 512], BF16); qmT = P.sb("qmT", [128, 4, 512], BF16)
    ot = P.sb("ot", [128, 16, 512])
    yna = ot[:, 0:8, :]; ymem = ot[:, 8:12, :]
    ymix_na = P.sb("ymix_na", [128, 8, 512], BF16); ymix_mem = P.sb("ymix_mem", [128, 4, 512], BF16)
    tabt = [P.sb("tab%d" % i, [128, 768]) for i in range(2)]
    Ssb = [P.sb("Ssb%d" % i, [128, 768]) for i in range(2)]
    PT = [P.sb("PT%d" % i, [128, 768], BF16) for i in range(2)]
    rsum = P.sb("rsum", [128, 512]); sqs = [P.sb("sqs%d" % i, [128, 512], BF16) for i in range(2)]
    Pm = P.sb("Pm", [128, 2, 512], BF16)
    rs2 = P.sb("rs2", [128, 512])
    ps = [P.ps("s1ps%d" % i, [128, 512]) for i in range(8)]
    W = WStream(P, D, 2)
    xsrc = D["xT_own"]
    cnt = {"p": 0}
    def pbank():
        cnt["p"] += 1
        return ps[1 + (cnt["p"] % 2)]

    def kv_tile_gen(m, fast=False):
        slot = m % 3
        for _ in xprep_gen(P, G, S, xsrc, 512 * m, G["gpre"], xg, rstd, ps[0], want_col=rcol, ps_tr=ps[1],
                           stage=(ot if fast else None)):
            yield
        for half in range(2):
            wb = W.next("w_in", 2 + half)
            for hh in range(4):
                pb = pbank()
                for k in range(16):
                    P.mm(pb[:, :], wb[:, k, hh * 128:(hh + 1) * 128], xg[:, k, :], start=(k == 0), stop=(k == 15),
                         r=[wb.name, xg.name + str(k)])
                    if k == 7:
                        yield
                P.tt(V, kT[slot][:, half * 4 + hh, :], pb[:, :], rstd[:], ALU.mult)
                yield
        for half in range(2):
            wb = W.next("w_in", 4 + half)
            for j in range(4):
                pb = pbank()
                for k in range(16):
                    P.mm(pb[:, :], xg[:, k, j * 128:(j + 1) * 128], wb[:, k, :], start=(k == 0), stop=(k == 15),
                         r=[wb.name, xg.name + str(k)])
                    if k == 7:
                        yield
                P.act(Vr[slot][:, j, half * 512:(half + 1) * 512], pb[:, :], ACT.Copy, scale=rcol[:, j:j + 1])
                yield

    def kv_tile(m):
        for _ in kv_tile_gen(m, fast=True):
            pass

    def q_prep_gen(i, fast):
        for _ in xprep_gen(P, G, S, xsrc, 512 * i + 256, G["gpre"], xg, rstd, ps[0] if fast else ps[3],
                           stage=(ot if fast else None)):
            yield

    def q_tile(i):
        for half in range(2):
            wb = W.next("w_in", half)
            for hh in range(4):
                pb = pbank()
                for k in range(16):
                    P.mm(pb[:, :], wb[:, k, hh * 128:(hh + 1) * 128], xg[:, k, :], start=(k == 0), stop=(k == 15),
                         r=[wb.name, xg.name + str(k)])
                P.tt(V, qT[:, half * 4 + hh, :], pb[:, :], rstd[:], ALU.mult)
        wb = W.next("w_in", 6)
        for hh in range(4):
            pb = pbank()
            for k in range(16):
                P.mm(pb[:, :], wb[:, k, hh * 128:(hh + 1) * 128], xg[:, k, :], start=(k == 0), stop=(k == 15),
                     r=[wb.name, xg.name + str(k)])
            P.tt(V, qmT[:, hh, :], pb[:, :], rstd[:], ALU.mult)

    def na(i, filler=None, ff2t=()):
        steps = [(jj, h) for jj in range(4) for h in range(8)]
        SA = [(ps[3], ps[4]), (ps[5], ps[6])]

        def scores(n):
            jj, h = steps[n]
            j = 4 * i + jj
            cls, b, nch = na_class(j)
            tb = tabt[n % 2]
            pa, pb_ = SA[n % 2]
            P.dma("sync", tb[:, 0:nch * 128], D["na_tab"][cls, h, :, 0:nch * 128])
            for c in range(nch):
                idx = b // 2 + c
                kt = kT[(idx // 4) % 3]
                pb = pa if c < 4 else pb_
                cc = c % 4
                P.mm(pb[:, cc * 128:(cc + 1) * 128], kt[:, h, (idx % 4) * 128:(idx % 4 + 1) * 128],
                     qT[:, h, jj * 128:(jj + 1) * 128], start=True, stop=True)

        def soft(n):
            jj, h = steps[n]
            j = 4 * i + jj
            cls, b, nch = na_class(j)
            tb = tabt[n % 2]; Sb = Ssb[n % 2]; Pt = PT[n % 2]
            pa, pb_ = SA[n % 2]
            P.stt(Sb[:, 0:512], pa[:, 0:512], QSCALE, tb[:, 0:512], ALU.mult, ALU.add)
            w2 = (nch - 4) * 128
            P.stt(Sb[:, 512:512 + w2], pb_[:, 0:w2], QSCALE, tb[:, 512:512 + w2], ALU.mult, ALU.add)
            P.act(Pt[:, 0:nch * 128], Sb[:, 0:nch * 128], ACT.Exp)

        def rest(n):
            jj, h = steps[n]
            j = 4 * i + jj
            cls, b, nch = na_class(j)
            Pt = PT[n % 2]
            for c in range(nch):
                idx = b // 2 + c
                vt = Vr[(idx // 4) % 3]
                P.mm(ps[7][:, 0:128], vt[:, idx % 4, h * 128:(h + 1) * 128], Pt[:, c * 128:(c + 1) * 128],
                     start=(c == 0), stop=(c == nch - 1))
            for c in range(nch):
                P.mm(ps[7][:, 128:256], G["onesb"][:], Pt[:, c * 128:(c + 1) * 128], start=(c == 0), stop=(c == nch - 1))
            P.act(rsum[:, 0:128], ps[7][:, 128:256], ACT.Ln)
            P.act(rsum[:, 0:128], rsum[:, 0:128], ACT.Exp, scale=-1.0)
            P.tt(V, yna[:, h, jj * 128:(jj + 1) * 128], ps[7][:, 0:128], rsum[:, 0:128], ALU.mult)

        scores(0)
        soft(0)
        for n in range(32):
            if n + 1 < 32:
                scores(n + 1)
                soft(n + 1)
            if filler is not None:
                next(filler, None)
            rest(n)
            if filler is not None and n % 4 == 3:
                next(filler, None)
            if n % 8 == 4 and ff2t:
                ff2t.pop(0)()
        if filler is not None:
            for _ in filler:
                pass

    def memattn(i):
        for h in range(4):
            for c in range(2):
                P.mm(ps[3 + c][:, :], G["kmemT"][:, h, c * 128:(c + 1) * 128], qmT[:, h, :], start=True, stop=True)
                P.act(Pm[:, c, :], ps[3 + c][:, :], ACT.Exp, scale=QSCALE)
            pb = pbank()
            for c in range(2):
                P.mm(pb[:, :], G["Vmem"][:, c, h * 128:(h + 1) * 128], Pm[:, c, :], start=(c == 0), stop=(c == 1))
            pb2 = pbank()
            for c in range(2):
                P.mm(pb2[:, :], G["onesb"][:], Pm[:, c, :], start=(c == 0), stop=(c == 1))
            P.act(rsum[:], pb2[:, :], ACT.Ln)
            P.act(rsum[:], rsum[:], ACT.Exp, scale=-1.0)
            P.tt(V, ymem[:, h, :], pb[:, :], rsum[:], ALU.mult)

    def groupnorm(y, nch, gcol0, out, D_):
        for c in range(nch):
            P.tt(V, sqs[c % 2][:], y[:, c, :], y[:, c, :], ALU.mult)
            P.mm(ps[0][:, :], G["onesb"][:], sqs[c % 2][:], start=(c == 0), stop=(c == nch - 1))
        P.act(rs2[:], ps[0][:, :], ACT.Ln, scale=1.0 / D_, bias=G["epst"][:, 0:1])
        P.act(rs2[:], rs2[:], ACT.Exp, scale=-0.5)
        for c in range(nch):
            P.stt(out[:, c, :], y[:, c, :], G["g_out"][:, gcol0 + c:gcol0 + c + 1], rs2[:], ALU.mult, ALU.mult)

    def wout(i, filler=None):
        pend = []
        for bq in range(4):
            if filler is not None:
                next(filler, None); next(filler, None)
            wb = W.next("w_out", bq)
            for dc in range(4):
                pb = pbank()
                for m in range(16):
                    if m < 8:
                        rhs = ymix_na[:, m, :]
                    elif m < 12:
                        rhs = G["yssm_n"][:, m - 8, 512 * i:512 * (i + 1)]
                    else:
                        rhs = ymix_mem[:, m - 12, :]
                    P.mm(pb[:, :], wb[:, m, dc * 128:(dc + 1) * 128], rhs, start=(m == 0), stop=(m == 15))
                d = bq * 4 + dc
                while pend:
                    pend.pop(0)()
                P.copy("scalar", ot[:, d, :], pb[:, :])
                P.tt(V, sqs[d % 2][:], ot[:, d, :], ot[:, d, :], ALU.mult)
                pend.append(lambda d=d: P.mm(ps[0][:, :], G["onesb"][:], sqs[d % 2][:], start=(d == 0), stop=(d == 15)))
        while pend:
            pend.pop(0)()
        P.act(rs2[:], ps[0][:, :], ACT.Ln, scale=1.0 / 2048, bias=G["epst"][:, 0:1])
        P.act(rs2[:], rs2[:], ACT.Exp, scale=-0.5)
        if filler is not None:
            for _ in filler:
                pass
        if i + 1 < nq:
            W.preload("w_in", 0); W.preload("w_in", 1)
        for d in range(16):
            P.stt(ot[:, d, :], ot[:, d, :], G["g_post"][:, d:d + 1], rs2[:], ALU.mult, ALU.mult)
        x1v = D["x1T"][:, 512 * i:512 * (i + 1)].rearrange("(k p) t -> p k t", p=128)
        P.op("gpsimd", lambda e, x1v=x1v: e.dma_start(out=x1v, in_=ot[:], accum_op=ALU.add),
             r=[ot.name, "x1T_%d" % i], w=["x1T_%d" % i], dma="x1acc")

    kv_tile(0)
    kv_tile(1)
    ff2t = conv_tasks_ff2(P, D)
    for i in range(nq):
        P.dma("sync", D["x1T"][:, 512 * i:512 * (i + 1)], xsrc[:, 512 * i + 256:512 * i + 768], key="x1cp", w=["x1T_%d" % i])
        if i == 0:
            for _ in q_prep_gen(0, True):
                pass
        q_tile(i)
        na(i, kv_tile_gen(i + 2) if i + 2 <= 4 else None, ff2t)
        memattn(i)
        groupnorm(yna, 8, 0, ymix_na, 1024)
        groupnorm(ymem, 4, 12, ymix_mem, 512)
        if dbg is not None and i == 0:
            P.dma("sync", dbg["yna"], yna, key="dbg_yna"); P.dma("sync", dbg["ymem"], ymem, key="dbg_ymem")
        wout(i, q_prep_gen(i + 1, False) if i + 1 < nq else None)
    P.flush()
    P.pop_scope()


def sweep2(P, G, D, nq=4):
    V = "vector"
    P.push_scope()
    S = {}
    S["sacc"] = P.sb("fsacc", [128, 512]); S["ssq"] = [P.sb("fssq%d" % i, [128, 512]) for i in range(2)]
    h2s = [P.sb("h2_%d" % i, [128, 16, 512], BF16) for i in range(2)]
    S["xst"] = S["ssq"]
    S["xsq"] = [P.sb("fxsq%d" % i, [128, 512], BF16) for i in range(2)]
    hid = P.sb("hid", [128, 64, 512], BF16)
    ft = P.sb("ft", [128, 16, 512])
    rstds = [P.sb("f_rstd%d" % i, [128, 512]) for i in range(2)]; rs2 = P.sb("f_rs2", [128, 512]); r4 = P.sb("f_r4", [128, 512])
    rl = [P.sb("f_rl%d" % i, [128, 512]) for i in range(2)]
    sqs = [P.sb("f_sqs%d" % i, [128, 512], BF16) for i in range(4)]
    ps = [P.ps("s2ps%d" % i, [128, 512]) for i in range(8)]
    W = WStream(P, D, 2)
    n = 0
    for i in range(nq):
        P.dma("sync", D["outT"][:, 512 * i:512 * (i + 1)], D["x1T"][:, 512 * i:512 * (i + 1)], key="ocp", w=["outT_%d" % i], r=[])
        h2 = h2s[i % 2]; rstd = rstds[i % 2]
        if i == 0:
            xprep(P, G, S, D["x1T"], 0, G["g_pre2"], h2, rstd, ps[0], stage=ft)
        filler = None
        if i + 1 < nq:
            filler = xprep_gen(P, G, S, D["x1T"], 512 * (i + 1), G["g_pre2"], h2s[(i + 1) % 2], rstds[(i + 1) % 2], ps[0])
        for bq in range(16):
            wb = W.next("w_ff1", bq)
            for c4 in range(4):
                pb = ps[1 + (n % 2)]; r = rl[n % 2]; n += 1
                for k in range(16):
                    P.mm(pb[:, :], wb[:, k, c4 * 128:(c4 + 1) * 128], h2[:, k, :], start=(k == 0), stop=(k == 15),
                         r=[wb.name, h2.name + str(k)])
                P.act(r[:], pb[:, :], ACT.Relu)
                P.tt(V, hid[:, bq * 4 + c4, :], r[:], r[:], ALU.mult)
        pend2 = []
        for cg in range(4):
            for jp in range(4):
                wb = W.next("w_ff2", cg * 4 + jp)
                if filler is not None:
                    next(filler, None)
                if jp == 1:
                    while pend2:
                        pend2.pop(0)()
                for jj in range(16):
                    for dc in range(4):
                        P.mm(ps[4 + dc][:, :], wb[:, jj, dc * 128:(dc + 1) * 128], hid[:, jp * 16 + jj, :],
                             start=(jp == 0 and jj == 0), stop=(jp == 3 and jj == 15))
            for dc in range(4):
                d = cg * 4 + dc
                P.copy("scalar", ft[:, d, :], ps[4 + dc][:, :])
                P.tt(V, sqs[d % 4][:], ft[:, d, :], ft[:, d, :], ALU.mult)
                pend2.append(lambda d=d: P.mm(ps[3][:, :], G["onesb"][:], sqs[d % 4][:], start=(d == 0), stop=(d == 15)))
        while pend2:
            pend2.pop(0)()
        if i + 1 < nq:
            W.preload("w_ff1", 0); W.preload("w_ff1", 1)
        P.tt(V, r4[:], rstd[:], rstd[:], ALU.mult)
        P.tt(V, rs2[:], r4[:], r4[:], ALU.mult)
        P.tt(V, rs2[:], rs2[:], ps[3][:, :], ALU.mult)
        P.act(rs2[:], rs2[:], ACT.Ln, scale=1.0 / 2048, bias=G["epst"][:, 0:1])
        P.act(rs2[:], rs2[:], ACT.Exp, scale=-0.5)
        P.tt(V, rs2[:], rs2[:], r4[:], ALU.mult)
        for d in range(16):
            P.stt(ft[:, d, :], ft[:, d, :], G["g_post2"][:, d:d + 1], rs2[:], ALU.mult, ALU.mult)
        ov = D["outT"][:, 512 * i:512 * (i + 1)].rearrange("(k p) t -> p k t", p=128)
        P.op("gpsimd", lambda e, ov=ov: e.dma_start(out=ov, in_=ft[:], accum_op=ALU.add),
             r=[ft.name, "outT_%d" % i], w=["outT_%d" % i], dma="oacc")
    P.flush()
    P.pop_scope()


def colvec(v):
    return np.ascontiguousarray(np.asarray(v, np.float32).reshape(-1, 128).T)


def build_program(shapes, debug=False, stages="SGLM12"):
    nc = bass.Bass("TRN2", target_bir_lowering=False)
    D = {}
    for name, shp in shapes.items():
        D[name] = nc.dram_tensor(name, list(shp), F32, kind="ExternalInput").ap()
    D["x1T"] = nc.dram_tensor("x1T", [2048, 2048], F32, kind="Internal").ap()
    for nm, lst in WCONV.items():
        D["wc_" + nm] = nc.dram_tensor("wc_" + nm, [len(lst), 128, 16, 512], BF16, kind="Internal").ap()
    D["wc_w_ff2"] = nc.dram_tensor("wc_w_ff2", [16, 128, 16, 512], BF16, kind="Internal").ap()
    D["outT"] = nc.dram_tensor("outT", [2048, 2048], F32, kind="ExternalOutput").ap()
    dbg = None
    if debug:
        dbg = {"yna": nc.dram_tensor("dbg_yna", [128, 8, 512], F32, kind="ExternalOutput").ap(),
               "ymem": nc.dram_tensor("dbg_ymem", [128, 4, 512], F32, kind="ExternalOutput").ap(),
               "yssm": nc.dram_tensor("dbg_yssm", [128, 4, 2048], BF16, kind="ExternalOutput").ap(),
               "x1T": nc.dram_tensor("dbg_x1T", [2048, 2048], F32, kind="ExternalOutput").ap()}
    P = Prog(nc)
    G = {}
    G["identf"] = P.sb("identf", [128, 128]); G["identb"] = P.sb("identb", [128, 128], BF16)
    G["onesb"] = P.sb("onesb", [128, 128], BF16); G["onesf"] = P.sb("onesf", [128, 128])
    for nm in ["gpre", "g_out", "g_post", "g_pre2", "g_post2", "g_mem"]:
        G[nm] = P.sb("t_" + nm, [128, 16])
        P.dma("sync", G[nm][:], D[nm])
    G["bglu"] = P.sb("t_bglu", [128, 4]); P.dma("sync", G["bglu"][:], D["bglu"])
    G["epst"] = P.sb("epst", [128, 1])
    G["yssm_n"] = P.sb("yssm_n", [128, 4, 2048], BF16)
    G["kmemT"] = P.sb("kmemT", [128, 4, 256], BF16); G["Vmem"] = P.sb("Vmem", [128, 2, 512], BF16)
    P.dma("sync", G["identf"][:], D["ident"]); P.dma("gpsimd", G["identb"][:], D["ident"])
    P.memset("vector", G["onesb"][:], 1.0); P.memset("vector", G["onesf"][:], 1.0)
    P.memset("vector", G["epst"][:], EPS)
    CS = {"n": 0}
    ctasks = []
    if "S" in stages:
        P.push_scope()
        G["WB"] = P.sb("WB", [128, 32, 2, 128], BF16)
        G["WC"] = P.sb("WC", [128, 32, 2, 128], BF16)
        G["T"] = P.sb("T", [128, 32, 128], BF16)
        G["R"] = P.sb("R", [128, 32]); G["LN"] = P.sb("LN", [128, 2, 32])
        G["th8"] = P.sb("th8", [128, 32]); G["a8"] = P.sb("a8", [128, 32])
        G["iota"] = P.sb("iota_t", [128, 129]); P.dma("sync", G["iota"][:], D["iota"])
        TA = P.sb("TA", [128, 32, 129]); TB = P.sb("TB", [128, 32, 129])
        G["Qr"] = TA; G["Qi"] = TB; G["TC"] = TA; G["TS"] = TB
        ssm_alloc(P, G)
        P.memset("vector", G["Floc"][:], 0.0)
        ctasks = conv_tasks(P, D, G, CS)
        phase_s(P, nc, D, G, flush=False)
        load_wssm(P, G, D["w_in"], G["gpre"])
        make_tables(P, G, "Q", flush=False)
        P.flush()
        P.pop_scope(); P.pop_scope()
        P.push_scope()
        xbf = [P.sb("xbf%d" % i, [128, 16, 1024], BF16) for i in range(2)]
        PSG = P.ps("psG", [128, 1024])
        PS = {"g": PSG, "f": [PSG[:, 0:512], PSG[:, 512:1024]] + [P.ps("psf%d" % i, [128, 512]) for i in range(2, 6)],
              "b": [P.ps("psb%d" % i, [128, 1024], BF16) for i in range(2)]}
        PS["f"] += [PS["f"][0], PS["f"][1]]
        ssm_load(P, D["xT_all"], 0, xbf[0])
        for u in range(16):
            if u + 1 < 16:
                ssm_load(P, D["xT_all"], (u + 1) * 1024, xbf[(u + 1) % 2])
            for _ in range(2):
                if ctasks:
                    ctasks.pop(0)()
            ssm_unit(P, G, D["xT_all"], u * 1024, xbf[u % 2], PS, "G", u, preloaded=True)
        P.flush()
        P.pop_scope()
        P.push_scope()
        Mt = P.sb("cmask_t", [128, 16, 2, 32])
        P.dma("sync", Mt[:], D["cmask"])
        carry_compute(P, G, Mt)
        make_tables(P, G, "E", flush=False)
        if "M" in stages:
            mem_kv(P, G, D, flush=False)
        P.flush()
        if "M" in stages:
            P.pop_scope()
        P.pop_scope(); P.pop_scope()
        P.push_scope()
        xb = P.sb("xbfL", [128, 16, 1024], BF16)
        G["Dt"] = [P.sb("Dt%d" % i, [128, 2, 128]) for i in range(2)]
        G["Wb"] = [P.sb("Wb%d" % i, [128, 2, 129]) for i in range(2)]
        LO = {}
        LO["Tm2"] = [P.sb("Tm2_%d" % i, [128, 4, 128]) for i in range(2)]
        LO["Zp"] = [P.sb("Zp%d" % i, [128, 2, 2, 128], BF16) for i in range(2)]
        LO["Ysb"] = [P.sb("Ysb%d" % i, [128, 4, 128]) for i in range(2)]
        LO["Rout"] = xb[:, 0:8, :].bitcast(F32)
        _yb = xb[:, 8:16, :].bitcast(F32)
        LO["ysT"] = apx(_yb, 0, [[1024, 4], [1, 1024]])
        for nm in ["ga", "gl"]:
            LO[nm] = P.sb("L_" + nm, [128, 4, 256])
        LO["gb"] = LO["ga"]; LO["zt"] = LO["ga"]
        LO["gbf"] = P.sb("L_gbf", [128, 4, 256], BF16); LO["sq"] = P.sb("L_sq", [128, 4, 256], BF16)
        LO["rs"] = P.sb("L_rs", [128, 256])
        LO["Wglu"] = P.sb("L_wglu", [128, 4, 512], BF16)
        P.dma("gpsimd", LO["Wglu"][:], D["w_glu"].rearrange("(k p) c -> p k c", p=128))
        PSG = P.ps("psLG", [128, 1024])
        PS = {"g": PSG, "f": [PSG[:, 0:512], PSG[:, 512:1024]] + [P.ps("psLf%d" % i, [128, 512]) for i in range(2, 6)],
              "b": [P.ps("psLb%d" % i, [128, 1024], BF16) for i in range(2)]}
        PS["f"] += [PS["f"][0], PS["f"][1]]
        for k in range(2):
            ssm_unit(P, G, D["xT_own"], 256 + k * 1024, xb, PS, "L", None, kown=k, LO=LO)
            ssm_finish(P, G, LO, PS, k)
        if debug:
            P.dma("sync", dbg["yssm"], G["yssm_n"][:], key="dbg_yssm")
        P.flush()
        P.pop_scope()
        P.pop_scope()
    else:
        P.memset("vector", G["yssm_n"][:], 0.0)
        if "M" in stages:
            mem_kv(P, G, D)
    assert not ctasks
    if "1" in stages:
        sweep1(P, G, D, dbg=dbg)
        if debug:
            P.dma("sync", dbg["x1T"], D["x1T"], key="dbg_x1T")
    if "2" in stages:
        sweep2(P, G, D)
    P.flush()
    P.close()
    return nc


def host_inputs(inp, core, shared):
    m = dict(shared)
    t0 = core * 2048
    xT = shared["xT_all"]
    own = np.zeros((2048, 2560), np.float32)
    lo = t0 - 256; hi = t0 + 2304
    a = max(lo, 0); b = min(hi, 16384)
    own[:, a - lo:b - lo] = xT[:, a:b]
    m["xT_own"] = own
    m["na_tab"] = host_na_tables(np.asarray(inp["na_rpb"][0], np.float32), core)
    m["cmask"] = host_carry_masks(core)
    return m


def host_shared(inp):
    sc, B, C, dcol = host_ssm_layout(inp)
    ident, iota, maskF, maskB = host_consts()
    f = lambda a: np.ascontiguousarray(np.asarray(a, np.float32))
    sh = {"ssm_sc": sc, "ssm_B": B, "ssm_C": C, "ssm_dcol": dcol, "iota": iota, "maskF": maskF, "maskB": maskB, "ident": ident,
          "xT_all": f(np.asarray(inp["x"][0]).T), "w_in": f(inp["w_in"][0]), "w_out": f(inp["w_out"][0]),
          "w_ff1": f(inp["w_ff1"][0]), "w_ff2": f(inp["w_ff2"][0]), "w_glu": f(inp["w_glu"][0]),
          "w_mem_kv": f(inp["w_mem_kv"][0]), "memT": f(np.asarray(inp["mem"][0]).T),
          "gpre": colvec(inp["norm_mix_pre"][0]), "g_post": colvec(inp["norm_mix_post"][0]),
          "g_pre2": colvec(inp["norm_mlp_pre"][0]), "g_post2": colvec(inp["norm_mlp_post"][0]),
          "g_mem": colvec(inp["mem_norm"][0]),
          "g_out": colvec(np.concatenate([np.asarray(inp["out_norm_na"][0]), np.asarray(inp["out_norm_ssm"][0]),
                                          np.asarray(inp["out_norm_mem"][0])])),
          "bglu": colvec(inp["b_glu"][0])}
    return sh


_CACHE = {}


def kernel(**inputs):
    inp = {k: np.asarray(v) for k, v in inputs.items()}
    sh = host_shared(inp)
    maps = [host_inputs(inp, c, sh) for c in range(8)]
    shapes = {k: v.shape for k, v in maps[0].items()}
    debug = bool(int(_os.environ.get("MK_DEBUG", "0")))
    stages = _os.environ.get("MK_STAGES", "SGLM12")
    ncores = int(_os.environ.get("MK_CORES", "8"))
    nc = build_program(shapes, debug=debug, stages=stages)
    res = run_bass_kernel_spmd(nc, maps[:ncores], core_ids=list(range(ncores)))
    if debug:
        _CACHE["res"] = res.results
    out = np.zeros((1, 16384, 2048), np.float32)
    for c in range(ncores):
        out[0, c * 2048:(c + 1) * 2048, :] = res.results[c]["outT"].T
    return out
```
